# Optimizing a Trainium2 kernel written in Bass

```python
import jax, jax.numpy as jnp
from jax import lax
import numpy as np

D_MODEL = 1024
BATCH = 1
SEQ = 16384
DEPTH = 4

MLA_HEADS = 8
QK_NOPE_DIM = 128
QK_ROPE_DIM = 64
V_HEAD_DIM = 128
Q_LORA_RANK = 384
KV_LORA_RANK = 256
ROPE_THETA = 10000.0
Q_BLOCK = 128
FOURIER_GROUPS = 4
FOURIER_GROUP_DIM = 128
FOURIER_WIDTH = FOURIER_GROUPS * FOURIER_GROUP_DIM
EVEN_IN_WIDTH = Q_LORA_RANK + KV_LORA_RANK + QK_ROPE_DIM + FOURIER_WIDTH
EVEN_MIX_WIDTH = MLA_HEADS * V_HEAD_DIM + FOURIER_WIDTH
SGU_CHUNK = 128
SGU_GROUPS = 8
SGU_WIDTH = 2 * D_MODEL
SGU_GROUP_DIM = SGU_WIDTH // SGU_GROUPS
D_FF = ((8 * D_MODEL + 3 * 256 - 1) // (3 * 256)) * 256
N_EVEN = (DEPTH + 1) // 2
N_ODD = DEPTH // 2
DN_ALPHA = (2 * DEPTH) ** 0.25
DN_BETA = (8 * DEPTH) ** -0.25
LN_EPS = 1e-5
RMS_EPS = 1e-6

kernel_name = "hybrid_mla_fnet_sgu_deepnorm_encoder"


def layer_norm(x, g, b):
    xf = x.astype(jnp.float32)
    mu = jnp.mean(xf, axis=-1, keepdims=True)
    var = jnp.mean(jnp.square(xf - mu), axis=-1, keepdims=True)
    return ((xf - mu) * lax.rsqrt(var + LN_EPS) * g + b).astype(x.dtype)


def rms_norm(x, g):
    xf = x.astype(jnp.float32)
    ms = jnp.mean(jnp.square(xf), axis=-1, keepdims=True)
    return (xf * lax.rsqrt(ms + RMS_EPS) * g).astype(x.dtype)


def rotary_tables(seq):
    inv = 1.0 / (ROPE_THETA ** (jnp.arange(0, QK_ROPE_DIM, 2, dtype=jnp.float32) / QK_ROPE_DIM))
    ang = jnp.arange(seq, dtype=jnp.float32)[:, None] * inv[None, :]
    return jnp.cos(ang), jnp.sin(ang)


def apply_rotary(t, cos, sin):
    tf = t.astype(jnp.float32)
    t1, t2 = jnp.split(tf, 2, axis=-1)
    return jnp.concatenate([t1 * cos - t2 * sin, t1 * sin + t2 * cos], axis=-1).astype(t.dtype)


def mla_attention(q_nope, q_rope, k_nope, k_rope, v):
    B, S, H, _ = q_nope.shape
    nb = S // Q_BLOCK
    scale = (QK_NOPE_DIM + QK_ROPE_DIM) ** -0.5

    def to_blocks(t):
        return jnp.moveaxis(t.reshape(B, nb, Q_BLOCK, *t.shape[2:]), 1, 0)

    def attend(blk):
        qn, qr = blk
        s = (jnp.einsum('bqhd,bkhd->bhqk', qn, k_nope, preferred_element_type=jnp.float32)
             + jnp.einsum('bqhr,bkr->bhqk', qr, k_rope, preferred_element_type=jnp.float32))
        p = jax.nn.softmax(s * scale, axis=-1)
        return jnp.einsum('bhqk,bkhd->bqhd', p.astype(v.dtype), v)

    out = lax.map(attend, (to_blocks(q_nope), to_blocks(q_rope)))
    return jnp.moveaxis(out, 0, 1).reshape(B, S, H * V_HEAD_DIM)


def fourier_mix(f):
    B, S, _ = f.shape
    fg = f.astype(jnp.float32).reshape(B, S, FOURIER_GROUPS, FOURIER_GROUP_DIM)
    y = jnp.fft.fft2(fg, axes=(1, 3), norm="ortho").real
    return y.reshape(B, S, FOURIER_WIDTH).astype(f.dtype)


def even_mixer(x, w_in, q_norm, w_uq, kv_norm, w_uk, w_uv, w_out, cos, sin):
    B, S, _ = x.shape
    h = x @ w_in
    c_q, c_kv, k_r, f = jnp.split(
        h, [Q_LORA_RANK, Q_LORA_RANK + KV_LORA_RANK, Q_LORA_RANK + KV_LORA_RANK + QK_ROPE_DIM], axis=-1)
    q = (rms_norm(c_q, q_norm) @ w_uq).reshape(B, S, MLA_HEADS, QK_NOPE_DIM + QK_ROPE_DIM)
    q_nope = q[..., :QK_NOPE_DIM]
    q_rope = apply_rotary(q[..., QK_NOPE_DIM:], cos[:, None, :], sin[:, None, :])
    c_kv = rms_norm(c_kv, kv_norm)
    k_nope = (c_kv @ w_uk).reshape(B, S, MLA_HEADS, QK_NOPE_DIM)
    v = (c_kv @ w_uv).reshape(B, S, MLA_HEADS, V_HEAD_DIM)
    k_rope = apply_rotary(k_r, cos, sin)
    attn = mla_attention(q_nope, q_rope, k_nope, k_rope, v)
    return jnp.concatenate([attn, fourier_mix(f)], axis=-1) @ w_out


def odd_mixer(x, w_in, norm_g, norm_b, w_s, b_s, w_out):
    B, S, _ = x.shape
    z = jax.nn.gelu(x @ w_in, approximate=False)
    u, v = jnp.split(z, 2, axis=-1)
    v = layer_norm(v, norm_g, norm_b)
    vc = v.reshape(B, S // SGU_CHUNK, SGU_CHUNK, SGU_GROUPS, SGU_GROUP_DIM)
    s = (jnp.einsum('gpq,bnqgc->bnpgc', w_s, vc)
         + jnp.swapaxes(b_s, 0, 1)[None, None, :, :, None])
    return (u * s.reshape(B, S, SGU_WIDTH)) @ w_out


def swiglu(x, w_gate, w_up, w_down):
    return (jax.nn.silu(x @ w_gate) * (x @ w_up)) @ w_down


def setup_inputs(seed: int = 0) -> dict:
    key = jax.random.key(seed)
    ks = jax.random.split(key, 24)

    def nrm(k, shape, scale):
        return jax.random.normal(k, shape, jnp.float32) * scale

    def gain(k, shape):
        return 1.0 + 0.02 * jax.random.normal(k, shape, jnp.float32)

    def small(k, shape):
        return 0.02 * jax.random.normal(k, shape, jnp.float32)

    E, O, L = N_EVEN, N_ODD, DEPTH
    return {
        "x": jax.random.normal(ks[0], (BATCH, SEQ, D_MODEL), jnp.float32),
        "even_w_in": nrm(ks[1], (E, D_MODEL, EVEN_IN_WIDTH), D_MODEL ** -0.5),
        "even_q_norm": gain(ks[2], (E, Q_LORA_RANK)),
        "even_w_uq": nrm(ks[3], (E, Q_LORA_RANK, MLA_HEADS * (QK_NOPE_DIM + QK_ROPE_DIM)), Q_LORA_RANK ** -0.5),
        "even_kv_norm": gain(ks[4], (E, KV_LORA_RANK)),
        "even_w_uk": nrm(ks[5], (E, KV_LORA_RANK, MLA_HEADS * QK_NOPE_DIM), KV_LORA_RANK ** -0.5),
        "even_w_uv": nrm(ks[6], (E, KV_LORA_RANK, MLA_HEADS * V_HEAD_DIM), KV_LORA_RANK ** -0.5 * DN_BETA),
        "even_w_out": nrm(ks[7], (E, EVEN_MIX_WIDTH, D_MODEL), EVEN_MIX_WIDTH ** -0.5 * DN_BETA),
        "odd_w_in": nrm(ks[8], (O, D_MODEL, 2 * SGU_WIDTH), D_MODEL ** -0.5),
        "odd_sgu_norm_g": gain(ks[9], (O, SGU_WIDTH)),
        "odd_sgu_norm_b": small(ks[10], (O, SGU_WIDTH)),
        "odd_w_spatial": nrm(ks[11], (O, SGU_GROUPS, SGU_CHUNK, SGU_CHUNK), SGU_CHUNK ** -0.5),
        "odd_b_spatial": gain(ks[12], (O, SGU_GROUPS, SGU_CHUNK)),
        "odd_w_out": nrm(ks[13], (O, SGU_WIDTH, D_MODEL), SGU_WIDTH ** -0.5 * DN_BETA),
        "mix_ln_g": gain(ks[14], (L, D_MODEL)),
        "mix_ln_b": small(ks[15], (L, D_MODEL)),
        "ffn_w_gate": nrm(ks[16], (L, D_MODEL, D_FF), D_MODEL ** -0.5),
        "ffn_w_up": nrm(ks[17], (L, D_MODEL, D_FF), D_MODEL ** -0.5 * DN_BETA),
        "ffn_w_down": nrm(ks[18], (L, D_FF, D_MODEL), D_FF ** -0.5 * DN_BETA),
        "ffn_ln_g": gain(ks[19], (L, D_MODEL)),
        "ffn_ln_b": small(ks[20], (L, D_MODEL)),
    }


def reference(x, even_w_in, even_q_norm, even_w_uq, even_kv_norm, even_w_uk, even_w_uv, even_w_out,
              odd_w_in, odd_sgu_norm_g, odd_sgu_norm_b, odd_w_spatial, odd_b_spatial, odd_w_out,
              mix_ln_g, mix_ln_b, ffn_w_gate, ffn_w_up, ffn_w_down, ffn_ln_g, ffn_ln_b):
    cos, sin = rotary_tables(x.shape[1])
    for layer in range(DEPTH):
        i = layer // 2
        if layer % 2 == 0:
            y = even_mixer(x, even_w_in[i], even_q_norm[i], even_w_uq[i], even_kv_norm[i],
                           even_w_uk[i], even_w_uv[i], even_w_out[i], cos, sin)
        else:
            y = odd_mixer(x, odd_w_in[i], odd_sgu_norm_g[i], odd_sgu_norm_b[i],
                          odd_w_spatial[i], odd_b_spatial[i], odd_w_out[i])
        x = layer_norm(DN_ALPHA * x + y, mix_ln_g[layer], mix_ln_b[layer])
        x = layer_norm(DN_ALPHA * x + swiglu(x, ffn_w_gate[layer], ffn_w_up[layer], ffn_w_down[layer]),
                       ffn_ln_g[layer], ffn_ln_b[layer])
    return x
```

```python
import contextlib
import numpy as np
import ml_dtypes
import concourse.bass as bass
import concourse.mybir as mybir
from concourse.bass_utils import run_bass_kernel_spmd

F32 = mybir.dt.float32
BF16 = mybir.dt.bfloat16
AF = mybir.ActivationFunctionType
ALU = mybir.AluOpType

NCORES = 8
SEQ = 16384
T = SEQ // NCORES
D = 1024
DFF = 2816
NFC = DFF // 128
DEPTH = 4
ALPHA = float((2 * DEPTH) ** 0.25)
LN_EPS = 1e-5
RMS_EPS = 1e-6
GROWS = 832
SCALE = float(192 ** -0.5)

ENGS = ["pe", "act", "dve", "pool", "sp"]
SEM_EPOCH = 20000
NDMA_SEMS = {"sp": 16, "pool": 8, "act": 4, "pe": 1, "dve": 1}


class Res:
    __slots__ = ("name", "w", "r")

    def __init__(self, name=""):
        self.name = name
        self.w = None
        self.r = []


class Op:
    __slots__ = ("eng", "fn", "deps", "sig", "sem", "val", "dma", "inc")


class Prog:
    def __init__(self, nc):
        self.nc = nc
        self.ops = {e: [] for e in ENGS}
        self.n_dma = {e: 0 for e in ENGS}
        self.dma_last = {}
        self.all_res = []

    def res(self, name=""):
        r = Res(name)
        self.all_res.append(r)
        return r

    def add(self, eng, fn, reads=(), writes=(), dma=False, inc=None, extra=()):
        op = Op()
        op.inc = inc if inc is not None else (16 if dma else 1)
        op.eng = eng
        op.fn = fn
        op.dma = dma
        op.sig = dma
        op.sem = None
        op.val = 0
        deps = set(extra)
        for r in reads:
            if r.w is not None:
                deps.add(r.w)
        for w in writes:
            if w.w is not None:
                deps.add(w.w)
            deps.update(w.r)
        for r in reads:
            r.r.append(op)
        for w in writes:
            w.w = op
            w.r = []
        deps.discard(op)
        if dma:
            if op.inc == 16:
                k = (eng, self.n_dma[eng] % NDMA_SEMS[eng])
                self.n_dma[eng] += 1
            else:
                k = (eng, "cc")
            prev = self.dma_last.get(k)
            if prev is not None:
                deps.add(prev)
            self.dma_last[k] = op
            op.sem = k
        if eng == "pe" and not dma:
            deps = {d for d in deps if not (d.eng == "pe" and not d.dma)}
        op.deps = deps
        self.ops[eng].append(op)
        return op

    def pe(self, fn, reads=(), writes=()):
        return self.add("pe", fn, reads, writes)

    def act(self, fn, reads=(), writes=()):
        return self.add("act", fn, reads, writes)

    def dve(self, fn, reads=(), writes=()):
        return self.add("dve", fn, reads, writes)

    def pool(self, fn, reads=(), writes=()):
        return self.add("pool", fn, reads, writes)

    def dma(self, q, out, in_, reads=(), writes=()):
        return self.add(q, lambda e: e.dma_start(out=out, in_=in_), reads, writes, dma=True)

    def barrier(self):
        last = []
        for e in ENGS:
            for op in reversed(self.ops[e]):
                if not op.dma:
                    last.append(op)
                    break
        last.extend(self.dma_last.values())
        for r in self.all_res:
            r.w = None
            r.r = []
        for e in ENGS:
            self.add(e, None, extra=last)

    def emit(self, final_ops=()):
        nc = self.nc
        for e in ENGS:
            for op in self.ops[e]:
                for d in op.deps:
                    d.sig = True
        for op in final_ops:
            op.sig = True
        sems = {}

        def get_sem(key):
            if key not in sems:
                sems[key] = nc.alloc_semaphore("s_%s_%s" % key)
            return sems[key]

        dma_cnt = {}
        for e in ENGS:
            c = 0
            for op in self.ops[e]:
                if op.dma:
                    k = op.sem
                    dma_cnt[k] = dma_cnt.get(k, 0) + op.inc
                    op.sem = get_sem(("d" + k[0], k[1]))
                    op.val = dma_cnt[k]
                elif op.sig:
                    ep = c // SEM_EPOCH
                    c += 1
                    op.sem = get_sem((e, ep))
                    op.val = c - ep * SEM_EPOCH
        self.nsems = len(sems)
        nwaits = {e: 0 for e in ENGS}
        ninst = {e: 0 for e in ENGS}

        def run(e, eo):
            seen = {}
            for op in self.ops[e]:
                need = {}
                for d in op.deps:
                    k = id(d.sem)
                    if seen.get(k, 0) >= d.val:
                        continue
                    if k not in need or need[k][1] < d.val:
                        need[k] = (d.sem, d.val)
                for k, (sm, v) in need.items():
                    eo.wait_ge(sm, v)
                    seen[k] = v
                    nwaits[e] += 1
                if op.fn is None:
                    if op.sig:
                        eo.nop().then_inc(op.sem, op.inc)
                    continue
                ins = op.fn(eo)
                ninst[e] += 1
                if op.sig:
                    ins.then_inc(op.sem, op.inc)
            if e == "sp":
                for op in final_ops:
                    if seen.get(id(op.sem), 0) < op.val:
                        eo.wait_ge(op.sem, op.val)
                        seen[id(op.sem)] = op.val

        with nc.Block() as block:
            @block.tensor
            def _(eo):
                run("pe", eo)

            @block.scalar
            def _(eo):
                run("act", eo)

            @block.vector
            def _(eo):
                run("dve", eo)

            @block.gpsimd
            def _(eo):
                run("pool", eo)

            @block.sync
            def _(eo):
                run("sp", eo)
        self.nwaits = nwaits
        self.ninst = ninst


def v3(t, a, b):
    return t[:, :].rearrange("p (a b) -> p a b", a=a, b=b)


class Ctx:
    def __init__(self, nc):
        self.nc = nc
        self.P = Prog(nc)
        self.dram = {}
        self.dres = {}
        self.lmap = {}

    def dt(self, name, shape, dtype, kind="Internal"):
        t = self.nc.dram_tensor(name, list(shape), dtype, kind=kind)
        self.dram[name] = t
        self.dres[name] = self.P.res(name)
        return t


class Phase:
    def __init__(self, cx, name):
        self.cx = cx
        self.name = name
        self.st = contextlib.ExitStack()
        self.n = 0

    def __enter__(self):
        self.st.__enter__()
        return self

    def __exit__(self, *a):
        self.cx.P.barrier()
        return self.st.__exit__(*a)

    def sb(self, cols, dtype, parts=128):
        self.n += 1
        t = self.st.enter_context(self.cx.nc.sbuf_tensor("%s_sb%d" % (self.name, self.n), [parts, cols], dtype))
        return t, self.cx.P.res()

    def ps(self, cols, dtype=F32):
        self.n += 1
        t = self.st.enter_context(self.cx.nc.psum_tensor("%s_ps%d" % (self.name, self.n), [128, cols], dtype))
        return t, self.cx.P.res()


class Rot:
    def __init__(self, items):
        self.items = items
        self.i = 0

    def next(self):
        it = self.items[self.i % len(self.items)]
        self.i += 1
        return it


def load_cast(cx, stg, dst3, dres, src3, A, B, eng="pool", scols=2048):
    P = cx.P
    rows = max(1, scols // B)
    a0 = 0
    while a0 < A:
        a1 = min(A, a0 + rows)
        s, sr = stg.next()
        n = (a1 - a0) * B
        sv = s[:, 0:n].rearrange("p (a b) -> p a b", a=a1 - a0, b=B)
        P.dma("sp", sv, src3[:, a0:a1, :], writes=[sr])
        P.add(eng, lambda e, o=dst3[:, a0:a1, :], i=sv: e.tensor_copy(out=o, in_=i), [sr], [dres])
        a0 = a1


class LNState:
    def __init__(self, cx, ph, g_ap, b_ap, ident, ident_r, pT, pT_r, nbuf=2):
        P = cx.P
        self.cx = cx
        self.g, self.gr = ph.sb(D, F32)
        self.b, self.br = ph.sb(D, F32)
        P.dma("sp", self.g[:, :], g_ap.partition_broadcast(128), writes=[self.gr])
        P.dma("sp", self.b[:, :], b_ap.partition_broadcast(128), writes=[self.br])
        self.xt = Rot([ph.sb(D, F32) for _ in range(nbuf)])
        self.zt = Rot([ph.sb(D, F32) for _ in range(nbuf)])
        self.xb = Rot([ph.sb(D, BF16) for _ in range(nbuf)])
        self.xT = Rot([ph.sb(D, BF16) for _ in range(nbuf)])
        self.st = Rot([ph.sb(16, F32) for _ in range(2)])
        self.ident, self.ident_r = ident, ident_r
        self.pT, self.pT_r = pT, pT_r

    def prefetch(self, xsrc, ti):
        cx = self.cx
        xt, xtr = self.xt.next()
        cx.P.dma("sp", xt[:, :], cx.dram[xsrc][ti * 128:(ti + 1) * 128, :], writes=[xtr])
        return xt, xtr

    def tile(self, ypsum, yres, ti, xdst, xpre, xTdst):
        cx = self.cx
        P = cx.P
        xt, xtr = xpre
        zt, ztr = self.zt.next()
        xb, xbr = self.xb.next()
        xT, xTr = self.xT.next()
        st, str_ = self.st.next()
        P.dve(lambda e: e.scalar_tensor_tensor(out=zt[:, :], in0=xt[:, :], scalar=ALPHA, in1=ypsum,
                                               op0=ALU.mult, op1=ALU.add), [xtr, yres], [ztr])
        for c in range(2):
            P.dve(lambda e, c=c: e.bn_stats(out=st[:, c * 6:(c + 1) * 6], in_=zt[:, c * 512:(c + 1) * 512]),
                  [ztr], [str_])
        P.dve(lambda e: e.bn_aggr(out=st[:, 12:14], in_=st[:, 0:12]), [str_], [str_])
        P.dve(lambda e: e.tensor_scalar(out=st[:, 14:15], in0=st[:, 13:14], scalar1=LN_EPS, scalar2=None,
                                        op0=ALU.add), [str_], [str_])
        P.act(lambda e: e.activation(out=st[:, 15:16], in_=st[:, 14:15], func=AF.Sqrt), [str_], [str_])
        P.dve(lambda e: e.reciprocal(out=st[:, 14:15], in_=st[:, 15:16]), [str_], [str_])
        P.dve(lambda e: e.tensor_scalar(out=zt[:, :], in0=zt[:, :], scalar1=st[:, 12:13], scalar2=st[:, 14:15],
                                        op0=ALU.subtract, op1=ALU.mult), [str_, ztr], [ztr])
        P.pool(lambda e: e.tensor_tensor(out=zt[:, :], in0=zt[:, :], in1=self.g[:, :], op=ALU.mult),
               [ztr, self.gr], [ztr])
        P.pool(lambda e: e.tensor_tensor(out=zt[:, :], in0=zt[:, :], in1=self.b[:, :], op=ALU.add),
               [ztr, self.br], [ztr])
        P.act(lambda e: e.activation(out=xb[:, :], in_=zt[:, :], func=AF.Copy), [ztr], [xbr])
        o1 = P.dma("pool", cx.dram[xdst][ti * 128:(ti + 1) * 128, :], zt[:, :], reads=[ztr], writes=[])
        if xTdst is None:
            return o1
        for c in range(8):
            P.pe(lambda e, c=c: e.transpose(out=self.pT[:, c * 128:(c + 1) * 128], in_=xb[:, c * 128:(c + 1) * 128],
                                            identity=self.ident[:, :]), [xbr, self.ident_r], [self.pT_r])
        P.dve(lambda e: e.tensor_copy(out=xT[:, :], in_=self.pT[:, :]), [self.pT_r], [xTr])
        o2 = None
        if xTdst is not None:
            o2 = P.dma("pool", cx.dram[xTdst][:, :, ti * 128:(ti + 1) * 128], v3(xT, 8, 128), reads=[xTr], writes=[])
        return o1


def load_ident(cx, ph):
    idt, idr = ph.sb(128, BF16)
    cx.P.dma("sp", idt[:, :], cx.dram["ident"][:, :], writes=[idr])
    return idt, idr


def phase_prep(cx, xsrc, xTdst):
    P = cx.P
    with Phase(cx, "prep") as ph:
        idt, idr = load_ident(cx, ph)
        pT, pTr = ph.ps(D, BF16)
        xt = Rot([ph.sb(D, F32) for _ in range(2)])
        xb = Rot([ph.sb(D, BF16) for _ in range(2)])
        xT = Rot([ph.sb(D, BF16) for _ in range(2)])
        for ti in range(T // 128):
            a, ar = xt.next()
            b, br = xb.next()
            c_, cr = xT.next()
            P.dma("sp", a[:, :], cx.dram[xsrc][ti * 128:(ti + 1) * 128, :], writes=[ar])
            P.act(lambda e, a=a, b=b: e.activation(out=b[:, :], in_=a[:, :], func=AF.Copy), [ar], [br])
            for c in range(8):
                P.pe(lambda e, c=c, b=b: e.transpose(out=pT[:, c * 128:(c + 1) * 128], in_=b[:, c * 128:(c + 1) * 128],
                                                    identity=idt[:, :]), [br, idr], [pTr])
            P.dve(lambda e, c_=c_: e.tensor_copy(out=c_[:, :], in_=pT[:, :]), [pTr], [cr])
            P.dma("pool", cx.dram[xTdst][:, :, ti * 128:(ti + 1) * 128], v3(c_, 8, 128), reads=[cr], writes=[])


def phase_ffn(cx, layer, xsrc, xTsrc, xdst, xTdst):
    P = cx.P
    last = None
    fl = cx.lmap["ffn"][layer]
    with Phase(cx, "ffn%d" % layer) as ph:
        idt, idr = load_ident(cx, ph)
        pT, pTr = ph.ps(D, BF16)
        ln = LNState(cx, ph, cx.dram["ffn_ln_g"][layer:layer + 1, :], cx.dram["ffn_ln_b"][layer:layer + 1, :],
                     idt, idr, pT, pTr)
        xTs, xTr = ph.sb(8 * T, BF16)
        xT3 = v3(xTs, 8, T)
        xTres = [P.res() for _ in range(4)]
        for q in range(4):
            P.dma("sp", xT3[:, :, q * 512:(q + 1) * 512], cx.dram[xTsrc][:, :, q * 512:(q + 1) * 512],
                  writes=[xTres[q]])
        stg = Rot([ph.sb(2048, F32) for _ in range(2)])
        wd, wdr = ph.sb(NFC * D, BF16)
        wd3 = v3(wd, NFC, D)
        wg = Rot([ph.sb(8 * 256, BF16) for _ in range(2)])
        wu = Rot([ph.sb(8 * 256, BF16) for _ in range(2)])
        hT, _ = ph.sb(NFC * 1024, BF16)
        hT3 = v3(hT, NFC, 1024)
        hres = [[P.res() for _ in range(2)] for _ in range(NFC)]
        sg = Rot([ph.sb(512, F32) for _ in range(2)])
        pg = Rot([ph.ps(512) for _ in range(2)])
        pu = Rot([ph.ps(512) for _ in range(2)])
        py, pyr = ph.ps(D)
        wgate = cx.dram["ffn_w_gate"][fl].rearrange("(kc p) f -> p kc f", p=128)
        wup = cx.dram["ffn_w_up"][fl].rearrange("(kc p) f -> p kc f", p=128)
        wdown = cx.dram["ffn_w_down"][fl].rearrange("(fc p) d -> p fc d", p=128)
        def load_fg(hh, fg):
            g_t, g_r = wg.next()
            u_t, u_r = wu.next()
            load_cast(cx, stg, v3(g_t, 8, 256), g_r, wgate[:, :, fg * 256:(fg + 1) * 256], 8, 256)
            load_cast(cx, stg, v3(u_t, 8, 256), u_r, wup[:, :, fg * 256:(fg + 1) * 256], 8, 256)
            if hh == 0:
                load_cast(cx, stg, wd3[:, 2 * fg:2 * fg + 2, :], wdr, wdown[:, 2 * fg:2 * fg + 2, :], 2, D)
            return g_t, g_r, u_t, u_r

        seq = [(hh, fg) for hh in range(2) for fg in range(NFC // 2)]
        pending = load_fg(*seq[0])
        for si, (hh, fg) in enumerate(seq):
            g_t, g_r, u_t, u_r = pending
            if si + 1 < len(seq):
                pending = load_fg(*seq[si + 1])
            g3 = v3(g_t, 8, 256)
            u3 = v3(u_t, 8, 256)
            for fc in range(2):
                fcg = fg * 2 + fc
                for tt in range(2):
                    q = hh * 2 + tt
                    a, ar = pg.next()
                    b, br = pu.next()
                    for kc in range(8):
                        P.pe(lambda e, kc=kc, a=a, g3=g3, fc=fc, q=q: e.matmul(
                            a[:, :], lhsT=g3[:, kc, fc * 128:(fc + 1) * 128],
                            rhs=xT3[:, kc, q * 512:(q + 1) * 512], start=(kc == 0), stop=(kc == 7)),
                            [g_r, xTres[q]], [ar])
                    for kc in range(8):
                        P.pe(lambda e, kc=kc, b=b, u3=u3, fc=fc, q=q: e.matmul(
                            b[:, :], lhsT=u3[:, kc, fc * 128:(fc + 1) * 128],
                            rhs=xT3[:, kc, q * 512:(q + 1) * 512], start=(kc == 0), stop=(kc == 7)),
                            [u_r, xTres[q]], [br])
                    s, sr = sg.next()
                    P.act(lambda e, s=s, a=a: e.activation(out=s[:, :], in_=a[:, :], func=AF.Silu), [ar], [sr])
                    P.dve(lambda e, s=s, b=b, fcg=fcg, tt=tt: e.tensor_tensor(
                        out=hT3[:, fcg, tt * 512:(tt + 1) * 512], in0=s[:, :], in1=b[:, :], op=ALU.mult),
                        [sr, br], [hres[fcg][tt]])
            if fg != NFC // 2 - 1:
                continue
            for tl in range(8):
                ti = hh * 8 + tl
                xpre = ln.prefetch(xsrc, ti)
                for half in range(2):
                    for fcg in range(NFC):
                        P.pe(lambda e, fcg=fcg, tl=tl, half=half: e.matmul(
                            py[:, half * 512:(half + 1) * 512], lhsT=hT3[:, fcg, tl * 128:(tl + 1) * 128],
                            rhs=wd3[:, fcg, half * 512:(half + 1) * 512], start=(fcg == 0), stop=(fcg == NFC - 1)),
                            [hres[fcg][tl // 4], wdr], [pyr])
                last = ln.tile(py[:, :], pyr, ti, xdst, xpre, xTdst)
    return last


def phase_odd(cx, layer, xsrc, xTsrc, xdst, xTdst):
    P = cx.P
    i = cx.lmap["odd"][layer]
    last = None
    with Phase(cx, "odd%d" % layer) as ph:
        idt, idr = load_ident(cx, ph)
        pT, pTr = ph.ps(D, BF16)
        ln = LNState(cx, ph, cx.dram["mix_ln_g"][layer:layer + 1, :], cx.dram["mix_ln_b"][layer:layer + 1, :],
                     idt, idr, pT, pTr, nbuf=1)
        stg = Rot([ph.sb(1024, F32) for _ in range(2)])
        wu, wur = ph.sb(8 * 2048, BF16)
        wv, wvr = ph.sb(8 * 2048, BF16)
        wo, wor = ph.sb(16 * D, BF16)
        wst, wsr = ph.sb(8 * 128, BF16)
        wu3, wv3, wo3, ws3 = v3(wu, 8, 2048), v3(wv, 8, 2048), v3(wo, 16, D), v3(wst, 8, 128)
        win = cx.dram["odd_w_in"][i].rearrange("(kc p) f -> p kc f", p=128)
        for n in range(4):
            load_cast(cx, stg, wu3[:, :, n * 512:(n + 1) * 512], wur, win[:, :, n * 512:(n + 1) * 512], 8, 512, scols=1024)
        for n in range(4):
            load_cast(cx, stg, wv3[:, :, n * 512:(n + 1) * 512], wvr, win[:, :, 2048 + n * 512:2048 + (n + 1) * 512],
                      8, 512, scols=1024)
        load_cast(cx, stg, ws3, wsr, cx.dram["odd_wsT"][i].rearrange("g q p -> q g p"), 8, 128, scols=1024)
        load_cast(cx, stg, wo3, wor, cx.dram["odd_w_out"][i].rearrange("(cc p) d -> p cc d", p=128), 16, D, scols=1024)
        sg, sgr = ph.sb(2048, F32)
        sb_, sbr = ph.sb(2048, F32)
        bt, btr = ph.sb(2048, F32)
        P.dma("sp", sg[:, :], cx.dram["odd_sgu_norm_g"][i:i + 1, :].partition_broadcast(128), writes=[sgr])
        P.dma("sp", sb_[:, :], cx.dram["odd_sgu_norm_b"][i:i + 1, :].partition_broadcast(128), writes=[sbr])
        P.dma("sp", bt[:, :], cx.dram["odd_b16"][i:i + 1, :].partition_broadcast(128), writes=[btr])
        xTt = Rot([ph.sb(8 * 512, BF16) for _ in range(2)])
        uT, _ = ph.sb(16 * 512, BF16)
        uT3 = v3(uT, 16, 512)
        ures = [P.res() for _ in range(16)]
        vb, vbr = ph.sb(2048, BF16)
        t32, t32r = ph.sb(2048, F32)
        vn, vnr = ph.sb(2048, BF16)
        mT, mTr = ph.sb(2048, BF16)
        mT3 = v3(mT, 16, 128)
        st, str_ = ph.sb(32, F32)
        pA = Rot([ph.ps(1024) for _ in range(2)])
        pS, pSr = ph.ps(1024)
        pu, pur = ph.ps(512)
        for stile in range(4):
            xt_, xtr_ = xTt.next()
            x3 = v3(xt_, 8, 512)
            P.dma("sp", x3, cx.dram[xTsrc][:, :, stile * 512:(stile + 1) * 512], writes=[xtr_])
            for cc in range(16):
                for kc in range(8):
                    P.pe(lambda e, kc=kc, cc=cc, x3=x3: e.matmul(pu[:, :], lhsT=wu3[:, kc, cc * 128:(cc + 1) * 128],
                                                             rhs=x3[:, kc, :], start=(kc == 0), stop=(kc == 7)),
                         [wur, xtr_], [pur])
                P.act(lambda e, cc=cc: e.activation(out=uT3[:, cc, :], in_=pu[:, :], func=AF.Gelu), [pur], [ures[cc]])
            for tl in range(4):
                ti = stile * 4 + tl
                xpre = ln.prefetch(xsrc, ti)
                for vh in range(2):
                    pv, pvr = pA.next()
                    for n in range(2):
                        for kc in range(8):
                            P.pe(lambda e, kc=kc, n=n, vh=vh, pv=pv, x3=x3, tl=tl: e.matmul(
                                pv[:, n * 512:(n + 1) * 512], lhsT=x3[:, kc, tl * 128:(tl + 1) * 128],
                                rhs=wv3[:, kc, (vh * 2 + n) * 512:(vh * 2 + n + 1) * 512],
                                start=(kc == 0), stop=(kc == 7)), [wvr, xtr_], [pvr])
                    P.act(lambda e, pv=pv, vh=vh: e.activation(out=vb[:, vh * 1024:(vh + 1) * 1024], in_=pv[:, :],
                                                               func=AF.Gelu), [pvr], [vbr])
                for c in range(4):
                    P.dve(lambda e, c=c: e.bn_stats(out=st[:, c * 6:(c + 1) * 6], in_=vb[:, c * 512:(c + 1) * 512]),
                          [vbr], [str_])
                P.dve(lambda e: e.bn_aggr(out=st[:, 24:26], in_=st[:, 0:24]), [str_], [str_])
                P.dve(lambda e: e.tensor_scalar(out=st[:, 26:27], in0=st[:, 25:26], scalar1=LN_EPS, scalar2=None,
                                                op0=ALU.add), [str_], [str_])
                P.act(lambda e: e.activation(out=st[:, 27:28], in_=st[:, 26:27], func=AF.Sqrt), [str_], [str_])
                P.dve(lambda e: e.reciprocal(out=st[:, 26:27], in_=st[:, 27:28]), [str_], [str_])
                P.dve(lambda e: e.scalar_tensor_tensor(out=t32[:, :], in0=vb[:, :], scalar=st[:, 24:25], in1=sg[:, :],
                                                       op0=ALU.subtract, op1=ALU.mult), [vbr, str_, sgr], [t32r])
                P.dve(lambda e: e.scalar_tensor_tensor(out=vn[:, :], in0=t32[:, :], scalar=st[:, 26:27], in1=sb_[:, :],
                                                       op0=ALU.mult, op1=ALU.add), [t32r, str_, sbr], [vnr])
                for sh in range(2):
                    for c8 in range(8):
                        cc = sh * 8 + c8
                        P.pe(lambda e, cc=cc, c8=c8: e.matmul(pS[:, c8 * 128:(c8 + 1) * 128],
                                                               lhsT=vn[:, cc * 128:(cc + 1) * 128],
                                                               rhs=ws3[:, cc // 2, :], start=True, stop=True),
                             [vnr, wsr], [pSr])
                    P.dve(lambda e, sh=sh: e.tensor_tensor(out=t32[:, sh * 1024:(sh + 1) * 1024], in0=pS[:, :],
                                                           in1=bt[:, sh * 1024:(sh + 1) * 1024], op=ALU.add),
                          [pSr, btr], [t32r])
                P.pool(lambda e, tl=tl: e.tensor_tensor(out=mT3, in0=v3(t32, 16, 128),
                                                        in1=uT3[:, :, tl * 128:(tl + 1) * 128], op=ALU.mult),
                       [t32r] + ures, [mTr])
                py, pyr = pA.next()
                for half in range(2):
                    for cc in range(16):
                        P.pe(lambda e, cc=cc, half=half, py=py: e.matmul(
                            py[:, half * 512:(half + 1) * 512], lhsT=mT3[:, cc, :],
                            rhs=wo3[:, cc, half * 512:(half + 1) * 512], start=(cc == 0), stop=(cc == 15)),
                            [mTr, wor], [pyr])
                last = ln.tile(py[:, :], pyr, ti, xdst, xpre, xTdst)
    return last


def rms_chunks(cx, ph, pacc, paccr, pss, pssr, ones, onesr, wT3, wr, col0, nch, x3, xr, tcols, gcol, gr,
               craw, crawr, csq, csqr, rb, rbr, out3, outr, dim):
    P = cx.P
    for c in range(nch):
        pa, par = pacc.next()
        for kc in range(8):
            P.pe(lambda e, kc=kc, c=c, pa=pa: e.matmul(pa[:, :], lhsT=wT3[:, kc, col0 + c * 128:col0 + (c + 1) * 128],
                                                        rhs=x3[:, kc, tcols], start=(kc == 0), stop=(kc == 7)),
                 [wr, xr], [par])
        P.act(lambda e, c=c, pa=pa: e.activation(out=craw[:, c * 512:(c + 1) * 512], in_=pa[:, :], func=AF.Copy),
              [par], [crawr])
        P.act(lambda e, c=c, pa=pa: e.activation(out=csq[:, c * 512:(c + 1) * 512], in_=pa[:, :], func=AF.Square),
              [par], [csqr])
    for c in range(nch):
        P.pe(lambda e, c=c: e.matmul(pss[:, :], lhsT=ones[:, :], rhs=csq[:, c * 512:(c + 1) * 512],
                                     start=(c == 0), stop=(c == nch - 1)), [onesr, csqr], [pssr])
    P.dve(lambda e: e.tensor_scalar(out=rb[:, :], in0=pss[:, :], scalar1=1.0 / dim, scalar2=RMS_EPS,
                                    op0=ALU.mult, op1=ALU.add), [pssr], [rbr])
    P.act(lambda e: e.activation(out=rb[:, :], in_=rb[:, :], func=AF.Sqrt), [rbr], [rbr])
    P.dve(lambda e: e.reciprocal(out=rb[:, :], in_=rb[:, :]), [rbr], [rbr])
    for c in range(nch):
        P.dve(lambda e, c=c: e.scalar_tensor_tensor(out=out3[:, c, :], in0=craw[:, c * 512:(c + 1) * 512],
                                                    scalar=gcol[:, c:c + 1], in1=rb[:, :],
                                                    op0=ALU.mult, op1=ALU.mult), [crawr, gr, rbr], [outr])


def rotary(cx, p0, p0r, p1, p1r, cos, sin, csr, tcols, t1, t1r, t2, t2r, out, outr):
    P = cx.P
    P.dve(lambda e: e.tensor_tensor(out=t1[0:64, :], in0=p0[0:64, :], in1=cos[0:64, tcols], op=ALU.mult),
          [p0r, csr], [t1r])
    P.dve(lambda e: e.tensor_tensor(out=t2[0:64, :], in0=p1[0:64, :], in1=sin[0:64, tcols], op=ALU.mult),
          [p1r, csr], [t2r])
    P.pool(lambda e: e.tensor_tensor(out=out, in0=t1[0:64, :], in1=t2[0:64, :], op=ALU.add), [t1r, t2r], [outr])


def phase_eproj(cx, layer, xTsrc, gb):
    P = cx.P
    i = cx.lmap["even"][layer]
    GB = cx.dram[gb]
    with Phase(cx, "eproj%d" % layer) as ph:
        stg = Rot([ph.sb(2048, F32) for _ in range(2)])
        xTs, xr = ph.sb(8 * T, BF16)
        x3 = v3(xTs, 8, T)
        P.dma("sp", x3, cx.dram[xTsrc][:, :, :], writes=[xr])
        win = cx.dram["even_w_in"][i].rearrange("(kc p) f -> p kc f", p=128)
        wkv, wkvr = ph.sb(8 * 256, BF16)
        wkr, wkrr = ph.sb(8 * 256, BF16)
        wf, wfr = ph.sb(8 * 512, BF16)
        wkv3, wkr3, wf3 = v3(wkv, 8, 256), v3(wkr, 8, 256), v3(wf, 8, 512)
        P.pool(lambda e: e.memset(wkr[:, :], 0.0), [], [wkrr])
        load_cast(cx, stg, wkv3, wkvr, win[:, :, 384:640], 8, 256)
        load_cast(cx, stg, wkr3[:, :, 0:64], wkrr, win[:, :, 640:704], 8, 64)
        load_cast(cx, stg, wkr3[:, :, 128:160], wkrr, win[:, :, 672:704], 8, 32)
        load_cast(cx, stg, wkr3[:, :, 160:192], wkrr, win[:, :, 640:672], 8, 32)
        load_cast(cx, stg, wf3, wfr, win[:, :, 704:1216], 8, 512)
        ones, onesr = ph.sb(128, BF16)
        P.pool(lambda e: e.memset(ones[:, :], 1.0), [], [onesr])
        gcol, gr = ph.sb(2, F32)
        P.dma("sp", gcol[:, :], cx.dram["kvg_t"][i], writes=[gr])
        cos, csr = ph.sb(T, F32)
        sin, _ = ph.sb(T, F32)
        P.dma("sp", cos[0:64, :], cx.dram["cos2"][:, :], writes=[csr])
        P.dma("sp", sin[0:64, :], cx.dram["sin2"][:, :], writes=[csr])
        craw, crawr = ph.sb(2 * 512, F32)
        csq, csqr = ph.sb(2 * 512, BF16)
        rb, rbr = ph.sb(512, F32)
        cn = Rot([ph.sb(2 * 512, BF16) for _ in range(2)])
        t1, t1r = ph.sb(512, F32)
        t2, t2r = ph.sb(512, F32)
        kro = Rot([ph.sb(512, BF16) for _ in range(2)])
        fo = Rot([ph.sb(512, BF16) for _ in range(2)])
        pacc = Rot([ph.ps(512) for _ in range(3)])
        pss, pssr = ph.ps(512)
        pk = [ph.ps(512) for _ in range(2)]
        pf = Rot([ph.ps(512) for _ in range(2)])
        fview = GB[0:512, :].rearrange("r (a c) -> (r a) c", c=512)
        for tt in range(4):
            tcols = slice(tt * 512, (tt + 1) * 512)
            o, orr = cn.next()
            o3 = v3(o, 2, 512)
            rms_chunks(cx, ph, pacc, None, pss, pssr, ones, onesr, wkv3, wkvr, 0, 2, x3, xr, tcols, gcol, gr,
                       craw, crawr, csq, csqr, rb, rbr, o3, orr, 256.0)
            P.dma("pool", GB[512:768, tcols].rearrange("(c p) t -> p c t", p=128), o3, reads=[orr], writes=[])
            for j in range(2):
                for kc in range(8):
                    P.pe(lambda e, kc=kc, j=j, tcols=tcols: e.matmul(pk[j][0][:, :], lhsT=wkr3[:, kc, j * 128:(j + 1) * 128],
                                                                     rhs=x3[:, kc, tcols], start=(kc == 0), stop=(kc == 7)),
                         [wkrr, xr], [pk[j][1]])
            ko, kor = kro.next()
            rotary(cx, pk[0][0], pk[0][1], pk[1][0], pk[1][1], cos, sin, csr, tcols, t1, t1r, t2, t2r, ko[0:64, :], kor)
            P.dma("pool", GB[768:832, tcols], ko[0:64, :], reads=[kor], writes=[])
            for tl in range(4):
                t0 = tt * 512 + tl * 128
                p, pr = pf.next()
                for kc in range(8):
                    P.pe(lambda e, kc=kc, p=p, t0=t0: e.matmul(p[:, :], lhsT=x3[:, kc, t0:t0 + 128], rhs=wf3[:, kc, :],
                                                              start=(kc == 0), stop=(kc == 7)), [wfr, xr], [pr])
                f_, fr = fo.next()
                P.act(lambda e, p=p, f_=f_: e.activation(out=f_[:, :], in_=p[:, :], func=AF.Copy), [pr], [fr])
                P.dma("pool", fview[t0:t0 + 128, :], f_[:, :], reads=[fr], writes=[])


def phase_eq(cx, layer, xTsrc):
    P = cx.P
    i = cx.lmap["even"][layer]
    with Phase(cx, "eq%d" % layer) as ph:
        stg = Rot([ph.sb(2048, F32) for _ in range(2)])
        xTs, xr = ph.sb(8 * T, BF16)
        x3 = v3(xTs, 8, T)
        P.dma("sp", x3, cx.dram[xTsrc][:, :, :], writes=[xr])
        win = cx.dram["even_w_in"][i].rearrange("(kc p) f -> p kc f", p=128)
        wuq_d = cx.dram["even_w_uq"][i].rearrange("(kc p) f -> p kc f", p=128)
        wq1, wq1r = ph.sb(8 * 384, BF16)
        wq13 = v3(wq1, 8, 384)
        load_cast(cx, stg, wq13, wq1r, win[:, :, 0:384], 8, 384)
        wuq, wuqr = ph.sb(3 * 1536, BF16)
        wuq3 = v3(wuq, 3, 1536)
        for n in range(3):
            load_cast(cx, stg, wuq3[:, :, n * 512:(n + 1) * 512], wuqr, wuq_d[:, :, n * 512:(n + 1) * 512], 3, 512)
        wsw, wswr = ph.sb(3 * 512, BF16)
        wsw3 = v3(wsw, 3, 512)
        for h in range(8):
            b = h * 192 + 128
            load_cast(cx, stg, wsw3[:, :, h * 64:h * 64 + 32], wswr, wuq_d[:, :, b + 32:b + 64], 3, 32)
            load_cast(cx, stg, wsw3[:, :, h * 64 + 32:h * 64 + 64], wswr, wuq_d[:, :, b:b + 32], 3, 32)
        ones, onesr = ph.sb(128, BF16)
        P.pool(lambda e: e.memset(ones[:, :], 1.0), [], [onesr])
        gcol, gr = ph.sb(3, F32)
        P.dma("sp", gcol[:, :], cx.dram["qg_t"][i], writes=[gr])
        cos, csr = ph.sb(T, F32)
        sin, _ = ph.sb(T, F32)
        P.dma("sp", cos[0:64, :], cx.dram["cos2"][:, :], writes=[csr])
        P.dma("sp", sin[0:64, :], cx.dram["sin2"][:, :], writes=[csr])
        craw, crawr = ph.sb(3 * 512, F32)
        csq, csqr = ph.sb(3 * 512, BF16)
        rb, rbr = ph.sb(512, F32)
        cqn, cqnr = ph.sb(3 * 512, BF16)
        cqn3 = v3(cqn, 3, 512)
        t1, t1r = ph.sb(512, F32)
        t2, t2r = ph.sb(512, F32)
        qno = Rot([ph.sb(512, BF16) for _ in range(2)])
        qro = Rot([ph.sb(512, BF16) for _ in range(2)])
        pacc = Rot([ph.ps(512) for _ in range(2)])
        pss, pssr = ph.ps(512)
        pn = Rot([ph.ps(512) for _ in range(2)])
        pk = [ph.ps(512) for _ in range(2)]
        for tt in range(4):
            tcols = slice(tt * 512, (tt + 1) * 512)
            rms_chunks(cx, ph, pacc, None, pss, pssr, ones, onesr, wq13, wq1r, 0, 3, x3, xr, tcols, gcol, gr,
                       craw, crawr, csq, csqr, rb, rbr, cqn3, cqnr, 384.0)
            for h in range(8):
                p, pr = pn.next()
                for kc in range(3):
                    P.pe(lambda e, kc=kc, p=p, h=h: e.matmul(p[:, :], lhsT=wuq3[:, kc, h * 192:h * 192 + 128],
                                                            rhs=cqn3[:, kc, :], start=(kc == 0), stop=(kc == 2)),
                         [wuqr, cqnr], [pr])
                qn, qnr = qno.next()
                P.act(lambda e, p=p, qn=qn: e.activation(out=qn[:, :], in_=p[:, :], func=AF.Copy), [pr], [qnr])
                P.dma("pool", cx.dram["QSn"][h, :, tcols], qn[:, :], reads=[qnr], writes=[])
                for kc in range(3):
                    P.pe(lambda e, kc=kc, h=h: e.matmul(pk[0][0][0:64, :], lhsT=wuq3[:, kc, h * 192 + 128:h * 192 + 192],
                                                        rhs=cqn3[:, kc, :], start=(kc == 0), stop=(kc == 2)),
                         [wuqr, cqnr], [pk[0][1]])
                for kc in range(3):
                    P.pe(lambda e, kc=kc, h=h: e.matmul(pk[1][0][0:64, :], lhsT=wsw3[:, kc, h * 64:(h + 1) * 64],
                                                        rhs=cqn3[:, kc, :], start=(kc == 0), stop=(kc == 2)),
                         [wswr, cqnr], [pk[1][1]])
                qr_, qrr = qro.next()
                rotary(cx, pk[0][0], pk[0][1], pk[1][0], pk[1][1], cos, sin, csr, tcols, t1, t1r, t2, t2r,
                       qr_[0:64, :], qrr)
                P.dma("pool", cx.dram["QSr"][h, :, tcols], qr_[0:64, :], reads=[qrr], writes=[])


def phase_ekv(cx, layer, ga):
    P = cx.P
    i = cx.lmap["even"][layer]
    GA = cx.dram[ga]
    KSv = cx.dram["KS"].ap().rearrange("h p t -> p h t")
    VSv = cx.dram["VS"].ap().rearrange("h p k d -> p h k d")
    with Phase(cx, "ekv%d" % layer) as ph:
        stg = Rot([ph.sb(2048, F32) for _ in range(2)])
        wuk, wukr = ph.sb(2 * 1024, BF16)
        wuv, wuvr = ph.sb(2 * 1024, BF16)
        wuk3, wuv3 = v3(wuk, 2, 1024), v3(wuv, 2, 1024)
        load_cast(cx, stg, wuk3, wukr, cx.dram["even_w_uk"][i].rearrange("(c p) f -> p c f", p=128), 2, 1024)
        load_cast(cx, stg, wuv3, wuvr, cx.dram["even_w_uv"][i].rearrange("(c p) f -> p c f", p=128), 2, 1024)
        lat = Rot([ph.sb(2 * 512, BF16) for _ in range(3)])
        kt = Rot([ph.sb(8 * 512, BF16) for _ in range(2)])
        vt = Rot([ph.sb(1024, BF16) for _ in range(3)])
        pk = Rot([ph.ps(512) for _ in range(4)])
        pv = Rot([ph.ps(1024) for _ in range(2)])
        ev = 0
        for b in range(SEQ // 512):
            r, bb = b // 4, b % 4
            l, lr = lat.next()
            l3 = v3(l, 2, 512)
            P.dma("sp", l3, GA[r, 512:768, bb * 512:(bb + 1) * 512].rearrange("(c p) t -> p c t", p=128), writes=[lr])
            k_, kr_ = kt.next()
            k3 = v3(k_, 8, 512)
            for h in range(8):
                p, pr = pk.next()
                for c in range(2):
                    P.pe(lambda e, c=c, h=h, p=p, l3=l3: e.matmul(p[:, :], lhsT=wuk3[:, c, h * 128:(h + 1) * 128],
                                                                 rhs=l3[:, c, :], start=(c == 0), stop=(c == 1)),
                         [wukr, lr], [pr])
                if ev % 2 == 0:
                    P.act(lambda e, p=p, h=h, k3=k3: e.activation(out=k3[:, h, :], in_=p[:, :], func=AF.Copy), [pr], [kr_])
                else:
                    P.dve(lambda e, p=p, h=h, k3=k3: e.tensor_copy(out=k3[:, h, :], in_=p[:, :]), [pr], [kr_])
                ev += 1
            P.dma("pool", KSv[:, :, b * 512:(b + 1) * 512], k3, reads=[kr_], writes=[])
            for t4 in range(4):
                p, pr = pv.next()
                for n in range(2):
                    for c in range(2):
                        P.pe(lambda e, c=c, n=n, p=p, l3=l3, t4=t4: e.matmul(
                            p[:, n * 512:(n + 1) * 512], lhsT=l3[:, c, t4 * 128:(t4 + 1) * 128],
                            rhs=wuv3[:, c, n * 512:(n + 1) * 512], start=(c == 0), stop=(c == 1)), [wuvr, lr], [pr])
                v_, vr_ = vt.next()
                if ev % 2 == 0:
                    P.act(lambda e, p=p, v_=v_: e.activation(out=v_[:, :], in_=p[:, :], func=AF.Copy), [pr], [vr_])
                else:
                    P.dve(lambda e, p=p, v_=v_: e.tensor_copy(out=v_[:, :], in_=p[:, :]), [pr], [vr_])
                ev += 1
                P.dma("pool", VSv[:, :, b * 4 + t4, :], v3(v_, 8, 128), reads=[vr_], writes=[])


def phase_efft(cx, layer, ga, mixT):
    P = cx.P
    GA = cx.dram[ga]
    with Phase(cx, "efft%d" % layer) as ph:
        cs1, cs1r = ph.sb(256, BF16)
        t3, t3r = ph.sb(128 * 64, BF16)
        cd, cdr = ph.sb(256, BF16)
        P.dma("sp", cs1[:, :], cx.dram["cs1"][:, :], writes=[cs1r])
        P.dma("sp", t3[:, :], cx.dram["t3"][:, :], writes=[t3r])
        P.dma("sp", cd[:, :], cx.dram["cdft"][:, :], writes=[cdr])
        t33 = v3(t3, 128, 64)
        Fg = Rot([ph.sb(128 * 128, BF16) for _ in range(2)])
        Ag, Agr = ph.sb(128 * 256, BF16)
        A4 = Ag[:, :].rearrange("p (c r k) -> p c r k", c=128, r=2, k=128)
        XT, XTr = ph.sb(2 * T, BF16)
        XT4 = XT[:, :].rearrange("p (r b k) -> p r b k", r=2, b=16, k=128)
        XT3 = v3(XT, 2, T)
        yT = Rot([ph.sb(512, BF16) for _ in range(2)])
        pA = Rot([ph.ps(512) for _ in range(3)])
        pX = Rot([ph.ps(512) for _ in range(2)])
        pY = Rot([ph.ps(512) for _ in range(2)])
        ev = 0
        for g in range(4):
            F, Fr = Fg.next()
            F3 = v3(F, 128, 128)
            for r in range(NCORES):
                src = GA[r, 0:512, :].rearrange("r (a c) -> (r a) c", c=512)[:, g * 128:(g + 1) * 128]
                P.dma("sp", F3[r * 16:(r + 1) * 16, :, :], src.rearrange("(t s) c -> t s c", s=128), writes=[Fr])
            for c2 in range(64):
                p, pr = pA.next()
                for j in range(2):
                    c = c2 * 2 + j
                    P.pe(lambda e, c=c, j=j, p=p, F3=F3: e.matmul(p[:, j * 256:(j + 1) * 256], lhsT=F3[:, :, c],
                                                                 rhs=cs1[:, :], start=True, stop=True),
                         [Fr, cs1r], [pr])
                dst = Ag[:, c2 * 512:(c2 + 1) * 512]
                if ev % 2 == 0:
                    P.act(lambda e, p=p, dst=dst: e.activation(out=dst, in_=p[:, :], func=AF.Copy), [pr], [Agr])
                else:
                    P.dve(lambda e, p=p, dst=dst: e.tensor_copy(out=dst, in_=p[:, :]), [pr], [Agr])
                ev += 1
            for kb in range(8):
                p, pr = pX.next()
                for kl in range(16):
                    k1 = kb * 16 + kl
                    P.pe(lambda e, k1=k1, kl=kl, p=p: e.matmul(p[:, kl * 32:(kl + 1) * 32], lhsT=A4[:, :, 0, k1],
                                                               rhs=t33[:, k1, 0:32], start=True, stop=False),
                         [Agr, t3r], [pr])
                    P.pe(lambda e, k1=k1, kl=kl, p=p: e.matmul(p[:, kl * 32:(kl + 1) * 32], lhsT=A4[:, :, 1, k1],
                                                               rhs=t33[:, k1, 32:64], start=False, stop=True),
                         [Agr, t3r], [pr])
                p3 = p[:, :].rearrange("p (k x) -> p k x", k=16, x=32)
                for ri in range(2):
                    src = p3[:, :, ri * 16:(ri + 1) * 16].rearrange("p k b -> p b k")
                    dst = XT4[:, ri, :, kb * 16:(kb + 1) * 16]
                    if ri == 0:
                        P.act(lambda e, src=src, dst=dst: e.activation(out=dst, in_=src, func=AF.Copy), [pr], [XTr])
                    else:
                        P.dve(lambda e, src=src, dst=dst: e.tensor_copy(out=dst, in_=src), [pr], [XTr])
            for tt in range(4):
                p, pr = pY.next()
                P.pe(lambda e, p=p, tt=tt: e.matmul(p[:, :], lhsT=cd[:, 0:128], rhs=XT3[:, 0, tt * 512:(tt + 1) * 512],
                                                    start=True, stop=False), [cdr, XTr], [pr])
                P.pe(lambda e, p=p, tt=tt: e.matmul(p[:, :], lhsT=cd[:, 128:256], rhs=XT3[:, 1, tt * 512:(tt + 1) * 512],
                                                    start=False, stop=True), [cdr, XTr], [pr])
                y, yr = yT.next()
                P.act(lambda e, p=p, y=y: e.activation(out=y[:, :], in_=p[:, :], func=AF.Copy), [pr], [yr])
                P.dma("pool", cx.dram[mixT][8 + g, :, tt * 512:(tt + 1) * 512], y[:, :], reads=[yr], writes=[])


def phase_eattn(cx, layer, ga, mixT):
    P = cx.P
    GA = cx.dram[ga]
    NCH = 8
    with Phase(cx, "eattn%d" % layer) as ph:
        ones, onesr = ph.sb(128, BF16)
        P.pool(lambda e: e.memset(ones[:, :], 1.0), [], [onesr])
        KR, _ = ph.sb(SEQ, BF16)
        krres = [P.res() for _ in range(NCH)]
        zr = P.res()
        P.pool(lambda e: e.memset(KR[64:128, :], 0.0), [], [zr])
        for r in range(NCH):
            P.dma("sp", KR[0:64, r * 2048:(r + 1) * 2048], GA[r, 768:832, :], writes=[krres[r]])
        KN, _ = ph.sb(SEQ, BF16)
        knres = [P.res() for _ in range(NCH)]
        VV, _ = ph.sb(SEQ, BF16)
        VV3 = v3(VV, 128, 128)
        vres = [P.res() for _ in range(NCH)]
        QN = Rot([ph.sb(T, BF16) for _ in range(2)])
        QR = Rot([ph.sb(T, BF16) for _ in range(2)])
        for (q_, qr_) in QR.items:
            P.pool(lambda e, q_=q_: e.memset(q_[64:128, :], 0.0), [], [qr_])
        PT = Rot([ph.sb(1024, BF16) for _ in range(3)])
        rs = Rot([ph.sb(512, F32) for _ in range(2)])
        ob = Rot([ph.sb(512, BF16) for _ in range(2)])
        pS = Rot([ph.ps(1024) for _ in range(2)])
        pO = Rot([ph.ps(512) for _ in range(2)])
        pR = Rot([ph.ps(512) for _ in range(2)])
        KSd, VSd = cx.dram["KS"], cx.dram["VS"]
        for h in range(8):
            for c in range(NCH):
                P.dma("sp", KN[:, c * 2048:(c + 1) * 2048], KSd[h, :, c * 2048:(c + 1) * 2048], writes=[knres[c]])
                P.dma("sp", VV3[:, c * 16:(c + 1) * 16, :], VSd[h, :, c * 16:(c + 1) * 16, :], writes=[vres[c]])
            qn, qnr = QN.next()
            qr, qrr = QR.next()
            P.dma("sp", qn[:, :], cx.dram["QSn"][h, :, :], writes=[qnr])
            P.dma("sp", qr[0:64, :], cx.dram["QSr"][h, :, :], writes=[qrr])
            for qt in range(4):
                qc = slice(qt * 512, (qt + 1) * 512)
                po, por = pO.next()
                pr_, prr = pR.next()

                def qk(pi, qn=qn, qr=qr, qnr=qnr, qrr=qrr, qc=qc):
                    s, sr = pS.next()
                    for j in range(2):
                        kt = pi * 2 + j
                        c = kt // 16
                        kc = slice(kt * 128, (kt + 1) * 128)
                        P.pe(lambda e, s=s, j=j, kc=kc: e.matmul(s[:, j * 512:(j + 1) * 512], lhsT=KN[:, kc], rhs=qn[:, qc],
                                                                 start=True, stop=False), [knres[c], qnr], [sr])
                        P.pe(lambda e, s=s, j=j, kc=kc: e.matmul(s[:, j * 512:(j + 1) * 512], lhsT=KR[:, kc], rhs=qr[:, qc],
                                                                 start=False, stop=True), [krres[c], zr, qrr], [sr])
                    return s, sr

                def pv(pi, pt, ptr, po=po, por=por, pr_=pr_, prr=prr):
                    for j in range(2):
                        kt = pi * 2 + j
                        c = kt // 16
                        P.pe(lambda e, j=j, kt=kt, pt=pt: e.matmul(po[:, :], lhsT=VV3[:, kt, :], rhs=pt[:, j * 512:(j + 1) * 512],
                                                                   start=(kt == 0), stop=(kt == 127)), [vres[c], ptr], [por])
                        P.pe(lambda e, j=j, kt=kt, pt=pt: e.matmul(pr_[:, :], lhsT=ones[:, :], rhs=pt[:, j * 512:(j + 1) * 512],
                                                                   start=(kt == 0), stop=(kt == 127)), [onesr, ptr], [prr])

                pend = None
                for pi in range(64):
                    s, sr = qk(pi)
                    pt, ptr = PT.next()
                    P.act(lambda e, s=s, pt=pt: e.activation(out=pt[:, :], in_=s[:, :], func=AF.Exp, scale=SCALE),
                          [sr], [ptr])
                    if pend is not None:
                        pv(*pend)
                    pend = (pi, pt, ptr)
                pv(*pend)
                r_, rr = rs.next()
                o_, orr = ob.next()
                P.dve(lambda e, r_=r_, pr_=pr_: e.reciprocal(out=r_[:, :], in_=pr_[:, :]), [prr], [rr])
                P.dve(lambda e, r_=r_, o_=o_, po=po: e.tensor_tensor(out=o_[:, :], in0=po[:, :], in1=r_[:, :], op=ALU.mult),
                      [por, rr], [orr])
                P.dma("pool", cx.dram[mixT][h, :, qc], o_[:, :], reads=[orr], writes=[])


def phase_eout(cx, layer, mixT, xsrc, xdst, xTdst):
    P = cx.P
    i = cx.lmap["even"][layer]
    last = None
    with Phase(cx, "eout%d" % layer) as ph:
        idt, idr = load_ident(cx, ph)
        pT, pTr = ph.ps(D, BF16)
        ln = LNState(cx, ph, cx.dram["mix_ln_g"][layer:layer + 1, :], cx.dram["mix_ln_b"][layer:layer + 1, :],
                     idt, idr, pT, pTr)
        stg = Rot([ph.sb(2048, F32) for _ in range(2)])
        wo, wor = ph.sb(12 * D, BF16)
        wo3 = v3(wo, 12, D)
        load_cast(cx, stg, wo3, wor, cx.dram["even_w_out"][i].rearrange("(c p) d -> p c d", p=128), 12, D)
        mx, mxr = ph.sb(12 * T, BF16)
        mx3 = v3(mx, 12, T)
        P.dma("sp", mx3, cx.dram[mixT].ap().rearrange("c p t -> p c t"), writes=[mxr])
        pY = Rot([ph.ps(D) for _ in range(2)])
        for ti in range(T // 128):
            xpre = ln.prefetch(xsrc, ti)
            py, pyr = pY.next()
            for half in range(2):
                for c in range(12):
                    P.pe(lambda e, c=c, half=half, py=py, ti=ti: e.matmul(
                        py[:, half * 512:(half + 1) * 512], lhsT=mx3[:, c, ti * 128:(ti + 1) * 128],
                        rhs=wo3[:, c, half * 512:(half + 1) * 512], start=(c == 0), stop=(c == 11)), [mxr, wor], [pyr])
            last = ln.tile(py[:, :], pyr, ti, xdst, xpre, xTdst)
    return last


W_EVEN = [("even_w_in", [D, 1216]), ("even_w_uq", [384, 1536]), ("even_w_uk", [256, 1024]),
          ("even_w_uv", [256, 1024]), ("even_w_out", [1536, D]), ("kvg_t", [128, 2]), ("qg_t", [128, 3])]
W_ODD = [("odd_w_in", [D, 4096]), ("odd_sgu_norm_g", [2048]), ("odd_sgu_norm_b", [2048]),
         ("odd_wsT", [8, 128, 128]), ("odd_b16", [2048]), ("odd_w_out", [2048, D])]
W_FFN = [("ffn_w_gate", [D, DFF]), ("ffn_w_up", [D, DFF]), ("ffn_w_down", [DFF, D])]
W_LN = [("mix_ln_g", [4, D]), ("mix_ln_b", [4, D]), ("ffn_ln_g", [4, D]), ("ffn_ln_b", [4, D])]
CONSTS = [("ident", [128, 128], BF16), ("cos2", [64, T], F32), ("sin2", [64, T], F32), ("cs1", [128, 256], BF16),
          ("t3", [128, 128 * 64], BF16), ("cdft", [128, 256], BF16)]

PROG_LAYERS = {"M": ([0], [], []), "A": ([0], [], []), "B": ([0, 2], [1], [0, 1]), "C": ([2], [3], [2, 3]),
               "F": ([0, 2], [1, 3], [0, 1, 2, 3])}


def build_program(kind):
    nc = bass.Bass("TRN2", target_bir_lowering=False)
    cx = Ctx(nc)
    ev, od, ff = PROG_LAYERS[kind]
    cx.lmap = {"even": {l: k for k, l in enumerate(ev)}, "odd": {l: k for k, l in enumerate(od)},
               "ffn": {l: k for k, l in enumerate(ff)}}
    cx.dt("x", [T, D], F32, "ExternalInput")
    for n, shp, dt_ in CONSTS:
        cx.dt(n, shp, dt_, "ExternalInput")
    for n, shp in W_LN:
        cx.dt(n, shp, F32, "ExternalInput")
    for grp, lays in ((W_EVEN, ev), (W_ODD, od), (W_FFN, ff)):
        if lays:
            for n, shp in grp:
                cx.dt(n, [len(lays)] + shp, F32, "ExternalInput")
    for n in ("xs0", "xs1"):
        cx.dt(n, [T, D], F32)
    for n in ("xT0", "xT1"):
        cx.dt(n, [128, 8, T], BF16)
    dk = "ExternalOutput" if kind == "M" else "Internal"
    cx.dt("QSn", [8, 128, T], BF16, dk)
    cx.dt("QSr", [8, 64, T], BF16, dk)
    cx.dt("KS", [8, 128, SEQ], BF16)
    cx.dt("VS", [8, 128, 128, 128], BF16)
    cx.dt("mixT", [12, 128, T], BF16, dk)
    P = cx.P

    def main_even(layer, ga, xsrc, xTsrc, xdst, xTdst):
        phase_eq(cx, layer, xTsrc)
        phase_ekv(cx, layer, ga)
        phase_efft(cx, layer, ga, "mixT")
        phase_eattn(cx, layer, ga, "mixT")
        phase_eout(cx, layer, "mixT", xsrc, xdst, xTdst)

    if kind == "A":
        cx.dt("gb", [GROWS, 2048], BF16, "ExternalOutput")
        phase_prep(cx, "x", "xT0")
        phase_eproj(cx, 0, "xT0", "gb")
    elif kind == "B":
        cx.dt("ga", [NCORES, GROWS, 2048], BF16, "ExternalInput")
        cx.dt("gb", [GROWS, 2048], BF16, "ExternalOutput")
        cx.dt("xmid", [T, D], F32, "ExternalOutput")
        phase_prep(cx, "x", "xT0")
        main_even(0, "ga", "x", "xT0", "xs1", "xT1")
        phase_ffn(cx, 0, "xs1", "xT1", "xs0", "xT0")
        phase_odd(cx, 1, "xs0", "xT0", "xs1", "xT1")
        phase_ffn(cx, 1, "xs1", "xT1", "xmid", "xT0")
        phase_eproj(cx, 2, "xT0", "gb")
    elif kind == "M":
        cx.dt("ga", [NCORES, GROWS, 2048], BF16, "ExternalInput")
        cx.dt("xmid", [T, D], F32, "ExternalOutput")
        phase_prep(cx, "x", "xT0")
        main_even(0, "ga", "x", "xT0", "xmid", None)
    elif kind == "C":
        cx.dt("ga", [NCORES, GROWS, 2048], BF16, "ExternalInput")
        cx.dt("out", [T, D], F32, "ExternalOutput")
        phase_prep(cx, "x", "xT0")
        main_even(2, "ga", "x", "xT0", "xs1", "xT1")
        phase_ffn(cx, 2, "xs1", "xT1", "xs0", "xT0")
        phase_odd(cx, 3, "xs0", "xT0", "xs1", "xT1")
        phase_ffn(cx, 3, "xs1", "xT1", "out", None)
    else:
        raise ValueError(kind)
    P.emit()
    return nc, cx


def _bf(a):
    return np.asarray(a, dtype=np.float32).astype(ml_dtypes.bfloat16)


def _const_tables(core):
    inv = 1.0 / (10000.0 ** (np.arange(0, 64, 2, dtype=np.float64) / 64.0))
    pos = np.arange(core * T, (core + 1) * T, dtype=np.float64)
    ang = (pos[:, None].astype(np.float32) * inv[None, :].astype(np.float32)).astype(np.float32)
    c, s_ = np.cos(ang.astype(np.float64)), np.sin(ang.astype(np.float64))
    cos2 = np.concatenate([c.T, c.T], 0).astype(np.float32)
    sin2 = np.concatenate([-s_.T, s_.T], 0).astype(np.float32)
    n = np.arange(128, dtype=np.float64)
    th = 2 * np.pi * np.outer(n, n) / 128.0
    cs1 = np.concatenate([np.cos(th), -np.sin(th)], 1)
    norm = 1.0 / np.sqrt(SEQ * 128.0)
    cdft = np.concatenate([np.cos(th) * norm, np.sin(th) * norm], 1)
    s2 = np.arange(128, dtype=np.float64)[:, None, None]
    k1 = np.arange(128, dtype=np.float64)[None, :, None]
    k2 = (16 * core + np.arange(16, dtype=np.float64))[None, None, :]
    k = k1 + 128.0 * k2
    ph = 2 * np.pi * ((s2 * k) % SEQ) / SEQ
    mre, mim = np.cos(ph), -np.sin(ph)
    t3 = np.concatenate([mre, mim, -mim, mre], 2).reshape(128, 128 * 64)
    return {"ident": _bf(np.eye(128)), "cos2": np.ascontiguousarray(cos2), "sin2": np.ascontiguousarray(sin2),
            "cs1": _bf(cs1), "t3": _bf(t3), "cdft": _bf(cdft)}


def _weights(inp, kind):
    ev, od, ff = PROG_LAYERS[kind]
    w = {}
    for n in ("mix_ln_g", "mix_ln_b", "ffn_ln_g", "ffn_ln_b"):
        w[n] = np.ascontiguousarray(inp[n], dtype=np.float32)
    if ev:
        ei = [l // 2 for l in ev]
        for n in ("even_w_in", "even_w_uq", "even_w_uk", "even_w_uv", "even_w_out"):
            w[n] = np.ascontiguousarray(np.asarray(inp[n], dtype=np.float32)[ei])
        w["kvg_t"] = np.ascontiguousarray(np.asarray(inp["even_kv_norm"], dtype=np.float32)[ei].reshape(len(ei), 2, 128).transpose(0, 2, 1))
        w["qg_t"] = np.ascontiguousarray(np.asarray(inp["even_q_norm"], dtype=np.float32)[ei].reshape(len(ei), 3, 128).transpose(0, 2, 1))
    if od:
        oi = [l // 2 for l in od]
        for n in ("odd_w_in", "odd_sgu_norm_g", "odd_sgu_norm_b", "odd_w_out"):
            w[n] = np.ascontiguousarray(np.asarray(inp[n], dtype=np.float32)[oi])
        w["odd_wsT"] = np.ascontiguousarray(np.asarray(inp["odd_w_spatial"], dtype=np.float32)[oi].transpose(0, 1, 3, 2))
        w["odd_b16"] = np.ascontiguousarray(np.repeat(np.asarray(inp["odd_b_spatial"], dtype=np.float32)[oi], 2, axis=1).reshape(len(oi), 2048))
    if ff:
        for n in ("ffn_w_gate", "ffn_w_up", "ffn_w_down"):
            w[n] = np.ascontiguousarray(np.asarray(inp[n], dtype=np.float32)[ff])
    return w


_PROGS = {}


def _prog(kind):
    if kind not in _PROGS:
        _PROGS[kind] = build_program(kind)[0]
    return _PROGS[kind]


def _launch(kind, inp, xs, ga=None):
    w = _weights(inp, kind)
    maps = []
    for c in range(NCORES):
        m = dict(w)
        m.update(_const_tables(c))
        m["x"] = np.ascontiguousarray(xs[c], dtype=np.float32)
        if ga is not None:
            m["ga"] = ga
        maps.append(m)
    res = run_bass_kernel_spmd(_prog(kind), maps, core_ids=list(range(NCORES)))
    return res.results


def kernel(**inp):
    x = np.asarray(inp["x"], dtype=np.float32).reshape(SEQ, D)
    xs = [x[c * T:(c + 1) * T] for c in range(NCORES)]
    ra = _launch("A", inp, xs)
    ga = np.ascontiguousarray(np.stack([ra[c]["gb"] for c in range(NCORES)], 0))
    rb = _launch("B", inp, xs, ga)
    ga = np.ascontiguousarray(np.stack([rb[c]["gb"] for c in range(NCORES)], 0))
    xs = [rb[c]["xmid"] for c in range(NCORES)]
    rc = _launch("C", inp, xs, ga)
    out = np.concatenate([rc[c]["out"] for c in range(NCORES)], 0)
    return out.reshape(1, SEQ, D).astype(np.float32)
```

```python
import contextlib
import numpy as np
import ml_dtypes
import concourse.bass as bass
import concourse.mybir as mybir
from concourse.bass_utils import run_bass_kernel_spmd

F32 = mybir.dt.float32
BF16 = mybir.dt.bfloat16
AF = mybir.ActivationFunctionType
ALU = mybir.AluOpType

NCORES = 8
SEQ = 16384
T = SEQ // NCORES
D = 1024
DFF = 2816
NFC = DFF // 128
DEPTH = 4
ALPHA = float((2 * DEPTH) ** 0.25)
LN_EPS = 1e-5
RMS_EPS = 1e-6
GROWS = 832
SCALE = float(192 ** -0.5)
NO_CC = False
DUMMY_CC = False

ENGS = ["pe", "act", "dve", "pool", "sp"]
SEM_EPOCH = 20000
NDMA_SEMS = {"sp": 16, "pool": 8, "act": 4, "pe": 1, "dve": 1}


class Res:
    __slots__ = ("name", "w", "r")

    def __init__(self, name=""):
        self.name = name
        self.w = None
        self.r = []


class Op:
    __slots__ = ("eng", "fn", "deps", "sig", "sem", "val", "dma", "inc")


class Prog:
    def __init__(self, nc):
        self.nc = nc
        self.ops = {e: [] for e in ENGS}
        self.n_dma = {e: 0 for e in ENGS}
        self.dma_last = {}
        self.all_res = []

    def res(self, name=""):
        r = Res(name)
        self.all_res.append(r)
        return r

    def add(self, eng, fn, reads=(), writes=(), dma=False, inc=None, extra=()):
        op = Op()
        op.inc = inc if inc is not None else (16 if dma else 1)
        op.eng = eng
        op.fn = fn
        op.dma = dma
        op.sig = dma
        op.sem = None
        op.val = 0
        deps = set(extra)
        for r in reads:
            if r.w is not None:
                deps.add(r.w)
        for w in writes:
            if w.w is not None:
                deps.add(w.w)
            deps.update(w.r)
        for r in reads:
            r.r.append(op)
        for w in writes:
            w.w = op
            w.r = []
        deps.discard(op)
        if dma:
            if op.inc == 16:
                k = (eng, self.n_dma[eng] % NDMA_SEMS[eng])
                self.n_dma[eng] += 1
            else:
                k = (eng, "cc")
            prev = self.dma_last.get(k)
            if prev is not None:
                deps.add(prev)
            self.dma_last[k] = op
            op.sem = k
        if eng == "pe" and not dma:
            deps = {d for d in deps if not (d.eng == "pe" and not d.dma)}
        op.deps = deps
        self.ops[eng].append(op)
        return op

    def pe(self, fn, reads=(), writes=()):
        return self.add("pe", fn, reads, writes)

    def act(self, fn, reads=(), writes=()):
        return self.add("act", fn, reads, writes)

    def dve(self, fn, reads=(), writes=()):
        return self.add("dve", fn, reads, writes)

    def pool(self, fn, reads=(), writes=()):
        return self.add("pool", fn, reads, writes)

    def dma(self, q, out, in_, reads=(), writes=()):
        return self.add(q, lambda e: e.dma_start(out=out, in_=in_), reads, writes, dma=True)

    def barrier(self):
        last = []
        for e in ENGS:
            for op in reversed(self.ops[e]):
                if not op.dma:
                    last.append(op)
                    break
        last.extend(self.dma_last.values())
        for r in self.all_res:
            r.w = None
            r.r = []
        for e in ENGS:
            self.add(e, None, extra=last)

    def emit(self, final_ops=()):
        nc = self.nc
        for e in ENGS:
            for op in self.ops[e]:
                for d in op.deps:
                    d.sig = True
        for op in final_ops:
            op.sig = True
        sems = {}

        def get_sem(key):
            if key not in sems:
                sems[key] = nc.alloc_semaphore("s_%s_%s" % key)
            return sems[key]

        dma_cnt = {}
        for e in ENGS:
            c = 0
            for op in self.ops[e]:
                if op.dma:
                    k = op.sem
                    dma_cnt[k] = dma_cnt.get(k, 0) + op.inc
                    op.sem = get_sem(("d" + k[0], k[1]))
                    op.val = dma_cnt[k]
                elif op.sig:
                    ep = c // SEM_EPOCH
                    c += 1
                    op.sem = get_sem((e, ep))
                    op.val = c - ep * SEM_EPOCH
        self.nsems = len(sems)
        nwaits = {e: 0 for e in ENGS}
        ninst = {e: 0 for e in ENGS}

        def run(e, eo):
            seen = {}
            for op in self.ops[e]:
                need = {}
                for d in op.deps:
                    k = id(d.sem)
                    if seen.get(k, 0) >= d.val:
                        continue
                    if k not in need or need[k][1] < d.val:
                        need[k] = (d.sem, d.val)
                for k, (sm, v) in need.items():
                    eo.wait_ge(sm, v)
                    seen[k] = v
                    nwaits[e] += 1
                if op.fn is None:
                    if op.sig:
                        eo.nop().then_inc(op.sem, op.inc)
                    continue
                ins = op.fn(eo)
                ninst[e] += 1
                if op.sig:
                    ins.then_inc(op.sem, op.inc)
            if e == "sp":
                for op in final_ops:
                    if seen.get(id(op.sem), 0) < op.val:
                        eo.wait_ge(op.sem, op.val)
                        seen[id(op.sem)] = op.val

        with nc.Block() as block:
            @block.tensor
            def _(eo):
                run("pe", eo)

            @block.scalar
            def _(eo):
                run("act", eo)

            @block.vector
            def _(eo):
                run("dve", eo)

            @block.gpsimd
            def _(eo):
                run("pool", eo)

            @block.sync
            def _(eo):
                run("sp", eo)
        self.nwaits = nwaits
        self.ninst = ninst


def v3(t, a, b):
    return t[:, :].rearrange("p (a b) -> p a b", a=a, b=b)


class Ctx:
    def __init__(self, nc):
        self.nc = nc
        self.P = Prog(nc)
        self.dram = {}
        self.dres = {}
        self.lmap = {}

    def dt(self, name, shape, dtype, kind="Internal"):
        t = self.nc.dram_tensor(name, list(shape), dtype, kind=kind)
        self.dram[name] = t
        self.dres[name] = self.P.res(name)
        return t


class Phase:
    def __init__(self, cx, name):
        self.cx = cx
        self.name = name
        self.st = contextlib.ExitStack()
        self.n = 0

    def __enter__(self):
        self.st.__enter__()
        return self

    def __exit__(self, *a):
        self.cx.P.barrier()
        return self.st.__exit__(*a)

    def sb(self, cols, dtype, parts=128):
        self.n += 1
        t = self.st.enter_context(self.cx.nc.sbuf_tensor("%s_sb%d" % (self.name, self.n), [parts, cols], dtype))
        return t, self.cx.P.res()

    def ps(self, cols, dtype=F32):
        self.n += 1
        t = self.st.enter_context(self.cx.nc.psum_tensor("%s_ps%d" % (self.name, self.n), [128, cols], dtype))
        return t, self.cx.P.res()


class Rot:
    def __init__(self, items):
        self.items = items
        self.i = 0

    def next(self):
        it = self.items[self.i % len(self.items)]
        self.i += 1
        return it


def load_cast(cx, stg, dst3, dres, src3, A, B, eng="pool", scols=2048):
    cx.P.dma("pool", dst3, src3, writes=[dres])


class LNState:
    def __init__(self, cx, ph, g_ap, b_ap, ident, ident_r, pT, pT_r, nbuf=2):
        P = cx.P
        self.cx = cx
        self.g, self.gr = ph.sb(D, F32)
        self.b, self.br = ph.sb(D, F32)
        P.dma("sp", self.g[:, :], g_ap.partition_broadcast(128), writes=[self.gr])
        P.dma("sp", self.b[:, :], b_ap.partition_broadcast(128), writes=[self.br])
        self.xt = Rot([ph.sb(D, F32) for _ in range(nbuf)])
        self.zt = Rot([ph.sb(D, F32) for _ in range(nbuf)])
        self.xb = Rot([ph.sb(D, BF16) for _ in range(nbuf)])
        self.xT = Rot([ph.sb(D, BF16) for _ in range(nbuf)])
        self.st = Rot([ph.sb(16, F32) for _ in range(2)])
        self.mh, self.mhr = ph.sb(1, F32)
        P.pool(lambda e: e.memset(self.mh[:, :], -0.5), [], [self.mhr])
        self.ident, self.ident_r = ident, ident_r
        self.pT, self.pT_r = pT, pT_r

    def prefetch(self, xsrc, ti):
        cx = self.cx
        xt, xtr = self.xt.next()
        cx.P.dma("sp", xt[:, :], cx.dram[xsrc][ti * 128:(ti + 1) * 128, :], writes=[xtr])
        return xt, xtr

    def tile(self, ypsum, yres, ti, xdst, xpre, xTdst):
        cx = self.cx
        P = cx.P
        self.flush()
        xt, xtr = xpre
        zt, ztr = self.zt.next()
        xb, xbr = self.xb.next()
        xT, xTr = self.xT.next()
        st, str_ = self.st.next()
        P.dve(lambda e: e.scalar_tensor_tensor(out=zt[:, :], in0=xt[:, :], scalar=ALPHA, in1=ypsum,
                                               op0=ALU.mult, op1=ALU.add), [xtr, yres], [ztr])
        for c in range(2):
            P.dve(lambda e, c=c: e.bn_stats(out=st[:, c * 6:(c + 1) * 6], in_=zt[:, c * 512:(c + 1) * 512]),
                  [ztr], [str_])
        P.dve(lambda e: e.bn_aggr(out=st[:, 12:14], in_=st[:, 0:12]), [str_], [str_])
        P.dve(lambda e: e.tensor_scalar(out=st[:, 14:15], in0=st[:, 13:14], scalar1=LN_EPS, scalar2=None,
                                        op0=ALU.add), [str_], [str_])
        P.pool(lambda e: e.tensor_tensor(out=st[:, 14:15], in0=st[:, 14:15], in1=self.mh[:, :], op=ALU.pow),
               [str_, self.mhr], [str_])
        P.dve(lambda e: e.tensor_scalar(out=zt[:, :], in0=zt[:, :], scalar1=st[:, 12:13], scalar2=st[:, 14:15],
                                        op0=ALU.subtract, op1=ALU.mult), [str_, ztr], [ztr])
        P.pool(lambda e: e.tensor_tensor(out=zt[:, :], in0=zt[:, :], in1=self.g[:, :], op=ALU.mult),
               [ztr, self.gr], [ztr])
        P.pool(lambda e: e.tensor_tensor(out=zt[:, :], in0=zt[:, :], in1=self.b[:, :], op=ALU.add),
               [ztr, self.br], [ztr])
        P.act(lambda e: e.activation(out=xb[:, :], in_=zt[:, :], func=AF.Copy), [ztr], [xbr])
        o1 = P.dma("pool", cx.dram[xdst][ti * 128:(ti + 1) * 128, :], zt[:, :], reads=[ztr], writes=[])
        if xTdst is not None:
            self.pending = (xb, xbr, xT, xTr, ti, xTdst)
        return o1

    pending = None

    def flush(self):
        if self.pending is None:
            return
        cx = self.cx
        P = cx.P
        xb, xbr, xT, xTr, ti, xTdst = self.pending
        self.pending = None
        for c in range(8):
            P.pe(lambda e, c=c: e.transpose(out=self.pT[:, c * 128:(c + 1) * 128], in_=xb[:, c * 128:(c + 1) * 128],
                                            identity=self.ident[:, :]), [xbr, self.ident_r], [self.pT_r])
        P.dve(lambda e: e.tensor_copy(out=xT[:, :], in_=self.pT[:, :]), [self.pT_r], [xTr])
        P.dma("pool", cx.dram[xTdst][:, :, ti * 128:(ti + 1) * 128], v3(xT, 8, 128), reads=[xTr], writes=[])


def load_ident(cx, ph):
    idt, idr = ph.sb(128, BF16)
    cx.P.dma("sp", idt[:, :], cx.dram["ident"][:, :], writes=[idr])
    return idt, idr


def phase_prep(cx, xsrc, xTdst):
    P = cx.P
    with Phase(cx, "prep") as ph:
        idt, idr = load_ident(cx, ph)
        pT, pTr = ph.ps(D, BF16)
        xt = Rot([ph.sb(D, F32) for _ in range(2)])
        xb = Rot([ph.sb(D, BF16) for _ in range(2)])
        xT = Rot([ph.sb(D, BF16) for _ in range(2)])
        for ti in range(T // 128):
            a, ar = xt.next()
            b, br = xb.next()
            c_, cr = xT.next()
            P.dma("sp", a[:, :], cx.dram[xsrc][ti * 128:(ti + 1) * 128, :], writes=[ar])
            P.act(lambda e, a=a, b=b: e.activation(out=b[:, :], in_=a[:, :], func=AF.Copy), [ar], [br])
            for c in range(8):
                P.pe(lambda e, c=c, b=b: e.transpose(out=pT[:, c * 128:(c + 1) * 128], in_=b[:, c * 128:(c + 1) * 128],
                                                    identity=idt[:, :]), [br, idr], [pTr])
            P.dve(lambda e, c_=c_: e.tensor_copy(out=c_[:, :], in_=pT[:, :]), [pTr], [cr])
            P.dma("pool", cx.dram[xTdst][:, :, ti * 128:(ti + 1) * 128], v3(c_, 8, 128), reads=[cr], writes=[])


def phase_ffn(cx, layer, xsrc, xTsrc, xdst, xTdst):
    P = cx.P
    last = None
    fl = cx.lmap["ffn"][layer]
    with Phase(cx, "ffn%d" % layer) as ph:
        idt, idr = load_ident(cx, ph)
        pT, pTr = ph.ps(D, BF16)
        ln = LNState(cx, ph, cx.dram["ffn_ln_g"][layer:layer + 1, :], cx.dram["ffn_ln_b"][layer:layer + 1, :],
                     idt, idr, pT, pTr)
        xTs, xTr = ph.sb(8 * T, BF16)
        xT3 = v3(xTs, 8, T)
        xTres = [P.res() for _ in range(4)]
        for q in range(4):
            P.dma("sp", xT3[:, :, q * 512:(q + 1) * 512], cx.dram[xTsrc][:, :, q * 512:(q + 1) * 512],
                  writes=[xTres[q]])
        stg = None
        wd, wdr = ph.sb(NFC * D, BF16)
        wd3 = v3(wd, NFC, D)
        wg = Rot([ph.sb(8 * 256, BF16) for _ in range(2)])
        wu = Rot([ph.sb(8 * 256, BF16) for _ in range(2)])
        hT, _ = ph.sb(NFC * 1024, BF16)
        hT3 = v3(hT, NFC, 1024)
        hres = [[P.res() for _ in range(2)] for _ in range(NFC)]
        sg = Rot([ph.sb(512, F32) for _ in range(2)])
        pA = Rot([ph.ps(D) for _ in range(3)])
        wgate = cx.dram["ffn_w_gate"][fl].rearrange("(kc p) f -> p kc f", p=128)
        wup = cx.dram["ffn_w_up"][fl].rearrange("(kc p) f -> p kc f", p=128)
        wdown = cx.dram["ffn_w_down"][fl].rearrange("(fc p) d -> p fc d", p=128)
        def load_fg(hh, fg):
            g_t, g_r = wg.next()
            u_t, u_r = wu.next()
            load_cast(cx, stg, v3(g_t, 8, 256), g_r, wgate[:, :, fg * 256:(fg + 1) * 256], 8, 256)
            load_cast(cx, stg, v3(u_t, 8, 256), u_r, wup[:, :, fg * 256:(fg + 1) * 256], 8, 256)
            if hh == 0:
                load_cast(cx, stg, wd3[:, 2 * fg:2 * fg + 2, :], wdr, wdown[:, 2 * fg:2 * fg + 2, :], 2, D)
            return g_t, g_r, u_t, u_r

        seq = [(hh, fg) for hh in range(2) for fg in range(NFC // 2)]
        pending = load_fg(*seq[0])
        for si, (hh, fg) in enumerate(seq):
            g_t, g_r, u_t, u_r = pending
            if si + 1 < len(seq):
                pending = load_fg(*seq[si + 1])
            g3 = v3(g_t, 8, 256)
            u3 = v3(u_t, 8, 256)
            for fc in range(2):
                fcg = fg * 2 + fc
                for tt in range(2):
                    q = hh * 2 + tt
                    ab_, ar = pA.next()
                    br = ar
                    a = ab_[:, 0:512]
                    b = ab_[:, 512:1024]
                    for kc in range(8):
                        P.pe(lambda e, kc=kc, a=a, g3=g3, fc=fc, q=q: e.matmul(
                            a, lhsT=g3[:, kc, fc * 128:(fc + 1) * 128],
                            rhs=xT3[:, kc, q * 512:(q + 1) * 512], start=(kc == 0), stop=(kc == 7)),
                            [g_r, xTres[q]], [ar])
                    for kc in range(8):
                        P.pe(lambda e, kc=kc, b=b, u3=u3, fc=fc, q=q: e.matmul(
                            b, lhsT=u3[:, kc, fc * 128:(fc + 1) * 128],
                            rhs=xT3[:, kc, q * 512:(q + 1) * 512], start=(kc == 0), stop=(kc == 7)),
                            [u_r, xTres[q]], [br])
                    s, sr = sg.next()
                    P.act(lambda e, s=s, a=a: e.activation(out=s[:, :], in_=a, func=AF.Silu), [ar], [sr])
                    P.dve(lambda e, s=s, b=b, fcg=fcg, tt=tt: e.tensor_tensor(
                        out=hT3[:, fcg, tt * 512:(tt + 1) * 512], in0=s[:, :], in1=b, op=ALU.mult),
                        [sr, br], [hres[fcg][tt]])
            if fg != NFC // 2 - 1:
                continue
            for tl in range(8):
                ti = hh * 8 + tl
                xpre = ln.prefetch(xsrc, ti)
                py, pyr = pA.next()
                for half in range(2):
                    for fcg in range(NFC):
                        P.pe(lambda e, fcg=fcg, tl=tl, half=half, py=py: e.matmul(
                            py[:, half * 512:(half + 1) * 512], lhsT=hT3[:, fcg, tl * 128:(tl + 1) * 128],
                            rhs=wd3[:, fcg, half * 512:(half + 1) * 512], start=(fcg == 0), stop=(fcg == NFC - 1)),
                            [hres[fcg][tl // 4], wdr], [pyr])
                last = ln.tile(py[:, :], pyr, ti, xdst, xpre, xTdst)
        ln.flush()
    return last


def phase_odd(cx, layer, xsrc, xTsrc, xdst, xTdst):
    P = cx.P
    i = cx.lmap["odd"][layer]
    last = None
    with Phase(cx, "odd%d" % layer) as ph:
        idt, idr = load_ident(cx, ph)
        pT, pTr = ph.ps(D, BF16)
        ln = LNState(cx, ph, cx.dram["mix_ln_g"][layer:layer + 1, :], cx.dram["mix_ln_b"][layer:layer + 1, :],
                     idt, idr, pT, pTr, nbuf=1)
        stg = None
        wu, wur = ph.sb(8 * 2048, BF16)
        wv, wvr = ph.sb(8 * 2048, BF16)
        wo, wor = ph.sb(16 * D, BF16)
        wst, wsr = ph.sb(8 * 128, BF16)
        wu3, wv3, wo3, ws3 = v3(wu, 8, 2048), v3(wv, 8, 2048), v3(wo, 16, D), v3(wst, 8, 128)
        win = cx.dram["odd_w_in"][i].rearrange("(kc p) f -> p kc f", p=128)
        wurs = [P.res() for _ in range(4)]
        wvrs = [P.res() for _ in range(4)]
        wors = [P.res() for _ in range(2)]
        for n in range(4):
            load_cast(cx, stg, wu3[:, :, n * 512:(n + 1) * 512], wurs[n], win[:, :, n * 512:(n + 1) * 512], 8, 512, scols=1024)
        for n in range(4):
            load_cast(cx, stg, wv3[:, :, n * 512:(n + 1) * 512], wvrs[n], win[:, :, 2048 + n * 512:2048 + (n + 1) * 512],
                      8, 512, scols=1024)
        load_cast(cx, stg, ws3, wsr, cx.dram["odd_wsT"][i].rearrange("g q p -> q g p"), 8, 128, scols=1024)
        wod = cx.dram["odd_w_out"][i].rearrange("(cc p) d -> p cc d", p=128)
        for hf in range(2):
            load_cast(cx, stg, wo3[:, :, hf * 512:(hf + 1) * 512], wors[hf], wod[:, :, hf * 512:(hf + 1) * 512], 16, 512)
        sg, sgr = ph.sb(2048, F32)
        sb_, sbr = ph.sb(2048, F32)
        bt, btr = ph.sb(2048, F32)
        P.dma("sp", sg[:, :], cx.dram["odd_sgu_norm_g"][i:i + 1, :].partition_broadcast(128), writes=[sgr])
        P.dma("sp", sb_[:, :], cx.dram["odd_sgu_norm_b"][i:i + 1, :].partition_broadcast(128), writes=[sbr])
        P.dma("sp", bt[:, :], cx.dram["odd_b16"][i:i + 1, :].partition_broadcast(128), writes=[btr])
        xTt = Rot([ph.sb(8 * 512, BF16) for _ in range(2)])
        uT, _ = ph.sb(16 * 512, BF16)
        uT3 = v3(uT, 16, 512)
        ures = [P.res() for _ in range(16)]
        vbs = Rot([ph.sb(2048, BF16) for _ in range(2)])
        t32, t32r = ph.sb(2048, F32)
        vn, vnr = ph.sb(2048, BF16)
        mT, mTr = ph.sb(2048, BF16)
        mT3 = v3(mT, 16, 128)
        st, str_ = ph.sb(32, F32)
        pA = Rot([ph.ps(1024) for _ in range(3)])
        pS = Rot([ph.ps(512) for _ in range(1)])
        state = {}

        def emit_u(stile):
            xt_, xtr_ = xTt.next()
            x3 = v3(xt_, 8, 512)
            P.dma("sp", x3, cx.dram[xTsrc][:, :, stile * 512:(stile + 1) * 512], writes=[xtr_])
            state["x"] = (x3, xtr_)
            for cc in range(16):
                pb, pbr = pA.next()
                pu = pb[:, 0:512]
                for kc in range(8):
                    P.pe(lambda e, kc=kc, cc=cc, x3=x3, pu=pu: e.matmul(pu, lhsT=wu3[:, kc, cc * 128:(cc + 1) * 128],
                                                                   rhs=x3[:, kc, :], start=(kc == 0), stop=(kc == 7)),
                         [wurs[cc // 4], xtr_], [pbr])
                P.act(lambda e, cc=cc, pu=pu: e.activation(out=uT3[:, cc, :], in_=pu, func=AF.Gelu), [pbr], [ures[cc]])

        def emit_v(ti):
            x3, xtr_ = state["x"]
            tl = ti % 4
            vb, vbr = vbs.next()
            for vh in range(2):
                pv, pvr = pA.next()
                for n in range(2):
                    for kc in range(8):
                        P.pe(lambda e, kc=kc, n=n, vh=vh, pv=pv, x3=x3, tl=tl: e.matmul(
                            pv[:, n * 512:(n + 1) * 512], lhsT=x3[:, kc, tl * 128:(tl + 1) * 128],
                            rhs=wv3[:, kc, (vh * 2 + n) * 512:(vh * 2 + n + 1) * 512],
                            start=(kc == 0), stop=(kc == 7)), [wvrs[vh * 2 + n], xtr_], [pvr])
                P.act(lambda e, pv=pv, vh=vh, vb=vb: e.activation(out=vb[:, vh * 1024:(vh + 1) * 1024], in_=pv[:, :],
                                                                  func=AF.Gelu), [pvr], [vbr])
            state[ti] = (vb, vbr)

        def emit_rest(ti):
            vb, vbr = state.pop(ti)
            tl = ti % 4
            xpre = ln.prefetch(xsrc, ti)
            for c in range(4):
                P.dve(lambda e, c=c: e.bn_stats(out=st[:, c * 6:(c + 1) * 6], in_=vb[:, c * 512:(c + 1) * 512]),
                      [vbr], [str_])
            P.dve(lambda e: e.bn_aggr(out=st[:, 24:26], in_=st[:, 0:24]), [str_], [str_])
            P.dve(lambda e: e.tensor_scalar(out=st[:, 26:27], in0=st[:, 25:26], scalar1=LN_EPS, scalar2=None,
                                            op0=ALU.add), [str_], [str_])
            P.pool(lambda e: e.tensor_tensor(out=st[:, 26:27], in0=st[:, 26:27], in1=ln.mh[:, :], op=ALU.pow),
                   [str_, ln.mhr], [str_])
            P.dve(lambda e: e.scalar_tensor_tensor(out=t32[:, :], in0=vb[:, :], scalar=st[:, 24:25], in1=sg[:, :],
                                                   op0=ALU.subtract, op1=ALU.mult), [vbr, str_, sgr], [t32r])
            P.dve(lambda e: e.scalar_tensor_tensor(out=vn[:, :], in0=t32[:, :], scalar=st[:, 26:27], in1=sb_[:, :],
                                                   op0=ALU.mult, op1=ALU.add), [t32r, str_, sbr], [vnr])
            for sq in range(4):
                ps_, psr = pS.next()
                for c4 in range(4):
                    cc = sq * 4 + c4
                    P.pe(lambda e, cc=cc, c4=c4, ps_=ps_: e.matmul(ps_[:, c4 * 128:(c4 + 1) * 128],
                                                                    lhsT=vn[:, cc * 128:(cc + 1) * 128],
                                                                    rhs=ws3[:, cc // 2, :], start=True, stop=True),
                         [vnr, wsr], [psr])
                P.dve(lambda e, sq=sq, ps_=ps_: e.tensor_tensor(out=t32[:, sq * 512:(sq + 1) * 512], in0=ps_[:, :],
                                                                in1=bt[:, sq * 512:(sq + 1) * 512], op=ALU.add),
                      [psr, btr], [t32r])
            P.pool(lambda e, tl=tl: e.tensor_tensor(out=mT3, in0=v3(t32, 16, 128),
                                                    in1=uT3[:, :, tl * 128:(tl + 1) * 128], op=ALU.mult),
                   [t32r] + ures, [mTr])
            py, pyr = pA.next()
            for half in range(2):
                for cc in range(16):
                    P.pe(lambda e, cc=cc, half=half, py=py: e.matmul(
                        py[:, half * 512:(half + 1) * 512], lhsT=mT3[:, cc, :],
                        rhs=wo3[:, cc, half * 512:(half + 1) * 512], start=(cc == 0), stop=(cc == 15)),
                        [mTr, wors[half]], [pyr])
            return ln.tile(py[:, :], pyr, ti, xdst, xpre, xTdst)

        NTT = T // 128
        emit_u(0)
        emit_v(0)
        for ti in range(NTT):
            if ti + 1 < NTT and (ti + 1) % 4 != 0:
                emit_v(ti + 1)
            last = emit_rest(ti)
            if ti + 1 < NTT and (ti + 1) % 4 == 0:
                emit_u((ti + 1) // 4)
                emit_v(ti + 1)
        ln.flush()
    return last


def rms_chunks(cx, ph, pacc, paccr, pss, pssr, ones, onesr, wT3, wr, col0, nch, x3, xr, tcols, gcol, gr,
               craw, crawr, csq, csqr, rb, rbr, out3, outr, dim):
    P = cx.P
    for c in range(nch):
        pa, par = pacc.next()
        for kc in range(8):
            P.pe(lambda e, kc=kc, c=c, pa=pa: e.matmul(pa[:, :], lhsT=wT3[:, kc, col0 + c * 128:col0 + (c + 1) * 128],
                                                        rhs=x3[:, kc, tcols], start=(kc == 0), stop=(kc == 7)),
                 [wr, xr], [par])
        P.act(lambda e, c=c, pa=pa: e.activation(out=craw[:, c * 512:(c + 1) * 512], in_=pa[:, :], func=AF.Copy),
              [par], [crawr])
        P.act(lambda e, c=c, pa=pa: e.activation(out=csq[:, c * 512:(c + 1) * 512], in_=pa[:, :], func=AF.Square),
              [par], [csqr])
    for c in range(nch):
        P.pe(lambda e, c=c: e.matmul(pss[:, :], lhsT=ones[:, :], rhs=csq[:, c * 512:(c + 1) * 512],
                                     start=(c == 0), stop=(c == nch - 1)), [onesr, csqr], [pssr])
    P.dve(lambda e: e.tensor_scalar(out=rb[:, :], in0=pss[:, :], scalar1=1.0 / dim, scalar2=RMS_EPS,
                                    op0=ALU.mult, op1=ALU.add), [pssr], [rbr])
    P.act(lambda e: e.activation(out=rb[:, :], in_=rb[:, :], func=AF.Sqrt), [rbr], [rbr])
    P.dve(lambda e: e.reciprocal(out=rb[:, :], in_=rb[:, :]), [rbr], [rbr])
    for c in range(nch):
        P.dve(lambda e, c=c: e.scalar_tensor_tensor(out=out3[:, c, :], in0=craw[:, c * 512:(c + 1) * 512],
                                                    scalar=gcol[:, c:c + 1], in1=rb[:, :],
                                                    op0=ALU.mult, op1=ALU.mult), [crawr, gr, rbr], [outr])


def rotary(cx, p0, p0r, p1, p1r, cos, sin, csr, tcols, t1, t1r, t2, t2r, out, outr):
    P = cx.P
    P.dve(lambda e: e.tensor_tensor(out=t1[0:64, :], in0=p0[0:64, :], in1=cos[0:64, tcols], op=ALU.mult),
          [p0r, csr], [t1r])
    P.dve(lambda e: e.tensor_tensor(out=t2[0:64, :], in0=p1[0:64, :], in1=sin[0:64, tcols], op=ALU.mult),
          [p1r, csr], [t2r])
    P.pool(lambda e: e.tensor_tensor(out=out, in0=t1[0:64, :], in1=t2[0:64, :], op=ALU.add), [t1r, t2r], [outr])


def phase_eproj(cx, layer, xTsrc, gb):
    P = cx.P
    i = cx.lmap["even"][layer]
    GB = cx.dram[gb]
    with Phase(cx, "eproj%d" % layer) as ph:
        stg = None
        xTs, xr = ph.sb(8 * T, BF16)
        x3 = v3(xTs, 8, T)
        P.dma("sp", x3, cx.dram[xTsrc][:, :, :], writes=[xr])
        win = cx.dram["even_w_in"][i].rearrange("(kc p) f -> p kc f", p=128)
        wkv, wkvr = ph.sb(8 * 256, BF16)
        wkr, wkrr = ph.sb(8 * 256, BF16)
        wf, wfr = ph.sb(8 * 512, BF16)
        wkv3, wkr3, wf3 = v3(wkv, 8, 256), v3(wkr, 8, 256), v3(wf, 8, 512)
        P.pool(lambda e: e.memset(wkr[:, :], 0.0), [], [wkrr])
        load_cast(cx, stg, wkv3, wkvr, win[:, :, 384:640], 8, 256)
        load_cast(cx, stg, wkr3[:, :, 0:64], wkrr, win[:, :, 640:704], 8, 64)
        load_cast(cx, stg, wkr3[:, :, 128:160], wkrr, win[:, :, 672:704], 8, 32)
        load_cast(cx, stg, wkr3[:, :, 160:192], wkrr, win[:, :, 640:672], 8, 32)
        load_cast(cx, stg, wf3, wfr, win[:, :, 704:1216], 8, 512)
        ones, onesr = ph.sb(128, BF16)
        P.pool(lambda e: e.memset(ones[:, :], 1.0), [], [onesr])
        gcol, gr = ph.sb(2, F32)
        P.dma("sp", gcol[:, :], cx.dram["kvg_t"][i], writes=[gr])
        cos, csr = ph.sb(T, F32)
        sin, _ = ph.sb(T, F32)
        P.dma("sp", cos[0:64, :], cx.dram["cos2"][:, :], writes=[csr])
        P.dma("sp", sin[0:64, :], cx.dram["sin2"][:, :], writes=[csr])
        craw, crawr = ph.sb(2 * 512, F32)
        csq, csqr = ph.sb(2 * 512, BF16)
        rb, rbr = ph.sb(512, F32)
        cn = Rot([ph.sb(2 * 512, BF16) for _ in range(2)])
        t1, t1r = ph.sb(512, F32)
        t2, t2r = ph.sb(512, F32)
        kro = Rot([ph.sb(512, BF16) for _ in range(2)])
        fo = Rot([ph.sb(512, BF16) for _ in range(2)])
        pacc = Rot([ph.ps(512) for _ in range(3)])
        pss, pssr = ph.ps(512)
        pk = [ph.ps(512) for _ in range(2)]
        pf = Rot([ph.ps(512) for _ in range(2)])
        fview = GB[0:512, :].rearrange("r (a c) -> (r a) c", c=512)
        for tt in range(4):
            tcols = slice(tt * 512, (tt + 1) * 512)
            o, orr = cn.next()
            o3 = v3(o, 2, 512)
            rms_chunks(cx, ph, pacc, None, pss, pssr, ones, onesr, wkv3, wkvr, 0, 2, x3, xr, tcols, gcol, gr,
                       craw, crawr, csq, csqr, rb, rbr, o3, orr, 256.0)
            P.dma("pool", GB[512:768, tcols].rearrange("(c p) t -> p c t", p=128), o3, reads=[orr], writes=[])
            for j in range(2):
                for kc in range(8):
                    P.pe(lambda e, kc=kc, j=j, tcols=tcols: e.matmul(pk[j][0][:, :], lhsT=wkr3[:, kc, j * 128:(j + 1) * 128],
                                                                     rhs=x3[:, kc, tcols], start=(kc == 0), stop=(kc == 7)),
                         [wkrr, xr], [pk[j][1]])
            ko, kor = kro.next()
            rotary(cx, pk[0][0], pk[0][1], pk[1][0], pk[1][1], cos, sin, csr, tcols, t1, t1r, t2, t2r, ko[0:64, :], kor)
            P.dma("pool", GB[768:832, tcols], ko[0:64, :], reads=[kor], writes=[])
            for tl in range(4):
                t0 = tt * 512 + tl * 128
                p, pr = pf.next()
                for kc in range(8):
                    P.pe(lambda e, kc=kc, p=p, t0=t0: e.matmul(p[:, :], lhsT=x3[:, kc, t0:t0 + 128], rhs=wf3[:, kc, :],
                                                              start=(kc == 0), stop=(kc == 7)), [wfr, xr], [pr])
                f_, fr = fo.next()
                P.act(lambda e, p=p, f_=f_: e.activation(out=f_[:, :], in_=p[:, :], func=AF.Copy), [pr], [fr])
                P.dma("pool", fview[t0:t0 + 128, :], f_[:, :], reads=[fr], writes=[])


def phase_eq(cx, layer, xTsrc):
    P = cx.P
    i = cx.lmap["even"][layer]
    with Phase(cx, "eq%d" % layer) as ph:
        stg = None
        xTs, xr = ph.sb(8 * T, BF16)
        x3 = v3(xTs, 8, T)
        P.dma("sp", x3, cx.dram[xTsrc][:, :, :], writes=[xr])
        win = cx.dram["even_w_in"][i].rearrange("(kc p) f -> p kc f", p=128)
        wuq_d = cx.dram["even_w_uq"][i].rearrange("(kc p) f -> p kc f", p=128)
        wq1, wq1r = ph.sb(8 * 384, BF16)
        wq13 = v3(wq1, 8, 384)
        load_cast(cx, stg, wq13, wq1r, win[:, :, 0:384], 8, 384)
        wuq, wuqr = ph.sb(3 * 1536, BF16)
        wuq3 = v3(wuq, 3, 1536)
        for n in range(3):
            load_cast(cx, stg, wuq3[:, :, n * 512:(n + 1) * 512], wuqr, wuq_d[:, :, n * 512:(n + 1) * 512], 3, 512)
        wsw, wswr = ph.sb(3 * 512, BF16)
        wsw3 = v3(wsw, 3, 512)
        for h in range(8):
            b = h * 192 + 128
            load_cast(cx, stg, wsw3[:, :, h * 64:h * 64 + 32], wswr, wuq_d[:, :, b + 32:b + 64], 3, 32)
            load_cast(cx, stg, wsw3[:, :, h * 64 + 32:h * 64 + 64], wswr, wuq_d[:, :, b:b + 32], 3, 32)
        ones, onesr = ph.sb(128, BF16)
        P.pool(lambda e: e.memset(ones[:, :], 1.0), [], [onesr])
        gcol, gr = ph.sb(3, F32)
        P.dma("sp", gcol[:, :], cx.dram["qg_t"][i], writes=[gr])
        cos, csr = ph.sb(T, F32)
        sin, _ = ph.sb(T, F32)
        P.dma("sp", cos[0:64, :], cx.dram["cos2"][:, :], writes=[csr])
        P.dma("sp", sin[0:64, :], cx.dram["sin2"][:, :], writes=[csr])
        craw, crawr = ph.sb(3 * 512, F32)
        csq, csqr = ph.sb(3 * 512, BF16)
        rb, rbr = ph.sb(512, F32)
        cqn, cqnr = ph.sb(3 * 512, BF16)
        cqn3 = v3(cqn, 3, 512)
        t1, t1r = ph.sb(512, F32)
        t2, t2r = ph.sb(512, F32)
        qno = Rot([ph.sb(512, BF16) for _ in range(2)])
        qro = Rot([ph.sb(512, BF16) for _ in range(2)])
        pacc = Rot([ph.ps(512) for _ in range(2)])
        pss, pssr = ph.ps(512)
        pn = Rot([ph.ps(512) for _ in range(2)])
        pk = [ph.ps(512) for _ in range(2)]
        for tt in range(4):
            tcols = slice(tt * 512, (tt + 1) * 512)
            rms_chunks(cx, ph, pacc, None, pss, pssr, ones, onesr, wq13, wq1r, 0, 3, x3, xr, tcols, gcol, gr,
                       craw, crawr, csq, csqr, rb, rbr, cqn3, cqnr, 384.0)
            for h in range(8):
                p, pr = pn.next()
                for kc in range(3):
                    P.pe(lambda e, kc=kc, p=p, h=h: e.matmul(p[:, :], lhsT=wuq3[:, kc, h * 192:h * 192 + 128],
                                                            rhs=cqn3[:, kc, :], start=(kc == 0), stop=(kc == 2)),
                         [wuqr, cqnr], [pr])
                qn, qnr = qno.next()
                P.act(lambda e, p=p, qn=qn: e.activation(out=qn[:, :], in_=p[:, :], func=AF.Copy), [pr], [qnr])
                P.dma("pool", cx.dram["QSn"][h, :, tcols], qn[:, :], reads=[qnr], writes=[])
                for kc in range(3):
                    P.pe(lambda e, kc=kc, h=h: e.matmul(pk[0][0][0:64, :], lhsT=wuq3[:, kc, h * 192 + 128:h * 192 + 192],
                                                        rhs=cqn3[:, kc, :], start=(kc == 0), stop=(kc == 2)),
                         [wuqr, cqnr], [pk[0][1]])
                for kc in range(3):
                    P.pe(lambda e, kc=kc, h=h: e.matmul(pk[1][0][0:64, :], lhsT=wsw3[:, kc, h * 64:(h + 1) * 64],
                                                        rhs=cqn3[:, kc, :], start=(kc == 0), stop=(kc == 2)),
                         [wswr, cqnr], [pk[1][1]])
                qr_, qrr = qro.next()
                rotary(cx, pk[0][0], pk[0][1], pk[1][0], pk[1][1], cos, sin, csr, tcols, t1, t1r, t2, t2r,
                       qr_[0:64, :], qrr)
                P.dma("pool", cx.dram["QSr"][h, :, tcols], qr_[0:64, :], reads=[qrr], writes=[])


def phase_ekv(cx, layer, ga):
    P = cx.P
    i = cx.lmap["even"][layer]
    GA = cx.dram[ga]
    KSv = cx.dram["KS"].ap().rearrange("h p t -> p h t")
    VSv = cx.dram["VS"].ap().rearrange("h p k d -> p h k d")
    with Phase(cx, "ekv%d" % layer) as ph:
        stg = None
        wuk, wukr = ph.sb(2 * 1024, BF16)
        wuv, wuvr = ph.sb(2 * 1024, BF16)
        wuk3, wuv3 = v3(wuk, 2, 1024), v3(wuv, 2, 1024)
        load_cast(cx, stg, wuk3, wukr, cx.dram["even_w_uk"][i].rearrange("(c p) f -> p c f", p=128), 2, 1024)
        load_cast(cx, stg, wuv3, wuvr, cx.dram["even_w_uv"][i].rearrange("(c p) f -> p c f", p=128), 2, 1024)
        lat = Rot([ph.sb(2 * 512, BF16) for _ in range(3)])
        kt = Rot([ph.sb(8 * 512, BF16) for _ in range(2)])
        vt = Rot([ph.sb(4096, BF16) for _ in range(2)])
        pk = Rot([ph.ps(512) for _ in range(4)])
        pv = Rot([ph.ps(1024) for _ in range(2)])
        ev = 0
        for b in range(SEQ // 512):
            r, bb = b // 4, b % 4
            l, lr = lat.next()
            l3 = v3(l, 2, 512)
            P.dma("sp", l3, GA[r, 512:768, bb * 512:(bb + 1) * 512].rearrange("(c p) t -> p c t", p=128), writes=[lr])
            k_, kr_ = kt.next()
            k3 = v3(k_, 8, 512)
            for h in range(8):
                p, pr = pk.next()
                for c in range(2):
                    P.pe(lambda e, c=c, h=h, p=p, l3=l3: e.matmul(p[:, :], lhsT=wuk3[:, c, h * 128:(h + 1) * 128],
                                                                 rhs=l3[:, c, :], start=(c == 0), stop=(c == 1)),
                         [wukr, lr], [pr])
                if ev % 2 == 0:
                    P.act(lambda e, p=p, h=h, k3=k3: e.activation(out=k3[:, h, :], in_=p[:, :], func=AF.Copy), [pr], [kr_])
                else:
                    P.dve(lambda e, p=p, h=h, k3=k3: e.tensor_copy(out=k3[:, h, :], in_=p[:, :]), [pr], [kr_])
                ev += 1
            P.dma("pool", KSv[:, :, b * 512:(b + 1) * 512], k3, reads=[kr_], writes=[])
            v_, vr_ = vt.next()
            v4 = v_[:, :].rearrange("p (h t d) -> p h t d", t=4, h=8, d=128)
            for t4 in range(4):
                p, pr = pv.next()
                for n in range(2):
                    for c in range(2):
                        P.pe(lambda e, c=c, n=n, p=p, l3=l3, t4=t4: e.matmul(
                            p[:, n * 512:(n + 1) * 512], lhsT=l3[:, c, t4 * 128:(t4 + 1) * 128],
                            rhs=wuv3[:, c, n * 512:(n + 1) * 512], start=(c == 0), stop=(c == 1)), [wuvr, lr], [pr])
                dst = v4[:, :, t4, :]
                src = p[:, :].rearrange("p (h d) -> p h d", h=8)
                if ev % 2 == 0:
                    P.act(lambda e, src=src, dst=dst: e.activation(out=dst, in_=src, func=AF.Copy), [pr], [vr_])
                else:
                    P.dve(lambda e, src=src, dst=dst: e.tensor_copy(out=dst, in_=src), [pr], [vr_])
                ev += 1
            P.dma("pool", VSv[:, :, b * 4:(b + 1) * 4, :].rearrange("p h t d -> p h (t d)"),
                  v_[:, :].rearrange("p (h x) -> p h x", h=8), reads=[vr_], writes=[])


def phase_efft(cx, layer, ga, mixT):
    P = cx.P
    GA = cx.dram[ga]
    with Phase(cx, "efft%d" % layer) as ph:
        cs1, cs1r = ph.sb(256, BF16)
        t3, t3r = ph.sb(128 * 64, BF16)
        cd, cdr = ph.sb(256, BF16)
        P.dma("sp", cs1[:, :], cx.dram["cs1"][:, :], writes=[cs1r])
        P.dma("sp", t3[:, :], cx.dram["t3"][:, :], writes=[t3r])
        P.dma("sp", cd[:, :], cx.dram["cdft"][:, :], writes=[cdr])
        t33 = v3(t3, 128, 64)
        Fg = Rot([ph.sb(128 * 128, BF16) for _ in range(2)])
        Ag, Agr = ph.sb(128 * 256, BF16)
        A5 = Ag[:, :].rearrange("p (r k c) -> p r k c", r=2, k=128, c=128)
        XT, XTr = ph.sb(2 * T, BF16)
        XT4 = XT[:, :].rearrange("p (r b k) -> p r b k", r=2, b=16, k=128)
        XT3 = v3(XT, 2, T)
        yT = Rot([ph.sb(512, BF16) for _ in range(2)])
        pA = Rot([ph.ps(512) for _ in range(3)])
        pX = Rot([ph.ps(512) for _ in range(2)])
        pY = Rot([ph.ps(512) for _ in range(2)])
        ev = 0
        for g in range(4):
            F, Fr = Fg.next()
            F3 = v3(F, 128, 128)
            for r in range(NCORES):
                src = GA[r, 0:512, :].rearrange("r (a c) -> (r a) c", c=512)[:, g * 128:(g + 1) * 128]
                P.dma("sp", F3[r * 16:(r + 1) * 16, :, :], src.rearrange("(t s) c -> t s c", s=128), writes=[Fr])
            for c2 in range(64):
                p, pr = pA.next()
                for j in range(2):
                    c = c2 * 2 + j
                    P.pe(lambda e, c=c, j=j, p=p, F3=F3: e.matmul(p[:, j * 256:(j + 1) * 256], lhsT=F3[:, :, c],
                                                                 rhs=cs1[:, :], start=True, stop=True),
                         [Fr, cs1r], [pr])
                dst = A5[:, :, :, c2 * 2:c2 * 2 + 2]
                src = p[:, :].rearrange("p (j r k) -> p r k j", j=2, r=2, k=128)
                if ev % 2 == 0:
                    P.act(lambda e, src=src, dst=dst: e.activation(out=dst, in_=src, func=AF.Copy), [pr], [Agr])
                else:
                    P.dve(lambda e, src=src, dst=dst: e.tensor_copy(out=dst, in_=src), [pr], [Agr])
                ev += 1
            for kb in range(8):
                p, pr = pX.next()
                for kl in range(16):
                    k1 = kb * 16 + kl
                    P.pe(lambda e, k1=k1, kl=kl, p=p: e.matmul(p[:, kl * 32:(kl + 1) * 32], lhsT=A5[:, 0, k1, :],
                                                               rhs=t33[:, k1, 0:32], start=True, stop=False),
                         [Agr, t3r], [pr])
                    P.pe(lambda e, k1=k1, kl=kl, p=p: e.matmul(p[:, kl * 32:(kl + 1) * 32], lhsT=A5[:, 1, k1, :],
                                                               rhs=t33[:, k1, 32:64], start=False, stop=True),
                         [Agr, t3r], [pr])
                p3 = p[:, :].rearrange("p (k x) -> p k x", k=16, x=32)
                for ri in range(2):
                    src = p3[:, :, ri * 16:(ri + 1) * 16].rearrange("p k b -> p b k")
                    dst = XT4[:, ri, :, kb * 16:(kb + 1) * 16]
                    if ri == 0:
                        P.act(lambda e, src=src, dst=dst: e.activation(out=dst, in_=src, func=AF.Copy), [pr], [XTr])
                    else:
                        P.dve(lambda e, src=src, dst=dst: e.tensor_copy(out=dst, in_=src), [pr], [XTr])
            for tt in range(4):
                p, pr = pY.next()
                P.pe(lambda e, p=p, tt=tt: e.matmul(p[:, :], lhsT=cd[:, 0:128], rhs=XT3[:, 0, tt * 512:(tt + 1) * 512],
                                                    start=True, stop=False), [cdr, XTr], [pr])
                P.pe(lambda e, p=p, tt=tt: e.matmul(p[:, :], lhsT=cd[:, 128:256], rhs=XT3[:, 1, tt * 512:(tt + 1) * 512],
                                                    start=False, stop=True), [cdr, XTr], [pr])
                y, yr = yT.next()
                P.act(lambda e, p=p, y=y: e.activation(out=y[:, :], in_=p[:, :], func=AF.Copy), [pr], [yr])
                P.dma("pool", cx.dram[mixT][8 + g, :, tt * 512:(tt + 1) * 512], y[:, :], reads=[yr], writes=[])


def phase_eattn(cx, layer, ga, mixT):
    P = cx.P
    GA = cx.dram[ga]
    NCH = 8
    with Phase(cx, "eattn%d" % layer) as ph:
        ones, onesr = ph.sb(128, BF16)
        P.pool(lambda e: e.memset(ones[:, :], 1.0), [], [onesr])
        KR, _ = ph.sb(SEQ, BF16)
        krres = [P.res() for _ in range(NCH)]
        zr = P.res()
        P.pool(lambda e: e.memset(KR[64:128, :], 0.0), [], [zr])
        for r in range(NCH):
            P.dma("sp", KR[0:64, r * 2048:(r + 1) * 2048], GA[r, 768:832, :], writes=[krres[r]])
        KN, _ = ph.sb(SEQ, BF16)
        knres = [P.res() for _ in range(NCH)]
        VV, _ = ph.sb(SEQ, BF16)
        VV3 = v3(VV, 128, 128)
        vres = [P.res() for _ in range(NCH)]
        QN = Rot([ph.sb(T, BF16) for _ in range(2)])
        QR = Rot([ph.sb(T, BF16) for _ in range(2)])
        for (q_, qr_) in QR.items:
            P.pool(lambda e, q_=q_: e.memset(q_[64:128, :], 0.0), [], [qr_])
        PT = Rot([ph.sb(1024, BF16) for _ in range(4)])
        ACC = Rot([ph.sb(1024, F32) for _ in range(2)])
        ACB = Rot([ph.sb(1024, BF16) for _ in range(2)])
        rs = Rot([ph.sb(512, F32) for _ in range(2)])
        ob = Rot([ph.sb(512, BF16) for _ in range(2)])
        pS = Rot([ph.ps(1024) for _ in range(2)])
        pO = Rot([ph.ps(512) for _ in range(2)])
        pR = Rot([ph.ps(512) for _ in range(2)])
        KSd, VSd = cx.dram["KS"], cx.dram["VS"]
        for h in range(8):
            for c in range(NCH):
                P.dma("sp", KN[:, c * 2048:(c + 1) * 2048], KSd[h, :, c * 2048:(c + 1) * 2048], writes=[knres[c]])
                P.dma("sp", VV3[:, c * 16:(c + 1) * 16, :], VSd[h, :, c * 16:(c + 1) * 16, :], writes=[vres[c]])
            qn, qnr = QN.next()
            qr, qrr = QR.next()
            P.dma("sp", qn[:, :], cx.dram["QSn"][h, :, :], writes=[qnr])
            P.dma("sp", qr[0:64, :], cx.dram["QSr"][h, :, :], writes=[qrr])
            for qt in range(4):
                qc = slice(qt * 512, (qt + 1) * 512)
                po, por = pO.next()
                pr_, prr = pR.next()
                acc, accr = ACC.next()

                def qk(pi, qn=qn, qr=qr, qnr=qnr, qrr=qrr, qc=qc):
                    s, sr = pS.next()
                    for j in range(2):
                        kt = pi * 2 + j
                        c = kt // 16
                        kc = slice(kt * 128, (kt + 1) * 128)
                        P.pe(lambda e, s=s, j=j, kc=kc: e.matmul(s[:, j * 512:(j + 1) * 512], lhsT=KN[:, kc], rhs=qn[:, qc],
                                                                 start=True, stop=False), [knres[c], qnr], [sr])
                        P.pe(lambda e, s=s, j=j, kc=kc: e.matmul(s[:, j * 512:(j + 1) * 512], lhsT=KR[:, kc], rhs=qr[:, qc],
                                                                 start=False, stop=True), [krres[c], zr, qrr], [sr])
                    return s, sr

                def pv(pi, pt, ptr, po=po, por=por, acc=acc, accr=accr):
                    for j in range(2):
                        kt = pi * 2 + j
                        c = kt // 16
                        P.pe(lambda e, j=j, kt=kt, pt=pt: e.matmul(po[:, :], lhsT=VV3[:, kt, :], rhs=pt[:, j * 512:(j + 1) * 512],
                                                                   start=(kt == 0), stop=(kt == 127)), [vres[c], ptr], [por])
                    if pi == 0:
                        P.dve(lambda e, pt=pt: e.tensor_copy(out=acc[:, :], in_=pt[:, :]), [ptr], [accr])
                    else:
                        P.dve(lambda e, pt=pt: e.tensor_tensor(out=acc[:, :], in0=acc[:, :], in1=pt[:, :], op=ALU.add),
                              [ptr, accr], [accr])

                pend = None
                for pi in range(64):
                    s, sr = qk(pi)
                    pt, ptr = PT.next()
                    P.act(lambda e, s=s, pt=pt: e.activation(out=pt[:, :], in_=s[:, :], func=AF.Exp, scale=SCALE),
                          [sr], [ptr])
                    if pend is not None:
                        pv(*pend)
                    pend = (pi, pt, ptr)
                pv(*pend)
                ab, abr = ACB.next()
                P.act(lambda e, ab=ab, acc=acc: e.activation(out=ab[:, :], in_=acc[:, :], func=AF.Copy), [accr], [abr])
                for j in range(2):
                    P.pe(lambda e, j=j, ab=ab, pr_=pr_: e.matmul(pr_[:, :], lhsT=ones[:, :], rhs=ab[:, j * 512:(j + 1) * 512],
                                                                 start=(j == 0), stop=(j == 1)), [onesr, abr], [prr])
                r_, rr = rs.next()
                o_, orr = ob.next()
                P.dve(lambda e, r_=r_, pr_=pr_: e.reciprocal(out=r_[:, :], in_=pr_[:, :]), [prr], [rr])
                P.dve(lambda e, r_=r_, o_=o_, po=po: e.tensor_tensor(out=o_[:, :], in0=po[:, :], in1=r_[:, :], op=ALU.mult),
                      [por, rr], [orr])
                P.dma("pool", cx.dram[mixT][h, :, qc], o_[:, :], reads=[orr], writes=[])


def phase_eout(cx, layer, mixT, xsrc, xdst, xTdst):
    P = cx.P
    i = cx.lmap["even"][layer]
    last = None
    with Phase(cx, "eout%d" % layer) as ph:
        idt, idr = load_ident(cx, ph)
        pT, pTr = ph.ps(D, BF16)
        ln = LNState(cx, ph, cx.dram["mix_ln_g"][layer:layer + 1, :], cx.dram["mix_ln_b"][layer:layer + 1, :],
                     idt, idr, pT, pTr)
        stg = None
        wo, wor = ph.sb(12 * D, BF16)
        wo3 = v3(wo, 12, D)
        load_cast(cx, stg, wo3, wor, cx.dram["even_w_out"][i].rearrange("(c p) d -> p c d", p=128), 12, D)
        mx, mxr = ph.sb(12 * T, BF16)
        mx3 = v3(mx, 12, T)
        P.dma("sp", mx3, cx.dram[mixT].ap().rearrange("c p t -> p c t"), writes=[mxr])
        pY = Rot([ph.ps(D) for _ in range(2)])
        for ti in range(T // 128):
            xpre = ln.prefetch(xsrc, ti)
            py, pyr = pY.next()
            for half in range(2):
                for c in range(12):
                    P.pe(lambda e, c=c, half=half, py=py, ti=ti: e.matmul(
                        py[:, half * 512:(half + 1) * 512], lhsT=mx3[:, c, ti * 128:(ti + 1) * 128],
                        rhs=wo3[:, c, half * 512:(half + 1) * 512], start=(c == 0), stop=(c == 11)), [mxr, wor], [pyr])
            last = ln.tile(py[:, :], pyr, ti, xdst, xpre, xTdst)
        ln.flush()
    return last


W_EVEN = [("even_w_in", [D, 1216]), ("even_w_uq", [384, 1536]), ("even_w_uk", [256, 1024]),
          ("even_w_uv", [256, 1024]), ("even_w_out", [1536, D]), ("kvg_t", [128, 2]), ("qg_t", [128, 3])]
W_ODD = [("odd_w_in", [D, 4096]), ("odd_sgu_norm_g", [2048]), ("odd_sgu_norm_b", [2048]),
         ("odd_wsT", [8, 128, 128]), ("odd_b16", [2048]), ("odd_w_out", [2048, D])]
W_FFN = [("ffn_w_gate", [D, DFF]), ("ffn_w_up", [D, DFF]), ("ffn_w_down", [DFF, D])]
W_LN = [("mix_ln_g", [4, D]), ("mix_ln_b", [4, D]), ("ffn_ln_g", [4, D]), ("ffn_ln_b", [4, D])]
CONSTS = [("ident", [128, 128], BF16), ("cos2", [64, T], F32), ("sin2", [64, T], F32), ("cs1", [128, 256], BF16),
          ("t3", [128, 128 * 64], BF16), ("cdft", [128, 256], BF16)]

PROG_LAYERS = {"F1": ([0], [], []), "M": ([0], [], []), "A": ([0], [], []), "B": ([0, 2], [1], [0, 1]), "C": ([2], [3], [2, 3]),
               "F": ([0, 2], [1, 3], [0, 1, 2, 3])}


def build_program(kind):
    nc = bass.Bass("TRN2", target_bir_lowering=False)
    cx = Ctx(nc)
    if kind.startswith("P:"):
        PROG_LAYERS[kind] = ([0], [1], [0])
    ev, od, ff = PROG_LAYERS[kind]
    cx.lmap = {"even": {l: k for k, l in enumerate(ev)}, "odd": {l: k for k, l in enumerate(od)},
               "ffn": {l: k for k, l in enumerate(ff)}}
    gath = {}
    if kind in ("F", "F1"):
        for l in (0, 2):
            gbh = nc.dram_tensor("gb%d" % l, [GROWS, 1024], F32)
            gah = nc.dram_tensor("ga%d" % l, [NCORES * GROWS, 1024], F32)
            cx.dram["gb%d" % l] = gbh.ap().bitcast(BF16)
            cx.dram["ga%d" % l] = gah.ap().bitcast(BF16).rearrange("(r g) c -> r g c", r=NCORES)
            gath[l] = (gbh, gah)
    cx.dt("x", [T, D], F32, "ExternalInput")
    for n, shp, dt_ in CONSTS:
        cx.dt(n, shp, dt_, "ExternalInput")
    for n, shp in W_LN:
        cx.dt(n, shp, F32, "ExternalInput")
    for grp, lays in ((W_EVEN, ev), (W_ODD, od), (W_FFN, ff)):
        if lays:
            for n, shp in grp:
                cx.dt(n, [len(lays)] + shp, F32, "ExternalInput")
    for n in ("xs0", "xs1"):
        cx.dt(n, [T, D], F32)
    for n in ("xT0", "xT1"):
        cx.dt(n, [128, 8, T], BF16)
    dk = "ExternalOutput" if kind == "M" else "Internal"
    cx.dt("QSn", [8, 128, T], BF16, dk)
    cx.dt("QSr", [8, 64, T], BF16, dk)
    cx.dt("KS", [8, 128, SEQ], BF16)
    cx.dt("VS", [8, 128, 128, 128], BF16)
    cx.dt("mixT", [12, 128, T], BF16, dk)
    P = cx.P

    def main_even(layer, ga, xsrc, xTsrc, xdst, xTdst):
        phase_eq(cx, layer, xTsrc)
        phase_ekv(cx, layer, ga)
        phase_efft(cx, layer, ga, "mixT")
        phase_eattn(cx, layer, ga, "mixT")
        phase_eout(cx, layer, "mixT", xsrc, xdst, xTdst)

    if kind.startswith("P:"):
        cx.dt("ga", [NCORES, GROWS, 2048], BF16, "ExternalInput")
        cx.dt("gb", [GROWS, 2048], BF16, "ExternalOutput")
        for p in kind[2:].split(","):
            if p == "prep":
                phase_prep(cx, "x", "xT0")
            elif p == "eproj":
                phase_eproj(cx, 0, "xT0", "gb")
            elif p == "eq":
                phase_eq(cx, 0, "xT0")
            elif p == "ekv":
                phase_ekv(cx, 0, "ga")
            elif p == "efft":
                phase_efft(cx, 0, "ga", "mixT")
            elif p == "eattn":
                phase_eattn(cx, 0, "ga", "mixT")
            elif p == "eout":
                phase_eout(cx, 0, "mixT", "x", "xs1", "xT1")
            elif p == "ffn":
                phase_ffn(cx, 0, "x", "xT0", "xs0", "xT1")
            elif p == "odd":
                phase_odd(cx, 1, "x", "xT0", "xs1", "xT1")
    elif kind == "A":
        cx.dt("gb", [GROWS, 2048], BF16, "ExternalOutput")
        phase_prep(cx, "x", "xT0")
        phase_eproj(cx, 0, "xT0", "gb")
    elif kind == "B":
        cx.dt("ga", [NCORES, GROWS, 2048], BF16, "ExternalInput")
        cx.dt("gb", [GROWS, 2048], BF16, "ExternalOutput")
        cx.dt("xmid", [T, D], F32, "ExternalOutput")
        phase_prep(cx, "x", "xT0")
        main_even(0, "ga", "x", "xT0", "xs1", "xT1")
        phase_ffn(cx, 0, "xs1", "xT1", "xs0", "xT0")
        phase_odd(cx, 1, "xs0", "xT0", "xs1", "xT1")
        phase_ffn(cx, 1, "xs1", "xT1", "xmid", "xT0")
        phase_eproj(cx, 2, "xT0", "gb")
    elif kind == "M":
        cx.dt("ga", [NCORES, GROWS, 2048], BF16, "ExternalInput")
        cx.dt("xmid", [T, D], F32, "ExternalOutput")
        phase_prep(cx, "x", "xT0")
        main_even(0, "ga", "x", "xT0", "xmid", None)
    elif kind == "C":
        cx.dt("ga", [NCORES, GROWS, 2048], BF16, "ExternalInput")
        cx.dt("out", [T, D], F32, "ExternalOutput")
        phase_prep(cx, "x", "xT0")
        main_even(2, "ga", "x", "xT0", "xs1", "xT1")
        phase_ffn(cx, 2, "xs1", "xT1", "xs0", "xT0")
        phase_odd(cx, 3, "xs0", "xT0", "xs1", "xT1")
        phase_ffn(cx, 3, "xs1", "xT1", "out", None)
    elif kind in ("F", "F1"):
        cx.dt("out", [T, D], F32, "ExternalOutput")
        def all_gather(l):
            gbh, gah = gath[l]
            if DUMMY_CC:
                dgi = nc.dram_tensor("dgi%d" % l, [16, 128], F32)
                dgo = nc.dram_tensor("dgo%d" % l, [NCORES * 16, 128], F32)
                P.add("pool", lambda e: e.collective_compute("AllGather", ALU.bypass,
                                                             replica_groups=[list(range(NCORES))],
                                                             ins=[dgi.ap().opt()], outs=[dgo.ap().opt()]),
                      dma=True, inc=1)
                P.barrier()
            if NO_CC:
                for r in range(NCORES):
                    P.dma("pool", gah[r * GROWS:(r + 1) * GROWS, :], gbh[:, :])
                P.barrier()
                return
            P.add("pool", lambda e: e.collective_compute("AllGather", ALU.bypass,
                                                         replica_groups=[list(range(NCORES))],
                                                         ins=[gbh.ap().opt()], outs=[gah.ap().opt()]),
                  dma=True, inc=1)
            P.barrier()

        phase_prep(cx, "x", "xT0")
        phase_eproj(cx, 0, "xT0", "gb0")
        all_gather(0)
        if kind == "F1":
            main_even(0, "ga0", "x", "xT0", "out", None)
            P.emit()
            return nc, cx
        main_even(0, "ga0", "x", "xT0", "xs1", "xT1")
        phase_ffn(cx, 0, "xs1", "xT1", "xs0", "xT0")
        phase_odd(cx, 1, "xs0", "xT0", "xs1", "xT1")
        phase_ffn(cx, 1, "xs1", "xT1", "xs0", "xT0")
        phase_eproj(cx, 2, "xT0", "gb2")
        all_gather(2)
        main_even(2, "ga2", "xs0", "xT0", "xs1", "xT1")
        phase_ffn(cx, 2, "xs1", "xT1", "xs0", "xT0")
        phase_odd(cx, 3, "xs0", "xT0", "xs1", "xT1")
        phase_ffn(cx, 3, "xs1", "xT1", "out", None)
    else:
        raise ValueError(kind)
    P.emit()
    return nc, cx


def _bf(a):
    return np.asarray(a, dtype=np.float32).astype(ml_dtypes.bfloat16)


def _const_tables(core):
    inv = 1.0 / (10000.0 ** (np.arange(0, 64, 2, dtype=np.float64) / 64.0))
    pos = np.arange(core * T, (core + 1) * T, dtype=np.float64)
    ang = (pos[:, None].astype(np.float32) * inv[None, :].astype(np.float32)).astype(np.float32)
    c, s_ = np.cos(ang.astype(np.float64)), np.sin(ang.astype(np.float64))
    cos2 = np.concatenate([c.T, c.T], 0).astype(np.float32)
    sin2 = np.concatenate([-s_.T, s_.T], 0).astype(np.float32)
    n = np.arange(128, dtype=np.float64)
    th = 2 * np.pi * np.outer(n, n) / 128.0
    cs1 = np.concatenate([np.cos(th), -np.sin(th)], 1)
    norm = 1.0 / np.sqrt(SEQ * 128.0)
    cdft = np.concatenate([np.cos(th) * norm, np.sin(th) * norm], 1)
    s2 = np.arange(128, dtype=np.float64)[:, None, None]
    k1 = np.arange(128, dtype=np.float64)[None, :, None]
    k2 = (16 * core + np.arange(16, dtype=np.float64))[None, None, :]
    k = k1 + 128.0 * k2
    ph = 2 * np.pi * ((s2 * k) % SEQ) / SEQ
    mre, mim = np.cos(ph), -np.sin(ph)
    t3 = np.concatenate([mre, mim, -mim, mre], 2).reshape(128, 128 * 64)
    return {"ident": _bf(np.eye(128)), "cos2": np.ascontiguousarray(cos2), "sin2": np.ascontiguousarray(sin2),
            "cs1": _bf(cs1), "t3": _bf(t3), "cdft": _bf(cdft)}


def _weights(inp, kind):
    ev, od, ff = PROG_LAYERS[kind]
    w = {}
    for n in ("mix_ln_g", "mix_ln_b", "ffn_ln_g", "ffn_ln_b"):
        w[n] = np.ascontiguousarray(inp[n], dtype=np.float32)
    if ev:
        ei = [l // 2 for l in ev]
        for n in ("even_w_in", "even_w_uq", "even_w_uk", "even_w_uv", "even_w_out"):
            w[n] = np.ascontiguousarray(np.asarray(inp[n], dtype=np.float32)[ei])
        w["kvg_t"] = np.ascontiguousarray(np.asarray(inp["even_kv_norm"], dtype=np.float32)[ei].reshape(len(ei), 2, 128).transpose(0, 2, 1))
        w["qg_t"] = np.ascontiguousarray(np.asarray(inp["even_q_norm"], dtype=np.float32)[ei].reshape(len(ei), 3, 128).transpose(0, 2, 1))
    if od:
        oi = [l // 2 for l in od]
        for n in ("odd_w_in", "odd_sgu_norm_g", "odd_sgu_norm_b", "odd_w_out"):
            w[n] = np.ascontiguousarray(np.asarray(inp[n], dtype=np.float32)[oi])
        w["odd_wsT"] = np.ascontiguousarray(np.asarray(inp["odd_w_spatial"], dtype=np.float32)[oi].transpose(0, 1, 3, 2))
        w["odd_b16"] = np.ascontiguousarray(np.repeat(np.asarray(inp["odd_b_spatial"], dtype=np.float32)[oi], 2, axis=1).reshape(len(oi), 2048))
    if ff:
        for n in ("ffn_w_gate", "ffn_w_up", "ffn_w_down"):
            w[n] = np.ascontiguousarray(np.asarray(inp[n], dtype=np.float32)[ff])
    return w


_PROGS = {}


def _prog(kind):
    if kind not in _PROGS:
        _PROGS[kind] = build_program(kind)[0]
    return _PROGS[kind]


def _launch(kind, inp, xs, ga=None):
    w = _weights(inp, kind)
    maps = []
    for c in range(NCORES):
        m = dict(w)
        m.update(_const_tables(c))
        m["x"] = np.ascontiguousarray(xs[c], dtype=np.float32)
        if ga is not None:
            m["ga"] = ga
        maps.append(m)
    res = run_bass_kernel_spmd(_prog(kind), maps, core_ids=list(range(NCORES)))
    return res.results


FUSED = False


def kernel(**inp):
    x = np.asarray(inp["x"], dtype=np.float32).reshape(SEQ, D)
    xs = [x[c * T:(c + 1) * T] for c in range(NCORES)]
    if FUSED:
        r = _launch("F", inp, xs)
        out = np.concatenate([r[c]["out"] for c in range(NCORES)], 0)
        return out.reshape(1, SEQ, D).astype(np.float32)
    ra = _launch("A", inp, xs)
    ga = np.ascontiguousarray(np.stack([ra[c]["gb"] for c in range(NCORES)], 0))
    rb = _launch("B", inp, xs, ga)
    ga = np.ascontiguousarray(np.stack([rb[c]["gb"] for c in range(NCORES)], 0))
    xs = [rb[c]["xmid"] for c in range(NCORES)]
    rc = _launch("C", inp, xs, ga)
    out = np.concatenate([rc[c]["out"] for c in range(NCORES)], 0)
    return out.reshape(1, SEQ, D).astype(np.float32)
```

```python
import contextlib
import numpy as np
import ml_dtypes
import concourse.bass as bass
import concourse.mybir as mybir
from concourse.bass_utils import run_bass_kernel_spmd

F32 = mybir.dt.float32
BF16 = mybir.dt.bfloat16
AF = mybir.ActivationFunctionType
ALU = mybir.AluOpType

NCORES = 8
SEQ = 16384
T = SEQ // NCORES
D = 1024
DFF = 2816
NFC = DFF // 128
DEPTH = 4
ALPHA = float((2 * DEPTH) ** 0.25)
LN_EPS = 1e-5
RMS_EPS = 1e-6
GROWS = 832
SCALE = float(192 ** -0.5)
NO_CC = False
DUMMY_CC = False
SCRATCH_AS_OUTPUT = True

ENGS = ["pe", "act", "dve", "pool", "sp"]
SEM_EPOCH = 20000
NDMA_SEMS = {"sp": 4, "pool": 2, "act": 1, "pe": 1, "dve": 1}


class Res:
    __slots__ = ("name", "w", "r")

    def __init__(self, name=""):
        self.name = name
        self.w = None
        self.r = []


class Op:
    __slots__ = ("eng", "fn", "deps", "sig", "sem", "val", "dma", "inc")


class Prog:
    def __init__(self, nc):
        self.nc = nc
        self.ops = {e: [] for e in ENGS}
        self.n_dma = {e: 0 for e in ENGS}
        self.dma_last = {}
        self.all_res = []

    def res(self, name=""):
        r = Res(name)
        self.all_res.append(r)
        return r

    def add(self, eng, fn, reads=(), writes=(), dma=False, inc=None, extra=()):
        op = Op()
        op.inc = inc if inc is not None else (16 if dma else 1)
        op.eng = eng
        op.fn = fn
        op.dma = dma
        op.sig = dma
        op.sem = None
        op.val = 0
        deps = set(extra)
        for r in reads:
            if r.w is not None:
                deps.add(r.w)
        for w in writes:
            if w.w is not None:
                deps.add(w.w)
            deps.update(w.r)
        for r in reads:
            r.r.append(op)
        for w in writes:
            w.w = op
            w.r = []
        deps.discard(op)
        if dma:
            if op.inc == 16:
                k = (eng, self.n_dma[eng] % NDMA_SEMS[eng])
                self.n_dma[eng] += 1
            else:
                k = (eng, "cc")
            prev = self.dma_last.get(k)
            if prev is not None:
                deps.add(prev)
            self.dma_last[k] = op
            op.sem = k
        if eng == "pe" and not dma:
            deps = {d for d in deps if not (d.eng == "pe" and not d.dma)}
        op.deps = deps
        self.ops[eng].append(op)
        return op

    def pe(self, fn, reads=(), writes=()):
        return self.add("pe", fn, reads, writes)

    def act(self, fn, reads=(), writes=()):
        return self.add("act", fn, reads, writes)

    def dve(self, fn, reads=(), writes=()):
        return self.add("dve", fn, reads, writes)

    def pool(self, fn, reads=(), writes=()):
        return self.add("pool", fn, reads, writes)

    def dma(self, q, out, in_, reads=(), writes=()):
        return self.add(q, lambda e: e.dma_start(out=out, in_=in_), reads, writes, dma=True)

    def barrier(self):
        last = []
        for e in ENGS:
            for op in reversed(self.ops[e]):
                if not op.dma:
                    last.append(op)
                    break
        last.extend(self.dma_last.values())
        for r in self.all_res:
            r.w = None
            r.r = []
        for e in ENGS:
            self.add(e, None, extra=last)

    def emit(self, final_ops=()):
        nc = self.nc
        for e in ENGS:
            for op in self.ops[e]:
                for d in op.deps:
                    d.sig = True
        for op in final_ops:
            op.sig = True
        sems = {}

        def get_sem(key):
            if key not in sems:
                sems[key] = nc.alloc_semaphore("s_%s_%s" % key)
            return sems[key]

        dma_cnt = {}
        for e in ENGS:
            c = 0
            for op in self.ops[e]:
                if op.dma:
                    k = op.sem
                    dma_cnt[k] = dma_cnt.get(k, 0) + op.inc
                    op.sem = get_sem(("d" + k[0], k[1]))
                    op.val = dma_cnt[k]
                elif op.sig:
                    ep = c // SEM_EPOCH
                    c += 1
                    op.sem = get_sem((e, ep))
                    op.val = c - ep * SEM_EPOCH
        self.nsems = len(sems)
        nwaits = {e: 0 for e in ENGS}
        ninst = {e: 0 for e in ENGS}

        def run(e, eo):
            seen = {}
            for op in self.ops[e]:
                need = {}
                for d in op.deps:
                    k = id(d.sem)
                    if seen.get(k, 0) >= d.val:
                        continue
                    if k not in need or need[k][1] < d.val:
                        need[k] = (d.sem, d.val)
                for k, (sm, v) in need.items():
                    eo.wait_ge(sm, v)
                    seen[k] = v
                    nwaits[e] += 1
                if op.fn is None:
                    if op.sig:
                        eo.nop().then_inc(op.sem, op.inc)
                    continue
                ins = op.fn(eo)
                ninst[e] += 1
                if op.sig:
                    ins.then_inc(op.sem, op.inc)
            if e == "sp":
                for op in final_ops:
                    if seen.get(id(op.sem), 0) < op.val:
                        eo.wait_ge(op.sem, op.val)
                        seen[id(op.sem)] = op.val

        with nc.Block() as block:
            @block.tensor
            def _(eo):
                run("pe", eo)

            @block.scalar
            def _(eo):
                run("act", eo)

            @block.vector
            def _(eo):
                run("dve", eo)

            @block.gpsimd
            def _(eo):
                run("pool", eo)

            @block.sync
            def _(eo):
                run("sp", eo)
        self.nwaits = nwaits
        self.ninst = ninst


def v3(t, a, b):
    return t[:, :].rearrange("p (a b) -> p a b", a=a, b=b)


class Ctx:
    def __init__(self, nc):
        self.nc = nc
        self.P = Prog(nc)
        self.dram = {}
        self.dres = {}
        self.lmap = {}

    def dt(self, name, shape, dtype, kind="Internal"):
        t = self.nc.dram_tensor(name, list(shape), dtype, kind=kind)
        self.dram[name] = t
        self.dres[name] = self.P.res(name)
        return t


class Phase:
    def __init__(self, cx, name):
        self.cx = cx
        self.name = name
        self.st = contextlib.ExitStack()
        self.n = 0

    def __enter__(self):
        self.st.__enter__()
        return self

    def __exit__(self, *a):
        self.cx.P.barrier()
        return self.st.__exit__(*a)

    def sb(self, cols, dtype, parts=128):
        self.n += 1
        t = self.st.enter_context(self.cx.nc.sbuf_tensor("%s_sb%d" % (self.name, self.n), [parts, cols], dtype))
        return t, self.cx.P.res()

    def ps(self, cols, dtype=F32):
        self.n += 1
        t = self.st.enter_context(self.cx.nc.psum_tensor("%s_ps%d" % (self.name, self.n), [128, cols], dtype))
        return t, self.cx.P.res()


class Rot:
    def __init__(self, items):
        self.items = items
        self.i = 0

    def next(self):
        it = self.items[self.i % len(self.items)]
        self.i += 1
        return it


def load_cast(cx, stg, dst3, dres, src3, A, B, eng="pool", scols=2048):
    cx.P.dma("pool", dst3, src3, writes=[dres])


class LNState:
    def __init__(self, cx, ph, g_ap, b_ap, ident, ident_r, pT, pT_r, nbuf=2):
        P = cx.P
        self.cx = cx
        self.g, self.gr = ph.sb(D, F32)
        self.b, self.br = ph.sb(D, F32)
        P.dma("sp", self.g[:, :], g_ap.partition_broadcast(128), writes=[self.gr])
        P.dma("sp", self.b[:, :], b_ap.partition_broadcast(128), writes=[self.br])
        self.xt = Rot([ph.sb(D, F32) for _ in range(nbuf)])
        self.zt = Rot([ph.sb(D, F32) for _ in range(nbuf)])
        self.xb = Rot([ph.sb(D, BF16) for _ in range(nbuf)])
        self.xT = Rot([ph.sb(D, BF16) for _ in range(nbuf)])
        self.st = Rot([ph.sb(16, F32) for _ in range(2)])
        self.mh, self.mhr = ph.sb(1, F32)
        P.pool(lambda e: e.memset(self.mh[:, :], -0.5), [], [self.mhr])
        self.ident, self.ident_r = ident, ident_r
        self.pT, self.pT_r = pT, pT_r

    def prefetch(self, xsrc, ti):
        cx = self.cx
        xt, xtr = self.xt.next()
        cx.P.dma("sp", xt[:, :], cx.dram[xsrc][ti * 128:(ti + 1) * 128, :], writes=[xtr])
        return xt, xtr

    def tile(self, ypsum, yres, ti, xdst, xpre, xTdst):
        cx = self.cx
        P = cx.P
        self.flush()
        xt, xtr = xpre
        zt, ztr = self.zt.next()
        xb, xbr = self.xb.next()
        xT, xTr = self.xT.next()
        st, str_ = self.st.next()
        P.dve(lambda e: e.scalar_tensor_tensor(out=zt[:, :], in0=xt[:, :], scalar=ALPHA, in1=ypsum,
                                               op0=ALU.mult, op1=ALU.add), [xtr, yres], [ztr])
        for c in range(2):
            P.dve(lambda e, c=c: e.bn_stats(out=st[:, c * 6:(c + 1) * 6], in_=zt[:, c * 512:(c + 1) * 512]),
                  [ztr], [str_])
        P.dve(lambda e: e.bn_aggr(out=st[:, 12:14], in_=st[:, 0:12]), [str_], [str_])
        P.dve(lambda e: e.tensor_scalar(out=st[:, 14:15], in0=st[:, 13:14], scalar1=LN_EPS, scalar2=None,
                                        op0=ALU.add), [str_], [str_])
        P.pool(lambda e: e.tensor_tensor(out=st[:, 14:15], in0=st[:, 14:15], in1=self.mh[:, :], op=ALU.pow),
               [str_, self.mhr], [str_])
        P.dve(lambda e: e.tensor_scalar(out=zt[:, :], in0=zt[:, :], scalar1=st[:, 12:13], scalar2=st[:, 14:15],
                                        op0=ALU.subtract, op1=ALU.mult), [str_, ztr], [ztr])
        P.pool(lambda e: e.tensor_tensor(out=zt[:, :], in0=zt[:, :], in1=self.g[:, :], op=ALU.mult),
               [ztr, self.gr], [ztr])
        P.pool(lambda e: e.tensor_tensor(out=zt[:, :], in0=zt[:, :], in1=self.b[:, :], op=ALU.add),
               [ztr, self.br], [ztr])
        P.act(lambda e: e.activation(out=xb[:, :], in_=zt[:, :], func=AF.Copy), [ztr], [xbr])
        o1 = P.dma("pool", cx.dram[xdst][ti * 128:(ti + 1) * 128, :], zt[:, :], reads=[ztr], writes=[])
        if xTdst is not None:
            self.pending = (xb, xbr, xT, xTr, ti, xTdst)
        return o1

    pending = None

    def flush(self):
        if self.pending is None:
            return
        cx = self.cx
        P = cx.P
        xb, xbr, xT, xTr, ti, xTdst = self.pending
        self.pending = None
        for c in range(8):
            P.pe(lambda e, c=c: e.transpose(out=self.pT[:, c * 128:(c + 1) * 128], in_=xb[:, c * 128:(c + 1) * 128],
                                            identity=self.ident[:, :]), [xbr, self.ident_r], [self.pT_r])
        P.dve(lambda e: e.tensor_copy(out=xT[:, :], in_=self.pT[:, :]), [self.pT_r], [xTr])
        P.dma("pool", cx.dram[xTdst][:, :, ti * 128:(ti + 1) * 128], v3(xT, 8, 128), reads=[xTr], writes=[])


def load_ident(cx, ph):
    idt, idr = ph.sb(128, BF16)
    cx.P.dma("sp", idt[:, :], cx.dram["ident"][:, :], writes=[idr])
    return idt, idr


def phase_prep(cx, xsrc, xTdst):
    P = cx.P
    with Phase(cx, "prep") as ph:
        idt, idr = load_ident(cx, ph)
        pT, pTr = ph.ps(D, BF16)
        xt = Rot([ph.sb(D, F32) for _ in range(2)])
        xb = Rot([ph.sb(D, BF16) for _ in range(2)])
        xT = Rot([ph.sb(D, BF16) for _ in range(2)])
        for ti in range(T // 128):
            a, ar = xt.next()
            b, br = xb.next()
            c_, cr = xT.next()
            P.dma("sp", a[:, :], cx.dram[xsrc][ti * 128:(ti + 1) * 128, :], writes=[ar])
            P.act(lambda e, a=a, b=b: e.activation(out=b[:, :], in_=a[:, :], func=AF.Copy), [ar], [br])
            for c in range(8):
                P.pe(lambda e, c=c, b=b: e.transpose(out=pT[:, c * 128:(c + 1) * 128], in_=b[:, c * 128:(c + 1) * 128],
                                                    identity=idt[:, :]), [br, idr], [pTr])
            P.dve(lambda e, c_=c_: e.tensor_copy(out=c_[:, :], in_=pT[:, :]), [pTr], [cr])
            P.dma("pool", cx.dram[xTdst][:, :, ti * 128:(ti + 1) * 128], v3(c_, 8, 128), reads=[cr], writes=[])


def phase_ffn(cx, layer, xsrc, xTsrc, xdst, xTdst):
    P = cx.P
    last = None
    fl = cx.lmap["ffn"][layer]
    with Phase(cx, "ffn%d" % layer) as ph:
        idt, idr = load_ident(cx, ph)
        pT, pTr = ph.ps(D, BF16)
        ln = LNState(cx, ph, cx.dram["ffn_ln_g"][layer:layer + 1, :], cx.dram["ffn_ln_b"][layer:layer + 1, :],
                     idt, idr, pT, pTr)
        xTs, xTr = ph.sb(8 * T, BF16)
        xT3 = v3(xTs, 8, T)
        xTres = [P.res() for _ in range(4)]
        for q in range(4):
            P.dma("sp", xT3[:, :, q * 512:(q + 1) * 512], cx.dram[xTsrc][:, :, q * 512:(q + 1) * 512],
                  writes=[xTres[q]])
        stg = None
        wd, wdr = ph.sb(NFC * D, BF16)
        wd3 = v3(wd, NFC, D)
        wg = Rot([ph.sb(8 * 256, BF16) for _ in range(2)])
        wu = Rot([ph.sb(8 * 256, BF16) for _ in range(2)])
        hT, _ = ph.sb(NFC * 1024, BF16)
        hT3 = v3(hT, NFC, 1024)
        hres = [[P.res() for _ in range(2)] for _ in range(NFC)]
        sg = Rot([ph.sb(512, F32) for _ in range(2)])
        pA = Rot([ph.ps(D) for _ in range(3)])
        wgate = cx.dram["ffn_w_gate"][fl].rearrange("(kc p) f -> p kc f", p=128)
        wup = cx.dram["ffn_w_up"][fl].rearrange("(kc p) f -> p kc f", p=128)
        wdown = cx.dram["ffn_w_down"][fl].rearrange("(fc p) d -> p fc d", p=128)
        def load_fg(hh, fg):
            g_t, g_r = wg.next()
            u_t, u_r = wu.next()
            load_cast(cx, stg, v3(g_t, 8, 256), g_r, wgate[:, :, fg * 256:(fg + 1) * 256], 8, 256)
            load_cast(cx, stg, v3(u_t, 8, 256), u_r, wup[:, :, fg * 256:(fg + 1) * 256], 8, 256)
            if hh == 0:
                load_cast(cx, stg, wd3[:, 2 * fg:2 * fg + 2, :], wdr, wdown[:, 2 * fg:2 * fg + 2, :], 2, D)
            return g_t, g_r, u_t, u_r

        seq = [(hh, fg) for hh in range(2) for fg in range(NFC // 2)]
        pending = load_fg(*seq[0])
        for si, (hh, fg) in enumerate(seq):
            g_t, g_r, u_t, u_r = pending
            if si + 1 < len(seq):
                pending = load_fg(*seq[si + 1])
            g3 = v3(g_t, 8, 256)
            u3 = v3(u_t, 8, 256)
            for fc in range(2):
                fcg = fg * 2 + fc
                for tt in range(2):
                    q = hh * 2 + tt
                    ab_, ar = pA.next()
                    br = ar
                    a = ab_[:, 0:512]
                    b = ab_[:, 512:1024]
                    for kc in range(8):
                        P.pe(lambda e, kc=kc, a=a, g3=g3, fc=fc, q=q: e.matmul(
                            a, lhsT=g3[:, kc, fc * 128:(fc + 1) * 128],
                            rhs=xT3[:, kc, q * 512:(q + 1) * 512], start=(kc == 0), stop=(kc == 7)),
                            [g_r, xTres[q]], [ar])
                    for kc in range(8):
                        P.pe(lambda e, kc=kc, b=b, u3=u3, fc=fc, q=q: e.matmul(
                            b, lhsT=u3[:, kc, fc * 128:(fc + 1) * 128],
                            rhs=xT3[:, kc, q * 512:(q + 1) * 512], start=(kc == 0), stop=(kc == 7)),
                            [u_r, xTres[q]], [br])
                    s, sr = sg.next()
                    P.act(lambda e, s=s, a=a: e.activation(out=s[:, :], in_=a, func=AF.Silu), [ar], [sr])
                    P.dve(lambda e, s=s, b=b, fcg=fcg, tt=tt: e.tensor_tensor(
                        out=hT3[:, fcg, tt * 512:(tt + 1) * 512], in0=s[:, :], in1=b, op=ALU.mult),
                        [sr, br], [hres[fcg][tt]])
            if fg != NFC // 2 - 1:
                continue
            for tl in range(8):
                ti = hh * 8 + tl
                xpre = ln.prefetch(xsrc, ti)
                py, pyr = pA.next()
                for half in range(2):
                    for fcg in range(NFC):
                        P.pe(lambda e, fcg=fcg, tl=tl, half=half, py=py: e.matmul(
                            py[:, half * 512:(half + 1) * 512], lhsT=hT3[:, fcg, tl * 128:(tl + 1) * 128],
                            rhs=wd3[:, fcg, half * 512:(half + 1) * 512], start=(fcg == 0), stop=(fcg == NFC - 1)),
                            [hres[fcg][tl // 4], wdr], [pyr])
                last = ln.tile(py[:, :], pyr, ti, xdst, xpre, xTdst)
        ln.flush()
    return last


def phase_odd(cx, layer, xsrc, xTsrc, xdst, xTdst):
    P = cx.P
    i = cx.lmap["odd"][layer]
    last = None
    with Phase(cx, "odd%d" % layer) as ph:
        idt, idr = load_ident(cx, ph)
        pT, pTr = ph.ps(D, BF16)
        ln = LNState(cx, ph, cx.dram["mix_ln_g"][layer:layer + 1, :], cx.dram["mix_ln_b"][layer:layer + 1, :],
                     idt, idr, pT, pTr, nbuf=1)
        stg = None
        wu, wur = ph.sb(8 * 2048, BF16)
        wv, wvr = ph.sb(8 * 2048, BF16)
        wo, wor = ph.sb(16 * D, BF16)
        wst, wsr = ph.sb(8 * 128, BF16)
        wu3, wv3, wo3, ws3 = v3(wu, 8, 2048), v3(wv, 8, 2048), v3(wo, 16, D), v3(wst, 8, 128)
        win = cx.dram["odd_w_in"][i].rearrange("(kc p) f -> p kc f", p=128)
        wurs = [P.res() for _ in range(4)]
        wvrs = [P.res() for _ in range(4)]
        wors = [P.res() for _ in range(2)]
        for n in range(4):
            load_cast(cx, stg, wu3[:, :, n * 512:(n + 1) * 512], wurs[n], win[:, :, n * 512:(n + 1) * 512], 8, 512, scols=1024)
        for n in range(4):
            load_cast(cx, stg, wv3[:, :, n * 512:(n + 1) * 512], wvrs[n], win[:, :, 2048 + n * 512:2048 + (n + 1) * 512],
                      8, 512, scols=1024)
        load_cast(cx, stg, ws3, wsr, cx.dram["odd_wsT"][i].rearrange("g q p -> q g p"), 8, 128, scols=1024)
        wod = cx.dram["odd_w_out"][i].rearrange("(cc p) d -> p cc d", p=128)
        for hf in range(2):
            load_cast(cx, stg, wo3[:, :, hf * 512:(hf + 1) * 512], wors[hf], wod[:, :, hf * 512:(hf + 1) * 512], 16, 512)
        sg, sgr = ph.sb(2048, F32)
        sb_, sbr = ph.sb(2048, F32)
        bt, btr = ph.sb(2048, F32)
        P.dma("sp", sg[:, :], cx.dram["odd_sgu_norm_g"][i:i + 1, :].partition_broadcast(128), writes=[sgr])
        P.dma("sp", sb_[:, :], cx.dram["odd_sgu_norm_b"][i:i + 1, :].partition_broadcast(128), writes=[sbr])
        P.dma("sp", bt[:, :], cx.dram["odd_b16"][i:i + 1, :].partition_broadcast(128), writes=[btr])
        xTt = Rot([ph.sb(8 * 512, BF16) for _ in range(2)])
        uT, _ = ph.sb(16 * 512, BF16)
        uT3 = v3(uT, 16, 512)
        ures = [P.res() for _ in range(16)]
        vbs = Rot([ph.sb(2048, BF16) for _ in range(2)])
        t32, t32r = ph.sb(2048, F32)
        vn, vnr = ph.sb(2048, BF16)
        mT, mTr = ph.sb(2048, BF16)
        mT3 = v3(mT, 16, 128)
        st, str_ = ph.sb(32, F32)
        pA = Rot([ph.ps(1024) for _ in range(3)])
        pS = Rot([ph.ps(512) for _ in range(1)])
        state = {}

        def emit_u(stile):
            xt_, xtr_ = xTt.next()
            x3 = v3(xt_, 8, 512)
            P.dma("sp", x3, cx.dram[xTsrc][:, :, stile * 512:(stile + 1) * 512], writes=[xtr_])
            state["x"] = (x3, xtr_)
            for cc in range(16):
                pb, pbr = pA.next()
                pu = pb[:, 0:512]
                for kc in range(8):
                    P.pe(lambda e, kc=kc, cc=cc, x3=x3, pu=pu: e.matmul(pu, lhsT=wu3[:, kc, cc * 128:(cc + 1) * 128],
                                                                   rhs=x3[:, kc, :], start=(kc == 0), stop=(kc == 7)),
                         [wurs[cc // 4], xtr_], [pbr])
                P.act(lambda e, cc=cc, pu=pu: e.activation(out=uT3[:, cc, :], in_=pu, func=AF.Gelu), [pbr], [ures[cc]])

        def emit_v(ti):
            x3, xtr_ = state["x"]
            tl = ti % 4
            vb, vbr = vbs.next()
            for vh in range(2):
                pv, pvr = pA.next()
                for n in range(2):
                    for kc in range(8):
                        P.pe(lambda e, kc=kc, n=n, vh=vh, pv=pv, x3=x3, tl=tl: e.matmul(
                            pv[:, n * 512:(n + 1) * 512], lhsT=x3[:, kc, tl * 128:(tl + 1) * 128],
                            rhs=wv3[:, kc, (vh * 2 + n) * 512:(vh * 2 + n + 1) * 512],
                            start=(kc == 0), stop=(kc == 7)), [wvrs[vh * 2 + n], xtr_], [pvr])
                P.act(lambda e, pv=pv, vh=vh, vb=vb: e.activation(out=vb[:, vh * 1024:(vh + 1) * 1024], in_=pv[:, :],
                                                                  func=AF.Gelu), [pvr], [vbr])
            state[ti] = (vb, vbr)

        def emit_rest(ti):
            vb, vbr = state.pop(ti)
            tl = ti % 4
            xpre = ln.prefetch(xsrc, ti)
            for c in range(4):
                P.dve(lambda e, c=c: e.bn_stats(out=st[:, c * 6:(c + 1) * 6], in_=vb[:, c * 512:(c + 1) * 512]),
                      [vbr], [str_])
            P.dve(lambda e: e.bn_aggr(out=st[:, 24:26], in_=st[:, 0:24]), [str_], [str_])
            P.dve(lambda e: e.tensor_scalar(out=st[:, 26:27], in0=st[:, 25:26], scalar1=LN_EPS, scalar2=None,
                                            op0=ALU.add), [str_], [str_])
            P.pool(lambda e: e.tensor_tensor(out=st[:, 26:27], in0=st[:, 26:27], in1=ln.mh[:, :], op=ALU.pow),
                   [str_, ln.mhr], [str_])
            P.dve(lambda e: e.scalar_tensor_tensor(out=t32[:, :], in0=vb[:, :], scalar=st[:, 24:25], in1=sg[:, :],
                                                   op0=ALU.subtract, op1=ALU.mult), [vbr, str_, sgr], [t32r])
            P.dve(lambda e: e.scalar_tensor_tensor(out=vn[:, :], in0=t32[:, :], scalar=st[:, 26:27], in1=sb_[:, :],
                                                   op0=ALU.mult, op1=ALU.add), [t32r, str_, sbr], [vnr])
            for sq in range(4):
                ps_, psr = pS.next()
                for c4 in range(4):
                    cc = sq * 4 + c4
                    P.pe(lambda e, cc=cc, c4=c4, ps_=ps_: e.matmul(ps_[:, c4 * 128:(c4 + 1) * 128],
                                                                    lhsT=vn[:, cc * 128:(cc + 1) * 128],
                                                                    rhs=ws3[:, cc // 2, :], start=True, stop=True),
                         [vnr, wsr], [psr])
                P.dve(lambda e, sq=sq, ps_=ps_: e.tensor_tensor(out=t32[:, sq * 512:(sq + 1) * 512], in0=ps_[:, :],
                                                                in1=bt[:, sq * 512:(sq + 1) * 512], op=ALU.add),
                      [psr, btr], [t32r])
            P.pool(lambda e, tl=tl: e.tensor_tensor(out=mT3, in0=v3(t32, 16, 128),
                                                    in1=uT3[:, :, tl * 128:(tl + 1) * 128], op=ALU.mult),
                   [t32r] + ures, [mTr])
            py, pyr = pA.next()
            for half in range(2):
                for cc in range(16):
                    P.pe(lambda e, cc=cc, half=half, py=py: e.matmul(
                        py[:, half * 512:(half + 1) * 512], lhsT=mT3[:, cc, :],
                        rhs=wo3[:, cc, half * 512:(half + 1) * 512], start=(cc == 0), stop=(cc == 15)),
                        [mTr, wors[half]], [pyr])
            return ln.tile(py[:, :], pyr, ti, xdst, xpre, xTdst)

        NTT = T // 128
        emit_u(0)
        emit_v(0)
        for ti in range(NTT):
            if ti + 1 < NTT and (ti + 1) % 4 != 0:
                emit_v(ti + 1)
            last = emit_rest(ti)
            if ti + 1 < NTT and (ti + 1) % 4 == 0:
                emit_u((ti + 1) // 4)
                emit_v(ti + 1)
        ln.flush()
    return last


def rms_chunks(cx, ph, pacc, paccr, pss, pssr, ones, onesr, wT3, wr, col0, nch, x3, xr, tcols, gcol, gr,
               craw, crawr, csq, csqr, rb, rbr, out3, outr, dim):
    P = cx.P
    for c in range(nch):
        pa, par = pacc.next()
        for kc in range(8):
            P.pe(lambda e, kc=kc, c=c, pa=pa: e.matmul(pa[:, :], lhsT=wT3[:, kc, col0 + c * 128:col0 + (c + 1) * 128],
                                                        rhs=x3[:, kc, tcols], start=(kc == 0), stop=(kc == 7)),
                 [wr, xr], [par])
        P.act(lambda e, c=c, pa=pa: e.activation(out=craw[:, c * 512:(c + 1) * 512], in_=pa[:, :], func=AF.Copy),
              [par], [crawr])
        P.act(lambda e, c=c, pa=pa: e.activation(out=csq[:, c * 512:(c + 1) * 512], in_=pa[:, :], func=AF.Square),
              [par], [csqr])
    for c in range(nch):
        P.pe(lambda e, c=c: e.matmul(pss[:, :], lhsT=ones[:, :], rhs=csq[:, c * 512:(c + 1) * 512],
                                     start=(c == 0), stop=(c == nch - 1)), [onesr, csqr], [pssr])
    P.dve(lambda e: e.tensor_scalar(out=rb[:, :], in0=pss[:, :], scalar1=1.0 / dim, scalar2=RMS_EPS,
                                    op0=ALU.mult, op1=ALU.add), [pssr], [rbr])
    P.act(lambda e: e.activation(out=rb[:, :], in_=rb[:, :], func=AF.Sqrt), [rbr], [rbr])
    P.dve(lambda e: e.reciprocal(out=rb[:, :], in_=rb[:, :]), [rbr], [rbr])
    for c in range(nch):
        P.dve(lambda e, c=c: e.scalar_tensor_tensor(out=out3[:, c, :], in0=craw[:, c * 512:(c + 1) * 512],
                                                    scalar=gcol[:, c:c + 1], in1=rb[:, :],
                                                    op0=ALU.mult, op1=ALU.mult), [crawr, gr, rbr], [outr])


def rotary(cx, p0, p0r, p1, p1r, cos, sin, csr, tcols, t1, t1r, t2, t2r, out, outr):
    P = cx.P
    P.dve(lambda e: e.tensor_tensor(out=t1[0:64, :], in0=p0[0:64, :], in1=cos[0:64, tcols], op=ALU.mult),
          [p0r, csr], [t1r])
    P.dve(lambda e: e.tensor_tensor(out=t2[0:64, :], in0=p1[0:64, :], in1=sin[0:64, tcols], op=ALU.mult),
          [p1r, csr], [t2r])
    P.pool(lambda e: e.tensor_tensor(out=out, in0=t1[0:64, :], in1=t2[0:64, :], op=ALU.add), [t1r, t2r], [outr])


def phase_eproj(cx, layer, xTsrc, gb):
    P = cx.P
    i = cx.lmap["even"][layer]
    GB = cx.dram[gb]
    with Phase(cx, "eproj%d" % layer) as ph:
        stg = None
        xTs, xr = ph.sb(8 * T, BF16)
        x3 = v3(xTs, 8, T)
        P.dma("sp", x3, cx.dram[xTsrc][:, :, :], writes=[xr])
        win = cx.dram["even_w_in"][i].rearrange("(kc p) f -> p kc f", p=128)
        wkv, wkvr = ph.sb(8 * 256, BF16)
        wkr, wkrr = ph.sb(8 * 256, BF16)
        wf, wfr = ph.sb(8 * 512, BF16)
        wkv3, wkr3, wf3 = v3(wkv, 8, 256), v3(wkr, 8, 256), v3(wf, 8, 512)
        P.pool(lambda e: e.memset(wkr[:, :], 0.0), [], [wkrr])
        load_cast(cx, stg, wkv3, wkvr, win[:, :, 384:640], 8, 256)
        load_cast(cx, stg, wkr3[:, :, 0:64], wkrr, win[:, :, 640:704], 8, 64)
        load_cast(cx, stg, wkr3[:, :, 128:160], wkrr, win[:, :, 672:704], 8, 32)
        load_cast(cx, stg, wkr3[:, :, 160:192], wkrr, win[:, :, 640:672], 8, 32)
        load_cast(cx, stg, wf3, wfr, win[:, :, 704:1216], 8, 512)
        ones, onesr = ph.sb(128, BF16)
        P.pool(lambda e: e.memset(ones[:, :], 1.0), [], [onesr])
        gcol, gr = ph.sb(2, F32)
        P.dma("sp", gcol[:, :], cx.dram["kvg_t"][i], writes=[gr])
        cos, csr = ph.sb(T, F32)
        sin, _ = ph.sb(T, F32)
        P.dma("sp", cos[0:64, :], cx.dram["cos2"][:, :], writes=[csr])
        P.dma("sp", sin[0:64, :], cx.dram["sin2"][:, :], writes=[csr])
        craw, crawr = ph.sb(2 * 512, F32)
        csq, csqr = ph.sb(2 * 512, BF16)
        rb, rbr = ph.sb(512, F32)
        cn = Rot([ph.sb(2 * 512, BF16) for _ in range(2)])
        t1, t1r = ph.sb(512, F32)
        t2, t2r = ph.sb(512, F32)
        kro = Rot([ph.sb(512, BF16) for _ in range(2)])
        fo = Rot([ph.sb(512, BF16) for _ in range(2)])
        pacc = Rot([ph.ps(512) for _ in range(3)])
        pss, pssr = ph.ps(512)
        pk = [ph.ps(512) for _ in range(2)]
        pf = Rot([ph.ps(512) for _ in range(2)])
        fview = GB[0:512, :].rearrange("r (a c) -> (r a) c", c=512)
        for tt in range(4):
            tcols = slice(tt * 512, (tt + 1) * 512)
            o, orr = cn.next()
            o3 = v3(o, 2, 512)
            rms_chunks(cx, ph, pacc, None, pss, pssr, ones, onesr, wkv3, wkvr, 0, 2, x3, xr, tcols, gcol, gr,
                       craw, crawr, csq, csqr, rb, rbr, o3, orr, 256.0)
            P.dma("pool", GB[512:768, tcols].rearrange("(c p) t -> p c t", p=128), o3, reads=[orr], writes=[])
            for j in range(2):
                for kc in range(8):
                    P.pe(lambda e, kc=kc, j=j, tcols=tcols: e.matmul(pk[j][0][:, :], lhsT=wkr3[:, kc, j * 128:(j + 1) * 128],
                                                                     rhs=x3[:, kc, tcols], start=(kc == 0), stop=(kc == 7)),
                         [wkrr, xr], [pk[j][1]])
            ko, kor = kro.next()
            rotary(cx, pk[0][0], pk[0][1], pk[1][0], pk[1][1], cos, sin, csr, tcols, t1, t1r, t2, t2r, ko[0:64, :], kor)
            P.dma("pool", GB[768:832, tcols], ko[0:64, :], reads=[kor], writes=[])
            for tl in range(4):
                t0 = tt * 512 + tl * 128
                p, pr = pf.next()
                for kc in range(8):
                    P.pe(lambda e, kc=kc, p=p, t0=t0: e.matmul(p[:, :], lhsT=x3[:, kc, t0:t0 + 128], rhs=wf3[:, kc, :],
                                                              start=(kc == 0), stop=(kc == 7)), [wfr, xr], [pr])
                f_, fr = fo.next()
                P.act(lambda e, p=p, f_=f_: e.activation(out=f_[:, :], in_=p[:, :], func=AF.Copy), [pr], [fr])
                P.dma("pool", fview[t0:t0 + 128, :], f_[:, :], reads=[fr], writes=[])


def phase_eq(cx, layer, xTsrc):
    P = cx.P
    i = cx.lmap["even"][layer]
    with Phase(cx, "eq%d" % layer) as ph:
        stg = None
        xTs, xr = ph.sb(8 * T, BF16)
        x3 = v3(xTs, 8, T)
        P.dma("sp", x3, cx.dram[xTsrc][:, :, :], writes=[xr])
        win = cx.dram["even_w_in"][i].rearrange("(kc p) f -> p kc f", p=128)
        wuq_d = cx.dram["even_w_uq"][i].rearrange("(kc p) f -> p kc f", p=128)
        wq1, wq1r = ph.sb(8 * 384, BF16)
        wq13 = v3(wq1, 8, 384)
        load_cast(cx, stg, wq13, wq1r, win[:, :, 0:384], 8, 384)
        wuq, wuqr = ph.sb(3 * 1536, BF16)
        wuq3 = v3(wuq, 3, 1536)
        for n in range(3):
            load_cast(cx, stg, wuq3[:, :, n * 512:(n + 1) * 512], wuqr, wuq_d[:, :, n * 512:(n + 1) * 512], 3, 512)
        wsw, wswr = ph.sb(3 * 512, BF16)
        wsw3 = v3(wsw, 3, 512)
        for h in range(8):
            b = h * 192 + 128
            load_cast(cx, stg, wsw3[:, :, h * 64:h * 64 + 32], wswr, wuq_d[:, :, b + 32:b + 64], 3, 32)
            load_cast(cx, stg, wsw3[:, :, h * 64 + 32:h * 64 + 64], wswr, wuq_d[:, :, b:b + 32], 3, 32)
        ones, onesr = ph.sb(128, BF16)
        P.pool(lambda e: e.memset(ones[:, :], 1.0), [], [onesr])
        gcol, gr = ph.sb(3, F32)
        P.dma("sp", gcol[:, :], cx.dram["qg_t"][i], writes=[gr])
        cos, csr = ph.sb(T, F32)
        sin, _ = ph.sb(T, F32)
        P.dma("sp", cos[0:64, :], cx.dram["cos2"][:, :], writes=[csr])
        P.dma("sp", sin[0:64, :], cx.dram["sin2"][:, :], writes=[csr])
        craw, crawr = ph.sb(3 * 512, F32)
        csq, csqr = ph.sb(3 * 512, BF16)
        rb, rbr = ph.sb(512, F32)
        cqn, cqnr = ph.sb(3 * 512, BF16)
        cqn3 = v3(cqn, 3, 512)
        t1, t1r = ph.sb(512, F32)
        t2, t2r = ph.sb(512, F32)
        qno = Rot([ph.sb(512, BF16) for _ in range(2)])
        qro = Rot([ph.sb(512, BF16) for _ in range(2)])
        pacc = Rot([ph.ps(512) for _ in range(2)])
        pss, pssr = ph.ps(512)
        pn = Rot([ph.ps(512) for _ in range(2)])
        pk = [ph.ps(512) for _ in range(2)]
        for tt in range(4):
            tcols = slice(tt * 512, (tt + 1) * 512)
            rms_chunks(cx, ph, pacc, None, pss, pssr, ones, onesr, wq13, wq1r, 0, 3, x3, xr, tcols, gcol, gr,
                       craw, crawr, csq, csqr, rb, rbr, cqn3, cqnr, 384.0)
            for h in range(8):
                p, pr = pn.next()
                for kc in range(3):
                    P.pe(lambda e, kc=kc, p=p, h=h: e.matmul(p[:, :], lhsT=wuq3[:, kc, h * 192:h * 192 + 128],
                                                            rhs=cqn3[:, kc, :], start=(kc == 0), stop=(kc == 2)),
                         [wuqr, cqnr], [pr])
                qn, qnr = qno.next()
                P.act(lambda e, p=p, qn=qn: e.activation(out=qn[:, :], in_=p[:, :], func=AF.Copy), [pr], [qnr])
                P.dma("pool", cx.dram["QSn"][h, :, tcols], qn[:, :], reads=[qnr], writes=[])
                for kc in range(3):
                    P.pe(lambda e, kc=kc, h=h: e.matmul(pk[0][0][0:64, :], lhsT=wuq3[:, kc, h * 192 + 128:h * 192 + 192],
                                                        rhs=cqn3[:, kc, :], start=(kc == 0), stop=(kc == 2)),
                         [wuqr, cqnr], [pk[0][1]])
                for kc in range(3):
                    P.pe(lambda e, kc=kc, h=h: e.matmul(pk[1][0][0:64, :], lhsT=wsw3[:, kc, h * 64:(h + 1) * 64],
                                                        rhs=cqn3[:, kc, :], start=(kc == 0), stop=(kc == 2)),
                         [wswr, cqnr], [pk[1][1]])
                qr_, qrr = qro.next()
                rotary(cx, pk[0][0], pk[0][1], pk[1][0], pk[1][1], cos, sin, csr, tcols, t1, t1r, t2, t2r,
                       qr_[0:64, :], qrr)
                P.dma("pool", cx.dram["QSr"][h, :, tcols], qr_[0:64, :], reads=[qrr], writes=[])


def phase_ekv(cx, layer, ga):
    P = cx.P
    i = cx.lmap["even"][layer]
    GA = cx.dram[ga]
    KSv = cx.dram["KS"].ap().rearrange("h p t -> p h t")
    VSv = cx.dram["VS"].ap().rearrange("h p k d -> p h k d")
    with Phase(cx, "ekv%d" % layer) as ph:
        stg = None
        wuk, wukr = ph.sb(2 * 1024, BF16)
        wuv, wuvr = ph.sb(2 * 1024, BF16)
        wuk3, wuv3 = v3(wuk, 2, 1024), v3(wuv, 2, 1024)
        load_cast(cx, stg, wuk3, wukr, cx.dram["even_w_uk"][i].rearrange("(c p) f -> p c f", p=128), 2, 1024)
        load_cast(cx, stg, wuv3, wuvr, cx.dram["even_w_uv"][i].rearrange("(c p) f -> p c f", p=128), 2, 1024)
        lat = Rot([ph.sb(2 * 512, BF16) for _ in range(3)])
        kt = Rot([ph.sb(8 * 512, BF16) for _ in range(2)])
        vt = Rot([ph.sb(4096, BF16) for _ in range(2)])
        pk = Rot([ph.ps(512) for _ in range(4)])
        pv = Rot([ph.ps(1024) for _ in range(2)])
        ev = 0
        for b in range(SEQ // 512):
            r, bb = b // 4, b % 4
            l, lr = lat.next()
            l3 = v3(l, 2, 512)
            P.dma("sp", l3, GA[r, 512:768, bb * 512:(bb + 1) * 512].rearrange("(c p) t -> p c t", p=128), writes=[lr])
            k_, kr_ = kt.next()
            k3 = v3(k_, 8, 512)
            for h in range(8):
                p, pr = pk.next()
                for c in range(2):
                    P.pe(lambda e, c=c, h=h, p=p, l3=l3: e.matmul(p[:, :], lhsT=wuk3[:, c, h * 128:(h + 1) * 128],
                                                                 rhs=l3[:, c, :], start=(c == 0), stop=(c == 1)),
                         [wukr, lr], [pr])
                if ev % 2 == 0:
                    P.act(lambda e, p=p, h=h, k3=k3: e.activation(out=k3[:, h, :], in_=p[:, :], func=AF.Copy), [pr], [kr_])
                else:
                    P.dve(lambda e, p=p, h=h, k3=k3: e.tensor_copy(out=k3[:, h, :], in_=p[:, :]), [pr], [kr_])
                ev += 1
            P.dma("pool", KSv[:, :, b * 512:(b + 1) * 512], k3, reads=[kr_], writes=[])
            v_, vr_ = vt.next()
            v4 = v_[:, :].rearrange("p (h t d) -> p h t d", t=4, h=8, d=128)
            for t4 in range(4):
                p, pr = pv.next()
                for n in range(2):
                    for c in range(2):
                        P.pe(lambda e, c=c, n=n, p=p, l3=l3, t4=t4: e.matmul(
                            p[:, n * 512:(n + 1) * 512], lhsT=l3[:, c, t4 * 128:(t4 + 1) * 128],
                            rhs=wuv3[:, c, n * 512:(n + 1) * 512], start=(c == 0), stop=(c == 1)), [wuvr, lr], [pr])
                dst = v4[:, :, t4, :]
                src = p[:, :].rearrange("p (h d) -> p h d", h=8)
                if ev % 2 == 0:
                    P.act(lambda e, src=src, dst=dst: e.activation(out=dst, in_=src, func=AF.Copy), [pr], [vr_])
                else:
                    P.dve(lambda e, src=src, dst=dst: e.tensor_copy(out=dst, in_=src), [pr], [vr_])
                ev += 1
            P.dma("pool", VSv[:, :, b * 4:(b + 1) * 4, :].rearrange("p h t d -> p h (t d)"),
                  v_[:, :].rearrange("p (h x) -> p h x", h=8), reads=[vr_], writes=[])


def phase_efft(cx, layer, ga, mixT):
    P = cx.P
    GA = cx.dram[ga]
    with Phase(cx, "efft%d" % layer) as ph:
        cs1, cs1r = ph.sb(256, BF16)
        t3, t3r = ph.sb(128 * 64, BF16)
        cd, cdr = ph.sb(256, BF16)
        P.dma("sp", cs1[:, :], cx.dram["cs1"][:, :], writes=[cs1r])
        P.dma("sp", t3[:, :], cx.dram["t3"][:, :], writes=[t3r])
        P.dma("sp", cd[:, :], cx.dram["cdft"][:, :], writes=[cdr])
        t33 = v3(t3, 128, 64)
        Fg = Rot([ph.sb(128 * 128, BF16) for _ in range(2)])
        Ag, Agr = ph.sb(128 * 256, BF16)
        A5 = Ag[:, :].rearrange("p (r k c) -> p r k c", r=2, k=128, c=128)
        XT, XTr = ph.sb(2 * T, BF16)
        XT4 = XT[:, :].rearrange("p (r b k) -> p r b k", r=2, b=16, k=128)
        XT3 = v3(XT, 2, T)
        yT = Rot([ph.sb(512, BF16) for _ in range(2)])
        pA = Rot([ph.ps(512) for _ in range(3)])
        pX = Rot([ph.ps(512) for _ in range(2)])
        pY = Rot([ph.ps(512) for _ in range(2)])
        ev = 0
        for g in range(4):
            F, Fr = Fg.next()
            F3 = v3(F, 128, 128)
            for r in range(NCORES):
                src = GA[r, 0:512, :].rearrange("r (a c) -> (r a) c", c=512)[:, g * 128:(g + 1) * 128]
                P.dma("sp", F3[r * 16:(r + 1) * 16, :, :], src.rearrange("(t s) c -> t s c", s=128), writes=[Fr])
            for c2 in range(64):
                p, pr = pA.next()
                for j in range(2):
                    c = c2 * 2 + j
                    P.pe(lambda e, c=c, j=j, p=p, F3=F3: e.matmul(p[:, j * 256:(j + 1) * 256], lhsT=F3[:, :, c],
                                                                 rhs=cs1[:, :], start=True, stop=True),
                         [Fr, cs1r], [pr])
                dst = A5[:, :, :, c2 * 2:c2 * 2 + 2]
                src = p[:, :].rearrange("p (j r k) -> p r k j", j=2, r=2, k=128)
                if ev % 2 == 0:
                    P.act(lambda e, src=src, dst=dst: e.activation(out=dst, in_=src, func=AF.Copy), [pr], [Agr])
                else:
                    P.dve(lambda e, src=src, dst=dst: e.tensor_copy(out=dst, in_=src), [pr], [Agr])
                ev += 1
            for kb in range(8):
                p, pr = pX.next()
                for kl in range(16):
                    k1 = kb * 16 + kl
                    P.pe(lambda e, k1=k1, kl=kl, p=p: e.matmul(p[:, kl * 32:(kl + 1) * 32], lhsT=A5[:, 0, k1, :],
                                                               rhs=t33[:, k1, 0:32], start=True, stop=False),
                         [Agr, t3r], [pr])
                    P.pe(lambda e, k1=k1, kl=kl, p=p: e.matmul(p[:, kl * 32:(kl + 1) * 32], lhsT=A5[:, 1, k1, :],
                                                               rhs=t33[:, k1, 32:64], start=False, stop=True),
                         [Agr, t3r], [pr])
                p3 = p[:, :].rearrange("p (k x) -> p k x", k=16, x=32)
                for ri in range(2):
                    src = p3[:, :, ri * 16:(ri + 1) * 16].rearrange("p k b -> p b k")
                    dst = XT4[:, ri, :, kb * 16:(kb + 1) * 16]
                    if ri == 0:
                        P.act(lambda e, src=src, dst=dst: e.activation(out=dst, in_=src, func=AF.Copy), [pr], [XTr])
                    else:
                        P.dve(lambda e, src=src, dst=dst: e.tensor_copy(out=dst, in_=src), [pr], [XTr])
            for tt in range(4):
                p, pr = pY.next()
                P.pe(lambda e, p=p, tt=tt: e.matmul(p[:, :], lhsT=cd[:, 0:128], rhs=XT3[:, 0, tt * 512:(tt + 1) * 512],
                                                    start=True, stop=False), [cdr, XTr], [pr])
                P.pe(lambda e, p=p, tt=tt: e.matmul(p[:, :], lhsT=cd[:, 128:256], rhs=XT3[:, 1, tt * 512:(tt + 1) * 512],
                                                    start=False, stop=True), [cdr, XTr], [pr])
                y, yr = yT.next()
                P.act(lambda e, p=p, y=y: e.activation(out=y[:, :], in_=p[:, :], func=AF.Copy), [pr], [yr])
                P.dma("pool", cx.dram[mixT][8 + g, :, tt * 512:(tt + 1) * 512], y[:, :], reads=[yr], writes=[])


def phase_eattn(cx, layer, ga, mixT):
    P = cx.P
    GA = cx.dram[ga]
    NCH = 8
    with Phase(cx, "eattn%d" % layer) as ph:
        ones, onesr = ph.sb(128, BF16)
        P.pool(lambda e: e.memset(ones[:, :], 1.0), [], [onesr])
        KR, _ = ph.sb(SEQ, BF16)
        krres = [P.res() for _ in range(NCH)]
        zr = P.res()
        P.pool(lambda e: e.memset(KR[64:128, :], 0.0), [], [zr])
        for r in range(NCH):
            P.dma("sp", KR[0:64, r * 2048:(r + 1) * 2048], GA[r, 768:832, :], writes=[krres[r]])
        KN, _ = ph.sb(SEQ, BF16)
        knres = [P.res() for _ in range(NCH)]
        VV, _ = ph.sb(SEQ, BF16)
        VV3 = v3(VV, 128, 128)
        vres = [P.res() for _ in range(NCH)]
        QN = Rot([ph.sb(T, BF16) for _ in range(2)])
        QR = Rot([ph.sb(T, BF16) for _ in range(2)])
        for (q_, qr_) in QR.items:
            P.pool(lambda e, q_=q_: e.memset(q_[64:128, :], 0.0), [], [qr_])
        PT = Rot([ph.sb(1024, BF16) for _ in range(4)])
        ACC = Rot([ph.sb(1024, F32) for _ in range(2)])
        ACB = Rot([ph.sb(1024, BF16) for _ in range(2)])
        rs = Rot([ph.sb(512, F32) for _ in range(2)])
        ob = Rot([ph.sb(512, BF16) for _ in range(2)])
        pS = Rot([ph.ps(1024) for _ in range(2)])
        pO = Rot([ph.ps(512) for _ in range(2)])
        pR = Rot([ph.ps(512) for _ in range(2)])
        KSd, VSd = cx.dram["KS"], cx.dram["VS"]
        for h in range(8):
            qn, qnr = QN.next()
            qr, qrr = QR.next()
            P.dma("sp", qn[:, :], cx.dram["QSn"][h, :, :], writes=[qnr])
            P.dma("sp", qr[0:64, :], cx.dram["QSr"][h, :, :], writes=[qrr])
            for c in range(NCH):
                P.dma("sp", KN[:, c * 2048:(c + 1) * 2048], KSd[h, :, c * 2048:(c + 1) * 2048], writes=[knres[c]])
                P.dma("sp", VV3[:, c * 16:(c + 1) * 16, :], VSd[h, :, c * 16:(c + 1) * 16, :], writes=[vres[c]])
            for qt in range(4):
                qc = slice(qt * 512, (qt + 1) * 512)
                po, por = pO.next()
                pr_, prr = pR.next()
                acc, accr = ACC.next()

                def qk(pi, qn=qn, qr=qr, qnr=qnr, qrr=qrr, qc=qc):
                    s, sr = pS.next()
                    for j in range(2):
                        kt = pi * 2 + j
                        c = kt // 16
                        kc = slice(kt * 128, (kt + 1) * 128)
                        P.pe(lambda e, s=s, j=j, kc=kc: e.matmul(s[:, j * 512:(j + 1) * 512], lhsT=KN[:, kc], rhs=qn[:, qc],
                                                                 start=True, stop=False), [knres[c], qnr], [sr])
                        P.pe(lambda e, s=s, j=j, kc=kc: e.matmul(s[:, j * 512:(j + 1) * 512], lhsT=KR[:, kc], rhs=qr[:, qc],
                                                                 start=False, stop=True), [krres[c], zr, qrr], [sr])
                    return s, sr

                def pv(pi, pt, ptr, po=po, por=por, acc=acc, accr=accr):
                    for j in range(2):
                        kt = pi * 2 + j
                        c = kt // 16
                        P.pe(lambda e, j=j, kt=kt, pt=pt: e.matmul(po[:, :], lhsT=VV3[:, kt, :], rhs=pt[:, j * 512:(j + 1) * 512],
                                                                   start=(kt == 0), stop=(kt == 127)), [vres[c], ptr], [por])
                    if pi == 0:
                        P.dve(lambda e, pt=pt: e.tensor_copy(out=acc[:, :], in_=pt[:, :]), [ptr], [accr])
                    else:
                        P.dve(lambda e, pt=pt: e.tensor_tensor(out=acc[:, :], in0=acc[:, :], in1=pt[:, :], op=ALU.add),
                              [ptr, accr], [accr])

                pend = None
                for pi in range(64):
                    s, sr = qk(pi)
                    pt, ptr = PT.next()
                    P.act(lambda e, s=s, pt=pt: e.activation(out=pt[:, :], in_=s[:, :], func=AF.Exp, scale=SCALE),
                          [sr], [ptr])
                    if pend is not None:
                        pv(*pend)
                    pend = (pi, pt, ptr)
                pv(*pend)
                ab, abr = ACB.next()
                P.act(lambda e, ab=ab, acc=acc: e.activation(out=ab[:, :], in_=acc[:, :], func=AF.Copy), [accr], [abr])
                for j in range(2):
                    P.pe(lambda e, j=j, ab=ab, pr_=pr_: e.matmul(pr_[:, :], lhsT=ones[:, :], rhs=ab[:, j * 512:(j + 1) * 512],
                                                                 start=(j == 0), stop=(j == 1)), [onesr, abr], [prr])
                r_, rr = rs.next()
                o_, orr = ob.next()
                P.dve(lambda e, r_=r_, pr_=pr_: e.reciprocal(out=r_[:, :], in_=pr_[:, :]), [prr], [rr])
                P.dve(lambda e, r_=r_, o_=o_, po=po: e.tensor_tensor(out=o_[:, :], in0=po[:, :], in1=r_[:, :], op=ALU.mult),
                      [por, rr], [orr])
                P.dma("pool", cx.dram[mixT][h, :, qc], o_[:, :], reads=[orr], writes=[])


def phase_eout(cx, layer, mixT, xsrc, xdst, xTdst):
    P = cx.P
    i = cx.lmap["even"][layer]
    last = None
    with Phase(cx, "eout%d" % layer) as ph:
        idt, idr = load_ident(cx, ph)
        pT, pTr = ph.ps(D, BF16)
        ln = LNState(cx, ph, cx.dram["mix_ln_g"][layer:layer + 1, :], cx.dram["mix_ln_b"][layer:layer + 1, :],
                     idt, idr, pT, pTr)
        stg = None
        wo, wor = ph.sb(12 * D, BF16)
        wo3 = v3(wo, 12, D)
        load_cast(cx, stg, wo3, wor, cx.dram["even_w_out"][i].rearrange("(c p) d -> p c d", p=128), 12, D)
        mx, mxr = ph.sb(12 * T, BF16)
        mx3 = v3(mx, 12, T)
        P.dma("sp", mx3, cx.dram[mixT].ap().rearrange("c p t -> p c t"), writes=[mxr])
        pY = Rot([ph.ps(D) for _ in range(2)])
        for ti in range(T // 128):
            xpre = ln.prefetch(xsrc, ti)
            py, pyr = pY.next()
            for half in range(2):
                for c in range(12):
                    P.pe(lambda e, c=c, half=half, py=py, ti=ti: e.matmul(
                        py[:, half * 512:(half + 1) * 512], lhsT=mx3[:, c, ti * 128:(ti + 1) * 128],
                        rhs=wo3[:, c, half * 512:(half + 1) * 512], start=(c == 0), stop=(c == 11)), [mxr, wor], [pyr])
            last = ln.tile(py[:, :], pyr, ti, xdst, xpre, xTdst)
        ln.flush()
    return last


W_EVEN = [("even_w_in", [D, 1216]), ("even_w_uq", [384, 1536]), ("even_w_uk", [256, 1024]),
          ("even_w_uv", [256, 1024]), ("even_w_out", [1536, D]), ("kvg_t", [128, 2]), ("qg_t", [128, 3])]
W_ODD = [("odd_w_in", [D, 4096]), ("odd_sgu_norm_g", [2048]), ("odd_sgu_norm_b", [2048]),
         ("odd_wsT", [8, 128, 128]), ("odd_b16", [2048]), ("odd_w_out", [2048, D])]
W_FFN = [("ffn_w_gate", [D, DFF]), ("ffn_w_up", [D, DFF]), ("ffn_w_down", [DFF, D])]
W_LN = [("mix_ln_g", [4, D]), ("mix_ln_b", [4, D]), ("ffn_ln_g", [4, D]), ("ffn_ln_b", [4, D])]
CONSTS = [("ident", [128, 128], BF16), ("cos2", [64, T], F32), ("sin2", [64, T], F32), ("cs1", [128, 256], BF16),
          ("t3", [128, 128 * 64], BF16), ("cdft", [128, 256], BF16)]

PROG_LAYERS = {"F1": ([0], [], []), "M": ([0], [], []), "A": ([0], [], []), "B": ([0, 2], [1], [0, 1]), "C": ([2], [3], [2, 3]),
               "F": ([0, 2], [1, 3], [0, 1, 2, 3])}


def build_program(kind):
    nc = bass.Bass("TRN2", target_bir_lowering=False)
    cx = Ctx(nc)
    if kind.startswith("P:"):
        PROG_LAYERS[kind] = ([0], [1], [0])
    ev, od, ff = PROG_LAYERS[kind]
    cx.lmap = {"even": {l: k for k, l in enumerate(ev)}, "odd": {l: k for k, l in enumerate(od)},
               "ffn": {l: k for k, l in enumerate(ff)}}
    gath = {}
    if kind in ("F", "F1"):
        for l in (0, 2):
            gbh = nc.dram_tensor("gb%d" % l, [GROWS, 1024], F32)
            gah = nc.dram_tensor("ga%d" % l, [NCORES * GROWS, 1024], F32)
            cx.dram["gb%d" % l] = gbh.ap().bitcast(BF16)
            cx.dram["ga%d" % l] = gah.ap().bitcast(BF16).rearrange("(r g) c -> r g c", r=NCORES)
            gath[l] = (gbh, gah)
    cx.dt("x", [T, D], F32, "ExternalInput")
    for n, shp, dt_ in CONSTS:
        cx.dt(n, shp, dt_, "ExternalInput")
    for n, shp in W_LN:
        cx.dt(n, shp, F32, "ExternalInput")
    for grp, lays in ((W_EVEN, ev), (W_ODD, od), (W_FFN, ff)):
        if lays:
            for n, shp in grp:
                cx.dt(n, [len(lays)] + shp, F32, "ExternalInput")
    sk = "ExternalOutput" if (kind in ("F", "F1") and SCRATCH_AS_OUTPUT) else "Internal"
    for n in ("xs0", "xs1"):
        cx.dt(n, [T, D], F32, sk)
    for n in ("xT0", "xT1"):
        cx.dt(n, [128, 8, T], BF16, sk)
    dk = "ExternalOutput" if kind == "M" else sk
    cx.dt("QSn", [8, 128, T], BF16, dk)
    cx.dt("QSr", [8, 64, T], BF16, dk)
    cx.dt("KS", [8, 128, SEQ], BF16, sk)
    cx.dt("VS", [8, 128, 128, 128], BF16, sk)
    cx.dt("mixT", [12, 128, T], BF16, dk)
    P = cx.P

    def main_even(layer, ga, xsrc, xTsrc, xdst, xTdst):
        phase_eq(cx, layer, xTsrc)
        phase_ekv(cx, layer, ga)
        phase_efft(cx, layer, ga, "mixT")
        phase_eattn(cx, layer, ga, "mixT")
        phase_eout(cx, layer, "mixT", xsrc, xdst, xTdst)

    if kind.startswith("P:"):
        cx.dt("ga", [NCORES, GROWS, 2048], BF16, "ExternalInput")
        cx.dt("gb", [GROWS, 2048], BF16, "ExternalOutput")
        for p in kind[2:].split(","):
            if p == "prep":
                phase_prep(cx, "x", "xT0")
            elif p == "eproj":
                phase_eproj(cx, 0, "xT0", "gb")
            elif p == "eq":
                phase_eq(cx, 0, "xT0")
            elif p == "ekv":
                phase_ekv(cx, 0, "ga")
            elif p == "efft":
                phase_efft(cx, 0, "ga", "mixT")
            elif p == "eattn":
                phase_eattn(cx, 0, "ga", "mixT")
            elif p == "eout":
                phase_eout(cx, 0, "mixT", "x", "xs1", "xT1")
            elif p == "ffn":
                phase_ffn(cx, 0, "x", "xT0", "xs0", "xT1")
            elif p == "odd":
                phase_odd(cx, 1, "x", "xT0", "xs1", "xT1")
    elif kind == "A":
        cx.dt("gb", [GROWS, 2048], BF16, "ExternalOutput")
        phase_prep(cx, "x", "xT0")
        phase_eproj(cx, 0, "xT0", "gb")
    elif kind == "B":
        cx.dt("ga", [NCORES, GROWS, 2048], BF16, "ExternalInput")
        cx.dt("gb", [GROWS, 2048], BF16, "ExternalOutput")
        cx.dt("xmid", [T, D], F32, "ExternalOutput")
        phase_prep(cx, "x", "xT0")
        main_even(0, "ga", "x", "xT0", "xs1", "xT1")
        phase_ffn(cx, 0, "xs1", "xT1", "xs0", "xT0")
        phase_odd(cx, 1, "xs0", "xT0", "xs1", "xT1")
        phase_ffn(cx, 1, "xs1", "xT1", "xmid", "xT0")
        phase_eproj(cx, 2, "xT0", "gb")
    elif kind == "M":
        cx.dt("ga", [NCORES, GROWS, 2048], BF16, "ExternalInput")
        cx.dt("xmid", [T, D], F32, "ExternalOutput")
        phase_prep(cx, "x", "xT0")
        main_even(0, "ga", "x", "xT0", "xmid", None)
    elif kind == "C":
        cx.dt("ga", [NCORES, GROWS, 2048], BF16, "ExternalInput")
        cx.dt("out", [T, D], F32, "ExternalOutput")
        phase_prep(cx, "x", "xT0")
        main_even(2, "ga", "x", "xT0", "xs1", "xT1")
        phase_ffn(cx, 2, "xs1", "xT1", "xs0", "xT0")
        phase_odd(cx, 3, "xs0", "xT0", "xs1", "xT1")
        phase_ffn(cx, 3, "xs1", "xT1", "out", None)
    elif kind in ("F", "F1"):
        cx.dt("out", [T, D], F32, "ExternalOutput")
        def all_gather(l):
            gbh, gah = gath[l]
            if DUMMY_CC:
                dgi = nc.dram_tensor("dgi%d" % l, [16, 128], F32)
                dgo = nc.dram_tensor("dgo%d" % l, [NCORES * 16, 128], F32)
                P.add("pool", lambda e: e.collective_compute("AllGather", ALU.bypass,
                                                             replica_groups=[list(range(NCORES))],
                                                             ins=[dgi.ap().opt()], outs=[dgo.ap().opt()]),
                      dma=True, inc=1)
                P.barrier()
            if NO_CC:
                for r in range(NCORES):
                    P.dma("pool", gah[r * GROWS:(r + 1) * GROWS, :], gbh[:, :])
                P.barrier()
                return
            P.add("pool", lambda e: e.collective_compute("AllGather", ALU.bypass,
                                                         replica_groups=[list(range(NCORES))],
                                                         ins=[gbh.ap().opt()], outs=[gah.ap().opt()]),
                  dma=True, inc=1)
            P.barrier()

        phase_prep(cx, "x", "xT0")
        phase_eproj(cx, 0, "xT0", "gb0")
        all_gather(0)
        if kind == "F1":
            main_even(0, "ga0", "x", "xT0", "out", None)
            P.emit()
            return nc, cx
        main_even(0, "ga0", "x", "xT0", "xs1", "xT1")
        phase_ffn(cx, 0, "xs1", "xT1", "xs0", "xT0")
        phase_odd(cx, 1, "xs0", "xT0", "xs1", "xT1")
        phase_ffn(cx, 1, "xs1", "xT1", "xs0", "xT0")
        phase_eproj(cx, 2, "xT0", "gb2")
        all_gather(2)
        main_even(2, "ga2", "xs0", "xT0", "xs1", "xT1")
        phase_ffn(cx, 2, "xs1", "xT1", "xs0", "xT0")
        phase_odd(cx, 3, "xs0", "xT0", "xs1", "xT1")
        phase_ffn(cx, 3, "xs1", "xT1", "out", None)
    else:
        raise ValueError(kind)
    P.emit()
    return nc, cx


def _bf(a):
    return np.asarray(a, dtype=np.float32).astype(ml_dtypes.bfloat16)


def _const_tables(core):
    inv = 1.0 / (10000.0 ** (np.arange(0, 64, 2, dtype=np.float64) / 64.0))
    pos = np.arange(core * T, (core + 1) * T, dtype=np.float64)
    ang = (pos[:, None].astype(np.float32) * inv[None, :].astype(np.float32)).astype(np.float32)
    c, s_ = np.cos(ang.astype(np.float64)), np.sin(ang.astype(np.float64))
    cos2 = np.concatenate([c.T, c.T], 0).astype(np.float32)
    sin2 = np.concatenate([-s_.T, s_.T], 0).astype(np.float32)
    n = np.arange(128, dtype=np.float64)
    th = 2 * np.pi * np.outer(n, n) / 128.0
    cs1 = np.concatenate([np.cos(th), -np.sin(th)], 1)
    norm = 1.0 / np.sqrt(SEQ * 128.0)
    cdft = np.concatenate([np.cos(th) * norm, np.sin(th) * norm], 1)
    s2 = np.arange(128, dtype=np.float64)[:, None, None]
    k1 = np.arange(128, dtype=np.float64)[None, :, None]
    k2 = (16 * core + np.arange(16, dtype=np.float64))[None, None, :]
    k = k1 + 128.0 * k2
    ph = 2 * np.pi * ((s2 * k) % SEQ) / SEQ
    mre, mim = np.cos(ph), -np.sin(ph)
    t3 = np.concatenate([mre, mim, -mim, mre], 2).reshape(128, 128 * 64)
    return {"ident": _bf(np.eye(128)), "cos2": np.ascontiguousarray(cos2), "sin2": np.ascontiguousarray(sin2),
            "cs1": _bf(cs1), "t3": _bf(t3), "cdft": _bf(cdft)}


def _weights(inp, kind):
    ev, od, ff = PROG_LAYERS[kind]
    w = {}
    for n in ("mix_ln_g", "mix_ln_b", "ffn_ln_g", "ffn_ln_b"):
        w[n] = np.ascontiguousarray(inp[n], dtype=np.float32)
    if ev:
        ei = [l // 2 for l in ev]
        for n in ("even_w_in", "even_w_uq", "even_w_uk", "even_w_uv", "even_w_out"):
            w[n] = np.ascontiguousarray(np.asarray(inp[n], dtype=np.float32)[ei])
        w["kvg_t"] = np.ascontiguousarray(np.asarray(inp["even_kv_norm"], dtype=np.float32)[ei].reshape(len(ei), 2, 128).transpose(0, 2, 1))
        w["qg_t"] = np.ascontiguousarray(np.asarray(inp["even_q_norm"], dtype=np.float32)[ei].reshape(len(ei), 3, 128).transpose(0, 2, 1))
    if od:
        oi = [l // 2 for l in od]
        for n in ("odd_w_in", "odd_sgu_norm_g", "odd_sgu_norm_b", "odd_w_out"):
            w[n] = np.ascontiguousarray(np.asarray(inp[n], dtype=np.float32)[oi])
        w["odd_wsT"] = np.ascontiguousarray(np.asarray(inp["odd_w_spatial"], dtype=np.float32)[oi].transpose(0, 1, 3, 2))
        w["odd_b16"] = np.ascontiguousarray(np.repeat(np.asarray(inp["odd_b_spatial"], dtype=np.float32)[oi], 2, axis=1).reshape(len(oi), 2048))
    if ff:
        for n in ("ffn_w_gate", "ffn_w_up", "ffn_w_down"):
            w[n] = np.ascontiguousarray(np.asarray(inp[n], dtype=np.float32)[ff])
    return w


_PROGS = {}


def _prog(kind):
    if kind not in _PROGS:
        _PROGS[kind] = build_program(kind)[0]
    return _PROGS[kind]


def _launch(kind, inp, xs, ga=None):
    w = _weights(inp, kind)
    maps = []
    for c in range(NCORES):
        m = dict(w)
        m.update(_const_tables(c))
        m["x"] = np.ascontiguousarray(xs[c], dtype=np.float32)
        if ga is not None:
            m["ga"] = ga
        maps.append(m)
    res = run_bass_kernel_spmd(_prog(kind), maps, core_ids=list(range(NCORES)))
    return res.results


FUSED = False


def kernel(**inp):
    x = np.asarray(inp["x"], dtype=np.float32).reshape(SEQ, D)
    xs = [x[c * T:(c + 1) * T] for c in range(NCORES)]
    if FUSED:
        r = _launch("F", inp, xs)
        out = np.concatenate([r[c]["out"] for c in range(NCORES)], 0)
        return out.reshape(1, SEQ, D).astype(np.float32)
    ra = _launch("A", inp, xs)
    ga = np.ascontiguousarray(np.stack([ra[c]["gb"] for c in range(NCORES)], 0))
    rb = _launch("B", inp, xs, ga)
    ga = np.ascontiguousarray(np.stack([rb[c]["gb"] for c in range(NCORES)], 0))
    xs = [rb[c]["xmid"] for c in range(NCORES)]
    rc = _launch("C", inp, xs, ga)
    out = np.concatenate([rc[c]["out"] for c in range(NCORES)], 0)
    return out.reshape(1, SEQ, D).astype(np.float32)
```

```python
import contextlib
import numpy as np
import ml_dtypes
import concourse.bass as bass
import concourse.mybir as mybir
from concourse.bass_utils import run_bass_kernel_spmd

F32 = mybir.dt.float32
BF16 = mybir.dt.bfloat16
AF = mybir.ActivationFunctionType
ALU = mybir.AluOpType

NCORES = 8
SEQ = 16384
T = SEQ // NCORES
D = 1024
DFF = 2816
NFC = DFF // 128
DEPTH = 4
ALPHA = float((2 * DEPTH) ** 0.25)
LN_EPS = 1e-5
RMS_EPS = 1e-6
GROWS = 832
SCALE = float(192 ** -0.5)
NO_CC = False
DUMMY_CC = False
SCRATCH_AS_OUTPUT = True

ENGS = ["pe", "act", "dve", "pool", "sp"]
SEM_EPOCH = 20000
NDMA_SEMS = {"sp": 4, "pool": 2, "act": 1, "pe": 1, "dve": 1}


class Res:
    __slots__ = ("name", "w", "r")

    def __init__(self, name=""):
        self.name = name
        self.w = None
        self.r = []


class Op:
    __slots__ = ("eng", "fn", "deps", "sig", "sem", "val", "dma", "inc")


class Prog:
    def __init__(self, nc):
        self.nc = nc
        self.ops = {e: [] for e in ENGS}
        self.n_dma = {e: 0 for e in ENGS}
        self.dma_last = {}
        self.all_res = []

    def res(self, name=""):
        r = Res(name)
        self.all_res.append(r)
        return r

    def add(self, eng, fn, reads=(), writes=(), dma=False, inc=None, extra=()):
        op = Op()
        op.inc = inc if inc is not None else (16 if dma else 1)
        op.eng = eng
        op.fn = fn
        op.dma = dma
        op.sig = dma
        op.sem = None
        op.val = 0
        deps = set(extra)
        for r in reads:
            if r.w is not None:
                deps.add(r.w)
        for w in writes:
            if w.w is not None:
                deps.add(w.w)
            deps.update(w.r)
        for r in reads:
            r.r.append(op)
        for w in writes:
            w.w = op
            w.r = []
        deps.discard(op)
        if dma:
            if op.inc == 16:
                k = (eng, self.n_dma[eng] % NDMA_SEMS[eng])
                self.n_dma[eng] += 1
            else:
                k = (eng, "cc")
            prev = self.dma_last.get(k)
            if prev is not None:
                deps.add(prev)
            self.dma_last[k] = op
            op.sem = k
        if eng == "pe" and not dma:
            deps = {d for d in deps if not (d.eng == "pe" and not d.dma)}
        op.deps = deps
        self.ops[eng].append(op)
        return op

    def pe(self, fn, reads=(), writes=()):
        return self.add("pe", fn, reads, writes)

    def act(self, fn, reads=(), writes=()):
        return self.add("act", fn, reads, writes)

    def dve(self, fn, reads=(), writes=()):
        return self.add("dve", fn, reads, writes)

    def pool(self, fn, reads=(), writes=()):
        return self.add("pool", fn, reads, writes)

    def dma(self, q, out, in_, reads=(), writes=()):
        return self.add(q, lambda e: e.dma_start(out=out, in_=in_), reads, writes, dma=True)

    def barrier(self):
        last = []
        for e in ENGS:
            for op in reversed(self.ops[e]):
                if not op.dma:
                    last.append(op)
                    break
        last.extend(self.dma_last.values())
        for r in self.all_res:
            r.w = None
            r.r = []
        for e in ENGS:
            self.add(e, None, extra=last)

    def emit(self, final_ops=()):
        nc = self.nc
        for e in ENGS:
            for op in self.ops[e]:
                for d in op.deps:
                    d.sig = True
        for op in final_ops:
            op.sig = True
        sems = {}

        def get_sem(key):
            if key not in sems:
                sems[key] = nc.alloc_semaphore("s_%s_%s" % key)
            return sems[key]

        dma_cnt = {}
        for e in ENGS:
            c = 0
            for op in self.ops[e]:
                if op.dma:
                    k = op.sem
                    dma_cnt[k] = dma_cnt.get(k, 0) + op.inc
                    op.sem = get_sem(("d" + k[0], k[1]))
                    op.val = dma_cnt[k]
                elif op.sig:
                    ep = c // SEM_EPOCH
                    c += 1
                    op.sem = get_sem((e, ep))
                    op.val = c - ep * SEM_EPOCH
        self.nsems = len(sems)
        nwaits = {e: 0 for e in ENGS}
        ninst = {e: 0 for e in ENGS}

        def run(e, eo):
            seen = {}
            for op in self.ops[e]:
                need = {}
                for d in op.deps:
                    k = id(d.sem)
                    if seen.get(k, 0) >= d.val:
                        continue
                    if k not in need or need[k][1] < d.val:
                        need[k] = (d.sem, d.val)
                for k, (sm, v) in need.items():
                    eo.wait_ge(sm, v)
                    seen[k] = v
                    nwaits[e] += 1
                if op.fn is None:
                    if op.sig:
                        eo.nop().then_inc(op.sem, op.inc)
                    continue
                ins = op.fn(eo)
                ninst[e] += 1
                if op.sig:
                    ins.then_inc(op.sem, op.inc)
            if e == "sp":
                for op in final_ops:
                    if seen.get(id(op.sem), 0) < op.val:
                        eo.wait_ge(op.sem, op.val)
                        seen[id(op.sem)] = op.val

        with nc.Block() as block:
            @block.tensor
            def _(eo):
                run("pe", eo)

            @block.scalar
            def _(eo):
                run("act", eo)

            @block.vector
            def _(eo):
                run("dve", eo)

            @block.gpsimd
            def _(eo):
                run("pool", eo)

            @block.sync
            def _(eo):
                run("sp", eo)
        self.nwaits = nwaits
        self.ninst = ninst


def v3(t, a, b):
    return t[:, :].rearrange("p (a b) -> p a b", a=a, b=b)


class Ctx:
    def __init__(self, nc):
        self.nc = nc
        self.P = Prog(nc)
        self.dram = {}
        self.dres = {}
        self.lmap = {}

    def dt(self, name, shape, dtype, kind="Internal"):
        t = self.nc.dram_tensor(name, list(shape), dtype, kind=kind)
        self.dram[name] = t
        self.dres[name] = self.P.res(name)
        return t


class Phase:
    def __init__(self, cx, name):
        self.cx = cx
        self.name = name
        self.st = contextlib.ExitStack()
        self.n = 0

    def __enter__(self):
        self.st.__enter__()
        return self

    def __exit__(self, *a):
        self.cx.P.barrier()
        return self.st.__exit__(*a)

    def sb(self, cols, dtype, parts=128):
        self.n += 1
        t = self.st.enter_context(self.cx.nc.sbuf_tensor("%s_sb%d" % (self.name, self.n), [parts, cols], dtype))
        return t, self.cx.P.res()

    def ps(self, cols, dtype=F32):
        self.n += 1
        t = self.st.enter_context(self.cx.nc.psum_tensor("%s_ps%d" % (self.name, self.n), [128, cols], dtype))
        return t, self.cx.P.res()


class Rot:
    def __init__(self, items):
        self.items = items
        self.i = 0

    def next(self):
        it = self.items[self.i % len(self.items)]
        self.i += 1
        return it


def load_cast(cx, stg, dst3, dres, src3, A, B, eng="pool", scols=2048):
    cx.P.dma("pool", dst3, src3, writes=[dres])


class LNState:
    def __init__(self, cx, ph, g_ap, b_ap, ident, ident_r, pT, pT_r, nbuf=2):
        P = cx.P
        self.cx = cx
        self.g, self.gr = ph.sb(D, F32)
        self.b, self.br = ph.sb(D, F32)
        P.dma("sp", self.g[:, :], g_ap.partition_broadcast(128), writes=[self.gr])
        P.dma("sp", self.b[:, :], b_ap.partition_broadcast(128), writes=[self.br])
        self.xt = Rot([ph.sb(D, F32) for _ in range(nbuf)])
        self.zt = Rot([ph.sb(D, F32) for _ in range(nbuf)])
        self.xb = Rot([ph.sb(D, BF16) for _ in range(nbuf)])
        self.xT = Rot([ph.sb(D, BF16) for _ in range(nbuf)])
        self.st = Rot([ph.sb(16, F32) for _ in range(2)])
        self.mh, self.mhr = ph.sb(1, F32)
        P.pool(lambda e: e.memset(self.mh[:, :], -0.5), [], [self.mhr])
        self.ident, self.ident_r = ident, ident_r
        self.pT, self.pT_r = pT, pT_r

    def prefetch(self, xsrc, ti):
        cx = self.cx
        xt, xtr = self.xt.next()
        cx.P.dma("sp", xt[:, :], cx.dram[xsrc][ti * 128:(ti + 1) * 128, :], writes=[xtr])
        return xt, xtr

    def tile(self, ypsum, yres, ti, xdst, xpre, xTdst):
        cx = self.cx
        P = cx.P
        self.flush()
        xt, xtr = xpre
        zt, ztr = self.zt.next()
        xb, xbr = self.xb.next()
        xT, xTr = self.xT.next()
        st, str_ = self.st.next()
        P.dve(lambda e: e.scalar_tensor_tensor(out=zt[:, :], in0=xt[:, :], scalar=ALPHA, in1=ypsum,
                                               op0=ALU.mult, op1=ALU.add), [xtr, yres], [ztr])
        for c in range(2):
            P.dve(lambda e, c=c: e.bn_stats(out=st[:, c * 6:(c + 1) * 6], in_=zt[:, c * 512:(c + 1) * 512]),
                  [ztr], [str_])
        P.dve(lambda e: e.bn_aggr(out=st[:, 12:14], in_=st[:, 0:12]), [str_], [str_])
        P.dve(lambda e: e.tensor_scalar(out=st[:, 14:15], in0=st[:, 13:14], scalar1=LN_EPS, scalar2=None,
                                        op0=ALU.add), [str_], [str_])
        P.pool(lambda e: e.tensor_tensor(out=st[:, 14:15], in0=st[:, 14:15], in1=self.mh[:, :], op=ALU.pow),
               [str_, self.mhr], [str_])
        P.dve(lambda e: e.tensor_scalar(out=zt[:, :], in0=zt[:, :], scalar1=st[:, 12:13], scalar2=st[:, 14:15],
                                        op0=ALU.subtract, op1=ALU.mult), [str_, ztr], [ztr])
        P.pool(lambda e: e.tensor_tensor(out=zt[:, :], in0=zt[:, :], in1=self.g[:, :], op=ALU.mult),
               [ztr, self.gr], [ztr])
        P.pool(lambda e: e.tensor_tensor(out=zt[:, :], in0=zt[:, :], in1=self.b[:, :], op=ALU.add),
               [ztr, self.br], [ztr])
        P.act(lambda e: e.activation(out=xb[:, :], in_=zt[:, :], func=AF.Copy), [ztr], [xbr])
        o1 = P.dma("pool", cx.dram[xdst][ti * 128:(ti + 1) * 128, :], zt[:, :], reads=[ztr], writes=[])
        if xTdst is not None:
            self.pending = (xb, xbr, xT, xTr, ti, xTdst)
        return o1

    pending = None

    def flush(self):
        if self.pending is None:
            return
        cx = self.cx
        P = cx.P
        xb, xbr, xT, xTr, ti, xTdst = self.pending
        self.pending = None
        for c in range(8):
            P.pe(lambda e, c=c: e.transpose(out=self.pT[:, c * 128:(c + 1) * 128], in_=xb[:, c * 128:(c + 1) * 128],
                                            identity=self.ident[:, :]), [xbr, self.ident_r], [self.pT_r])
        P.dve(lambda e: e.tensor_copy(out=xT[:, :], in_=self.pT[:, :]), [self.pT_r], [xTr])
        P.dma("pool", cx.dram[xTdst][:, :, ti * 128:(ti + 1) * 128], v3(xT, 8, 128), reads=[xTr], writes=[])


def load_ident(cx, ph):
    idt, idr = ph.sb(128, BF16)
    cx.P.dma("sp", idt[:, :], cx.dram["ident"][:, :], writes=[idr])
    return idt, idr


def phase_prep(cx, xsrc, xTdst):
    P = cx.P
    with Phase(cx, "prep") as ph:
        idt, idr = load_ident(cx, ph)
        pT, pTr = ph.ps(D, BF16)
        xt = Rot([ph.sb(D, F32) for _ in range(2)])
        xb = Rot([ph.sb(D, BF16) for _ in range(2)])
        xT = Rot([ph.sb(D, BF16) for _ in range(2)])
        for ti in range(T // 128):
            a, ar = xt.next()
            b, br = xb.next()
            c_, cr = xT.next()
            P.dma("sp", a[:, :], cx.dram[xsrc][ti * 128:(ti + 1) * 128, :], writes=[ar])
            P.act(lambda e, a=a, b=b: e.activation(out=b[:, :], in_=a[:, :], func=AF.Copy), [ar], [br])
            for c in range(8):
                P.pe(lambda e, c=c, b=b: e.transpose(out=pT[:, c * 128:(c + 1) * 128], in_=b[:, c * 128:(c + 1) * 128],
                                                    identity=idt[:, :]), [br, idr], [pTr])
            P.dve(lambda e, c_=c_: e.tensor_copy(out=c_[:, :], in_=pT[:, :]), [pTr], [cr])
            P.dma("pool", cx.dram[xTdst][:, :, ti * 128:(ti + 1) * 128], v3(c_, 8, 128), reads=[cr], writes=[])


def phase_ffn(cx, layer, xsrc, xTsrc, xdst, xTdst):
    P = cx.P
    last = None
    fl = cx.lmap["ffn"][layer]
    with Phase(cx, "ffn%d" % layer) as ph:
        idt, idr = load_ident(cx, ph)
        pT, pTr = ph.ps(D, BF16)
        ln = LNState(cx, ph, cx.dram["ffn_ln_g"][layer:layer + 1, :], cx.dram["ffn_ln_b"][layer:layer + 1, :],
                     idt, idr, pT, pTr)
        xTs, xTr = ph.sb(8 * T, BF16)
        xT3 = v3(xTs, 8, T)
        xTres = [P.res() for _ in range(4)]
        for q in range(4):
            P.dma("sp", xT3[:, :, q * 512:(q + 1) * 512], cx.dram[xTsrc][:, :, q * 512:(q + 1) * 512],
                  writes=[xTres[q]])
        stg = None
        wd, wdr = ph.sb(NFC * D, BF16)
        wd3 = v3(wd, NFC, D)
        wg = Rot([ph.sb(8 * 256, BF16) for _ in range(2)])
        wu = Rot([ph.sb(8 * 256, BF16) for _ in range(2)])
        hT, _ = ph.sb(NFC * 1024, BF16)
        hT3 = v3(hT, NFC, 1024)
        hres = [[P.res() for _ in range(2)] for _ in range(NFC)]
        sg = Rot([ph.sb(512, F32) for _ in range(2)])
        pA = Rot([ph.ps(D) for _ in range(3)])
        wgate = cx.dram["ffn_w_gate"][fl].rearrange("(kc p) f -> p kc f", p=128)
        wup = cx.dram["ffn_w_up"][fl].rearrange("(kc p) f -> p kc f", p=128)
        wdown = cx.dram["ffn_w_down"][fl].rearrange("(fc p) d -> p fc d", p=128)
        def load_fg(hh, fg):
            g_t, g_r = wg.next()
            u_t, u_r = wu.next()
            load_cast(cx, stg, v3(g_t, 8, 256), g_r, wgate[:, :, fg * 256:(fg + 1) * 256], 8, 256)
            load_cast(cx, stg, v3(u_t, 8, 256), u_r, wup[:, :, fg * 256:(fg + 1) * 256], 8, 256)
            if hh == 0:
                load_cast(cx, stg, wd3[:, 2 * fg:2 * fg + 2, :], wdr, wdown[:, 2 * fg:2 * fg + 2, :], 2, D)
            return g_t, g_r, u_t, u_r

        seq = [(hh, fg) for hh in range(2) for fg in range(NFC // 2)]
        pending = load_fg(*seq[0])
        for si, (hh, fg) in enumerate(seq):
            g_t, g_r, u_t, u_r = pending
            if si + 1 < len(seq):
                pending = load_fg(*seq[si + 1])
            g3 = v3(g_t, 8, 256)
            u3 = v3(u_t, 8, 256)
            for fc in range(2):
                fcg = fg * 2 + fc
                for tt in range(2):
                    q = hh * 2 + tt
                    ab_, ar = pA.next()
                    br = ar
                    a = ab_[:, 0:512]
                    b = ab_[:, 512:1024]
                    for kc in range(8):
                        P.pe(lambda e, kc=kc, a=a, g3=g3, fc=fc, q=q: e.matmul(
                            a, lhsT=g3[:, kc, fc * 128:(fc + 1) * 128],
                            rhs=xT3[:, kc, q * 512:(q + 1) * 512], start=(kc == 0), stop=(kc == 7)),
                            [g_r, xTres[q]], [ar])
                    for kc in range(8):
                        P.pe(lambda e, kc=kc, b=b, u3=u3, fc=fc, q=q: e.matmul(
                            b, lhsT=u3[:, kc, fc * 128:(fc + 1) * 128],
                            rhs=xT3[:, kc, q * 512:(q + 1) * 512], start=(kc == 0), stop=(kc == 7)),
                            [u_r, xTres[q]], [br])
                    s, sr = sg.next()
                    P.act(lambda e, s=s, a=a: e.activation(out=s[:, :], in_=a, func=AF.Silu), [ar], [sr])
                    P.dve(lambda e, s=s, b=b, fcg=fcg, tt=tt: e.tensor_tensor(
                        out=hT3[:, fcg, tt * 512:(tt + 1) * 512], in0=s[:, :], in1=b, op=ALU.mult),
                        [sr, br], [hres[fcg][tt]])
            if fg != NFC // 2 - 1:
                continue
            for tl in range(8):
                ti = hh * 8 + tl
                xpre = ln.prefetch(xsrc, ti)
                py, pyr = pA.next()
                for half in range(2):
                    for fcg in range(NFC):
                        P.pe(lambda e, fcg=fcg, tl=tl, half=half, py=py: e.matmul(
                            py[:, half * 512:(half + 1) * 512], lhsT=hT3[:, fcg, tl * 128:(tl + 1) * 128],
                            rhs=wd3[:, fcg, half * 512:(half + 1) * 512], start=(fcg == 0), stop=(fcg == NFC - 1)),
                            [hres[fcg][tl // 4], wdr], [pyr])
                last = ln.tile(py[:, :], pyr, ti, xdst, xpre, xTdst)
        ln.flush()
    return last


def phase_odd(cx, layer, xsrc, xTsrc, xdst, xTdst):
    P = cx.P
    i = cx.lmap["odd"][layer]
    last = None
    with Phase(cx, "odd%d" % layer) as ph:
        idt, idr = load_ident(cx, ph)
        pT, pTr = ph.ps(D, BF16)
        ln = LNState(cx, ph, cx.dram["mix_ln_g"][layer:layer + 1, :], cx.dram["mix_ln_b"][layer:layer + 1, :],
                     idt, idr, pT, pTr, nbuf=1)
        stg = None
        wu, wur = ph.sb(8 * 2048, BF16)
        wv, wvr = ph.sb(8 * 2048, BF16)
        wo, wor = ph.sb(16 * D, BF16)
        wst, wsr = ph.sb(8 * 128, BF16)
        wu3, wv3, wo3, ws3 = v3(wu, 8, 2048), v3(wv, 8, 2048), v3(wo, 16, D), v3(wst, 8, 128)
        win = cx.dram["odd_w_in"][i].rearrange("(kc p) f -> p kc f", p=128)
        wurs = [P.res() for _ in range(4)]
        wvrs = [P.res() for _ in range(4)]
        wors = [P.res() for _ in range(2)]
        for n in range(4):
            load_cast(cx, stg, wu3[:, :, n * 512:(n + 1) * 512], wurs[n], win[:, :, n * 512:(n + 1) * 512], 8, 512, scols=1024)
        for n in range(4):
            load_cast(cx, stg, wv3[:, :, n * 512:(n + 1) * 512], wvrs[n], win[:, :, 2048 + n * 512:2048 + (n + 1) * 512],
                      8, 512, scols=1024)
        load_cast(cx, stg, ws3, wsr, cx.dram["odd_wsT"][i].rearrange("g q p -> q g p"), 8, 128, scols=1024)
        wod = cx.dram["odd_w_out"][i].rearrange("(cc p) d -> p cc d", p=128)
        for hf in range(2):
            load_cast(cx, stg, wo3[:, :, hf * 512:(hf + 1) * 512], wors[hf], wod[:, :, hf * 512:(hf + 1) * 512], 16, 512)
        sg, sgr = ph.sb(2048, F32)
        sb_, sbr = ph.sb(2048, F32)
        bt, btr = ph.sb(2048, F32)
        P.dma("sp", sg[:, :], cx.dram["odd_sgu_norm_g"][i:i + 1, :].partition_broadcast(128), writes=[sgr])
        P.dma("sp", sb_[:, :], cx.dram["odd_sgu_norm_b"][i:i + 1, :].partition_broadcast(128), writes=[sbr])
        P.dma("sp", bt[:, :], cx.dram["odd_b16"][i:i + 1, :].partition_broadcast(128), writes=[btr])
        xTt = Rot([ph.sb(8 * 512, BF16) for _ in range(2)])
        uT, _ = ph.sb(16 * 512, BF16)
        uT3 = v3(uT, 16, 512)
        ures = [P.res() for _ in range(16)]
        vbs = Rot([ph.sb(2048, BF16) for _ in range(2)])
        t32, t32r = ph.sb(2048, F32)
        vn, vnr = ph.sb(2048, BF16)
        mT, mTr = ph.sb(2048, BF16)
        mT3 = v3(mT, 16, 128)
        st, str_ = ph.sb(32, F32)
        pA = Rot([ph.ps(1024) for _ in range(3)])
        pS = Rot([ph.ps(512) for _ in range(1)])
        state = {}

        def emit_u(stile):
            xt_, xtr_ = xTt.next()
            x3 = v3(xt_, 8, 512)
            P.dma("sp", x3, cx.dram[xTsrc][:, :, stile * 512:(stile + 1) * 512], writes=[xtr_])
            state["x"] = (x3, xtr_)
            for cc in range(16):
                pb, pbr = pA.next()
                pu = pb[:, 0:512]
                for kc in range(8):
                    P.pe(lambda e, kc=kc, cc=cc, x3=x3, pu=pu: e.matmul(pu, lhsT=wu3[:, kc, cc * 128:(cc + 1) * 128],
                                                                   rhs=x3[:, kc, :], start=(kc == 0), stop=(kc == 7)),
                         [wurs[cc // 4], xtr_], [pbr])
                P.act(lambda e, cc=cc, pu=pu: e.activation(out=uT3[:, cc, :], in_=pu, func=AF.Gelu), [pbr], [ures[cc]])

        def emit_v(ti):
            x3, xtr_ = state["x"]
            tl = ti % 4
            vb, vbr = vbs.next()
            for vh in range(2):
                pv, pvr = pA.next()
                for n in range(2):
                    for kc in range(8):
                        P.pe(lambda e, kc=kc, n=n, vh=vh, pv=pv, x3=x3, tl=tl: e.matmul(
                            pv[:, n * 512:(n + 1) * 512], lhsT=x3[:, kc, tl * 128:(tl + 1) * 128],
                            rhs=wv3[:, kc, (vh * 2 + n) * 512:(vh * 2 + n + 1) * 512],
                            start=(kc == 0), stop=(kc == 7)), [wvrs[vh * 2 + n], xtr_], [pvr])
                P.act(lambda e, pv=pv, vh=vh, vb=vb: e.activation(out=vb[:, vh * 1024:(vh + 1) * 1024], in_=pv[:, :],
                                                                  func=AF.Gelu), [pvr], [vbr])
            state[ti] = (vb, vbr)

        def emit_rest(ti):
            vb, vbr = state.pop(ti)
            tl = ti % 4
            xpre = ln.prefetch(xsrc, ti)
            for c in range(4):
                P.dve(lambda e, c=c: e.bn_stats(out=st[:, c * 6:(c + 1) * 6], in_=vb[:, c * 512:(c + 1) * 512]),
                      [vbr], [str_])
            P.dve(lambda e: e.bn_aggr(out=st[:, 24:26], in_=st[:, 0:24]), [str_], [str_])
            P.dve(lambda e: e.tensor_scalar(out=st[:, 26:27], in0=st[:, 25:26], scalar1=LN_EPS, scalar2=None,
                                            op0=ALU.add), [str_], [str_])
            P.pool(lambda e: e.tensor_tensor(out=st[:, 26:27], in0=st[:, 26:27], in1=ln.mh[:, :], op=ALU.pow),
                   [str_, ln.mhr], [str_])
            P.dve(lambda e: e.scalar_tensor_tensor(out=t32[:, :], in0=vb[:, :], scalar=st[:, 24:25], in1=sg[:, :],
                                                   op0=ALU.subtract, op1=ALU.mult), [vbr, str_, sgr], [t32r])
            P.dve(lambda e: e.scalar_tensor_tensor(out=vn[:, :], in0=t32[:, :], scalar=st[:, 26:27], in1=sb_[:, :],
                                                   op0=ALU.mult, op1=ALU.add), [t32r, str_, sbr], [vnr])
            for sq in range(4):
                ps_, psr = pS.next()
                for c4 in range(4):
                    cc = sq * 4 + c4
                    P.pe(lambda e, cc=cc, c4=c4, ps_=ps_: e.matmul(ps_[:, c4 * 128:(c4 + 1) * 128],
                                                                    lhsT=vn[:, cc * 128:(cc + 1) * 128],
                                                                    rhs=ws3[:, cc // 2, :], start=True, stop=True),
                         [vnr, wsr], [psr])
                P.dve(lambda e, sq=sq, ps_=ps_: e.tensor_tensor(out=t32[:, sq * 512:(sq + 1) * 512], in0=ps_[:, :],
                                                                in1=bt[:, sq * 512:(sq + 1) * 512], op=ALU.add),
                      [psr, btr], [t32r])
            P.pool(lambda e, tl=tl: e.tensor_tensor(out=mT3, in0=v3(t32, 16, 128),
                                                    in1=uT3[:, :, tl * 128:(tl + 1) * 128], op=ALU.mult),
                   [t32r] + ures, [mTr])
            py, pyr = pA.next()
            for half in range(2):
                for cc in range(16):
                    P.pe(lambda e, cc=cc, half=half, py=py: e.matmul(
                        py[:, half * 512:(half + 1) * 512], lhsT=mT3[:, cc, :],
                        rhs=wo3[:, cc, half * 512:(half + 1) * 512], start=(cc == 0), stop=(cc == 15)),
                        [mTr, wors[half]], [pyr])
            return ln.tile(py[:, :], pyr, ti, xdst, xpre, xTdst)

        NTT = T // 128
        emit_u(0)
        emit_v(0)
        for ti in range(NTT):
            if ti + 1 < NTT and (ti + 1) % 4 != 0:
                emit_v(ti + 1)
            last = emit_rest(ti)
            if ti + 1 < NTT and (ti + 1) % 4 == 0:
                emit_u((ti + 1) // 4)
                emit_v(ti + 1)
        ln.flush()
    return last


def rms_chunks(cx, ph, pacc, paccr, pss, pssr, ones, onesr, wT3, wr, col0, nch, x3, xr, tcols, gcol, gr,
               craw, crawr, csq, csqr, rb, rbr, out3, outr, dim):
    P = cx.P
    for c in range(nch):
        pa, par = pacc.next()
        for kc in range(8):
            P.pe(lambda e, kc=kc, c=c, pa=pa: e.matmul(pa[:, :], lhsT=wT3[:, kc, col0 + c * 128:col0 + (c + 1) * 128],
                                                        rhs=x3[:, kc, tcols], start=(kc == 0), stop=(kc == 7)),
                 [wr, xr], [par])
        P.act(lambda e, c=c, pa=pa: e.activation(out=craw[:, c * 512:(c + 1) * 512], in_=pa[:, :], func=AF.Copy),
              [par], [crawr])
        P.act(lambda e, c=c, pa=pa: e.activation(out=csq[:, c * 512:(c + 1) * 512], in_=pa[:, :], func=AF.Square),
              [par], [csqr])
    for c in range(nch):
        P.pe(lambda e, c=c: e.matmul(pss[:, :], lhsT=ones[:, :], rhs=csq[:, c * 512:(c + 1) * 512],
                                     start=(c == 0), stop=(c == nch - 1)), [onesr, csqr], [pssr])
    P.dve(lambda e: e.tensor_scalar(out=rb[:, :], in0=pss[:, :], scalar1=1.0 / dim, scalar2=RMS_EPS,
                                    op0=ALU.mult, op1=ALU.add), [pssr], [rbr])
    P.act(lambda e: e.activation(out=rb[:, :], in_=rb[:, :], func=AF.Sqrt), [rbr], [rbr])
    P.dve(lambda e: e.reciprocal(out=rb[:, :], in_=rb[:, :]), [rbr], [rbr])
    for c in range(nch):
        P.dve(lambda e, c=c: e.scalar_tensor_tensor(out=out3[:, c, :], in0=craw[:, c * 512:(c + 1) * 512],
                                                    scalar=gcol[:, c:c + 1], in1=rb[:, :],
                                                    op0=ALU.mult, op1=ALU.mult), [crawr, gr, rbr], [outr])


def rotary(cx, p0, p0r, p1, p1r, cos, sin, csr, tcols, t1, t1r, t2, t2r, out, outr):
    P = cx.P
    P.dve(lambda e: e.tensor_tensor(out=t1[0:64, :], in0=p0[0:64, :], in1=cos[0:64, tcols], op=ALU.mult),
          [p0r, csr], [t1r])
    P.dve(lambda e: e.tensor_tensor(out=t2[0:64, :], in0=p1[0:64, :], in1=sin[0:64, tcols], op=ALU.mult),
          [p1r, csr], [t2r])
    P.pool(lambda e: e.tensor_tensor(out=out, in0=t1[0:64, :], in1=t2[0:64, :], op=ALU.add), [t1r, t2r], [outr])


def phase_eproj(cx, layer, xTsrc, gb):
    P = cx.P
    i = cx.lmap["even"][layer]
    GB = cx.dram[gb]
    with Phase(cx, "eproj%d" % layer) as ph:
        stg = None
        xTs, xr = ph.sb(8 * T, BF16)
        x3 = v3(xTs, 8, T)
        P.dma("sp", x3, cx.dram[xTsrc][:, :, :], writes=[xr])
        win = cx.dram["even_w_in"][i].rearrange("(kc p) f -> p kc f", p=128)
        wkv, wkvr = ph.sb(8 * 256, BF16)
        wkr, wkrr = ph.sb(8 * 256, BF16)
        wf, wfr = ph.sb(8 * 512, BF16)
        wkv3, wkr3, wf3 = v3(wkv, 8, 256), v3(wkr, 8, 256), v3(wf, 8, 512)
        P.pool(lambda e: e.memset(wkr[:, :], 0.0), [], [wkrr])
        load_cast(cx, stg, wkv3, wkvr, win[:, :, 384:640], 8, 256)
        load_cast(cx, stg, wkr3[:, :, 0:64], wkrr, win[:, :, 640:704], 8, 64)
        load_cast(cx, stg, wkr3[:, :, 128:160], wkrr, win[:, :, 672:704], 8, 32)
        load_cast(cx, stg, wkr3[:, :, 160:192], wkrr, win[:, :, 640:672], 8, 32)
        load_cast(cx, stg, wf3, wfr, win[:, :, 704:1216], 8, 512)
        ones, onesr = ph.sb(128, BF16)
        P.pool(lambda e: e.memset(ones[:, :], 1.0), [], [onesr])
        gcol, gr = ph.sb(2, F32)
        P.dma("sp", gcol[:, :], cx.dram["kvg_t"][i], writes=[gr])
        cos, csr = ph.sb(T, F32)
        sin, _ = ph.sb(T, F32)
        P.dma("sp", cos[0:64, :], cx.dram["cos2"][:, :], writes=[csr])
        P.dma("sp", sin[0:64, :], cx.dram["sin2"][:, :], writes=[csr])
        craw, crawr = ph.sb(2 * 512, F32)
        csq, csqr = ph.sb(2 * 512, BF16)
        rb, rbr = ph.sb(512, F32)
        cn = Rot([ph.sb(2 * 512, BF16) for _ in range(2)])
        t1, t1r = ph.sb(512, F32)
        t2, t2r = ph.sb(512, F32)
        kro = Rot([ph.sb(512, BF16) for _ in range(2)])
        fo = Rot([ph.sb(512, BF16) for _ in range(2)])
        pacc = Rot([ph.ps(512) for _ in range(3)])
        pss, pssr = ph.ps(512)
        pk = [ph.ps(512) for _ in range(2)]
        pf = Rot([ph.ps(512) for _ in range(2)])
        fview = GB[0:512, :].rearrange("r (a c) -> (r a) c", c=512)
        for tt in range(4):
            tcols = slice(tt * 512, (tt + 1) * 512)
            o, orr = cn.next()
            o3 = v3(o, 2, 512)
            rms_chunks(cx, ph, pacc, None, pss, pssr, ones, onesr, wkv3, wkvr, 0, 2, x3, xr, tcols, gcol, gr,
                       craw, crawr, csq, csqr, rb, rbr, o3, orr, 256.0)
            P.dma("pool", GB[512:768, tcols].rearrange("(c p) t -> p c t", p=128), o3, reads=[orr], writes=[])
            for j in range(2):
                for kc in range(8):
                    P.pe(lambda e, kc=kc, j=j, tcols=tcols: e.matmul(pk[j][0][:, :], lhsT=wkr3[:, kc, j * 128:(j + 1) * 128],
                                                                     rhs=x3[:, kc, tcols], start=(kc == 0), stop=(kc == 7)),
                         [wkrr, xr], [pk[j][1]])
            ko, kor = kro.next()
            rotary(cx, pk[0][0], pk[0][1], pk[1][0], pk[1][1], cos, sin, csr, tcols, t1, t1r, t2, t2r, ko[0:64, :], kor)
            P.dma("pool", GB[768:832, tcols], ko[0:64, :], reads=[kor], writes=[])
            for tl in range(4):
                t0 = tt * 512 + tl * 128
                p, pr = pf.next()
                for kc in range(8):
                    P.pe(lambda e, kc=kc, p=p, t0=t0: e.matmul(p[:, :], lhsT=x3[:, kc, t0:t0 + 128], rhs=wf3[:, kc, :],
                                                              start=(kc == 0), stop=(kc == 7)), [wfr, xr], [pr])
                f_, fr = fo.next()
                P.act(lambda e, p=p, f_=f_: e.activation(out=f_[:, :], in_=p[:, :], func=AF.Copy), [pr], [fr])
                P.dma("pool", fview[t0:t0 + 128, :], f_[:, :], reads=[fr], writes=[])


def phase_eq(cx, layer, xTsrc):
    P = cx.P
    i = cx.lmap["even"][layer]
    with Phase(cx, "eq%d" % layer) as ph:
        stg = None
        xTs, xr = ph.sb(8 * T, BF16)
        x3 = v3(xTs, 8, T)
        P.dma("sp", x3, cx.dram[xTsrc][:, :, :], writes=[xr])
        win = cx.dram["even_w_in"][i].rearrange("(kc p) f -> p kc f", p=128)
        wuq_d = cx.dram["even_w_uq"][i].rearrange("(kc p) f -> p kc f", p=128)
        wq1, wq1r = ph.sb(8 * 384, BF16)
        wq13 = v3(wq1, 8, 384)
        load_cast(cx, stg, wq13, wq1r, win[:, :, 0:384], 8, 384)
        wuq, wuqr = ph.sb(3 * 1536, BF16)
        wuq3 = v3(wuq, 3, 1536)
        for n in range(3):
            load_cast(cx, stg, wuq3[:, :, n * 512:(n + 1) * 512], wuqr, wuq_d[:, :, n * 512:(n + 1) * 512], 3, 512)
        wsw, wswr = ph.sb(3 * 512, BF16)
        wsw3 = v3(wsw, 3, 512)
        wswrs = [P.res() for _ in range(8)]
        for h in range(8):
            b = h * 192 + 128
            load_cast(cx, stg, wsw3[:, :, h * 64:h * 64 + 32], wswrs[h], wuq_d[:, :, b + 32:b + 64], 3, 32)
            load_cast(cx, stg, wsw3[:, :, h * 64 + 32:h * 64 + 64], wswrs[h], wuq_d[:, :, b:b + 32], 3, 32)
        ones, onesr = ph.sb(128, BF16)
        P.pool(lambda e: e.memset(ones[:, :], 1.0), [], [onesr])
        gcol, gr = ph.sb(3, F32)
        P.dma("sp", gcol[:, :], cx.dram["qg_t"][i], writes=[gr])
        cos, csr = ph.sb(T, F32)
        sin, _ = ph.sb(T, F32)
        P.dma("sp", cos[0:64, :], cx.dram["cos2"][:, :], writes=[csr])
        P.dma("sp", sin[0:64, :], cx.dram["sin2"][:, :], writes=[csr])
        craw, crawr = ph.sb(3 * 512, F32)
        csq, csqr = ph.sb(3 * 512, BF16)
        rb, rbr = ph.sb(512, F32)
        cqn, cqnr = ph.sb(3 * 512, BF16)
        cqn3 = v3(cqn, 3, 512)
        t1, t1r = ph.sb(512, F32)
        t2, t2r = ph.sb(512, F32)
        qno = Rot([ph.sb(512, BF16) for _ in range(2)])
        qro = Rot([ph.sb(512, BF16) for _ in range(2)])
        pacc = Rot([ph.ps(512) for _ in range(2)])
        pss, pssr = ph.ps(512)
        pn = Rot([ph.ps(512) for _ in range(2)])
        pk = [ph.ps(512) for _ in range(2)]
        for tt in range(4):
            tcols = slice(tt * 512, (tt + 1) * 512)
            rms_chunks(cx, ph, pacc, None, pss, pssr, ones, onesr, wq13, wq1r, 0, 3, x3, xr, tcols, gcol, gr,
                       craw, crawr, csq, csqr, rb, rbr, cqn3, cqnr, 384.0)
            for h in range(8):
                p, pr = pn.next()
                for kc in range(3):
                    P.pe(lambda e, kc=kc, p=p, h=h: e.matmul(p[:, :], lhsT=wuq3[:, kc, h * 192:h * 192 + 128],
                                                            rhs=cqn3[:, kc, :], start=(kc == 0), stop=(kc == 2)),
                         [wuqr, cqnr], [pr])
                qn, qnr = qno.next()
                P.act(lambda e, p=p, qn=qn: e.activation(out=qn[:, :], in_=p[:, :], func=AF.Copy), [pr], [qnr])
                P.dma("pool", cx.dram["QSn"][h, :, tcols], qn[:, :], reads=[qnr], writes=[])
                for kc in range(3):
                    P.pe(lambda e, kc=kc, h=h: e.matmul(pk[0][0][0:64, :], lhsT=wuq3[:, kc, h * 192 + 128:h * 192 + 192],
                                                        rhs=cqn3[:, kc, :], start=(kc == 0), stop=(kc == 2)),
                         [wuqr, cqnr], [pk[0][1]])
                for kc in range(3):
                    P.pe(lambda e, kc=kc, h=h: e.matmul(pk[1][0][0:64, :], lhsT=wsw3[:, kc, h * 64:(h + 1) * 64],
                                                        rhs=cqn3[:, kc, :], start=(kc == 0), stop=(kc == 2)),
                         [wswrs[h], cqnr], [pk[1][1]])
                qr_, qrr = qro.next()
                rotary(cx, pk[0][0], pk[0][1], pk[1][0], pk[1][1], cos, sin, csr, tcols, t1, t1r, t2, t2r,
                       qr_[0:64, :], qrr)
                P.dma("pool", cx.dram["QSr"][h, :, tcols], qr_[0:64, :], reads=[qrr], writes=[])


def phase_ekv(cx, layer, ga):
    P = cx.P
    i = cx.lmap["even"][layer]
    GA = cx.dram[ga]
    KSv = cx.dram["KS"].ap().rearrange("h p t -> p h t")
    VSv = cx.dram["VS"].ap().rearrange("h p k d -> p h k d")
    with Phase(cx, "ekv%d" % layer) as ph:
        stg = None
        wuk, wukr = ph.sb(2 * 1024, BF16)
        wuv, wuvr = ph.sb(2 * 1024, BF16)
        wuk3, wuv3 = v3(wuk, 2, 1024), v3(wuv, 2, 1024)
        load_cast(cx, stg, wuk3, wukr, cx.dram["even_w_uk"][i].rearrange("(c p) f -> p c f", p=128), 2, 1024)
        load_cast(cx, stg, wuv3, wuvr, cx.dram["even_w_uv"][i].rearrange("(c p) f -> p c f", p=128), 2, 1024)
        lat = Rot([ph.sb(2 * 512, BF16) for _ in range(3)])
        kt = Rot([ph.sb(8 * 512, BF16) for _ in range(2)])
        vt = Rot([ph.sb(4096, BF16) for _ in range(2)])
        pk = Rot([ph.ps(512) for _ in range(4)])
        pv = Rot([ph.ps(1024) for _ in range(2)])
        ev = 0
        for b in range(SEQ // 512):
            r, bb = b // 4, b % 4
            l, lr = lat.next()
            l3 = v3(l, 2, 512)
            P.dma("sp", l3, GA[r, 512:768, bb * 512:(bb + 1) * 512].rearrange("(c p) t -> p c t", p=128), writes=[lr])
            k_, kr_ = kt.next()
            k3 = v3(k_, 8, 512)
            for h in range(8):
                p, pr = pk.next()
                for c in range(2):
                    P.pe(lambda e, c=c, h=h, p=p, l3=l3: e.matmul(p[:, :], lhsT=wuk3[:, c, h * 128:(h + 1) * 128],
                                                                 rhs=l3[:, c, :], start=(c == 0), stop=(c == 1)),
                         [wukr, lr], [pr])
                if ev % 2 == 0:
                    P.act(lambda e, p=p, h=h, k3=k3: e.activation(out=k3[:, h, :], in_=p[:, :], func=AF.Copy), [pr], [kr_])
                else:
                    P.dve(lambda e, p=p, h=h, k3=k3: e.tensor_copy(out=k3[:, h, :], in_=p[:, :]), [pr], [kr_])
                ev += 1
            P.dma("pool", KSv[:, :, b * 512:(b + 1) * 512], k3, reads=[kr_], writes=[])
            v_, vr_ = vt.next()
            v4 = v_[:, :].rearrange("p (h t d) -> p h t d", t=4, h=8, d=128)
            for t4 in range(4):
                p, pr = pv.next()
                for n in range(2):
                    for c in range(2):
                        P.pe(lambda e, c=c, n=n, p=p, l3=l3, t4=t4: e.matmul(
                            p[:, n * 512:(n + 1) * 512], lhsT=l3[:, c, t4 * 128:(t4 + 1) * 128],
                            rhs=wuv3[:, c, n * 512:(n + 1) * 512], start=(c == 0), stop=(c == 1)), [wuvr, lr], [pr])
                dst = v4[:, :, t4, :]
                src = p[:, :].rearrange("p (h d) -> p h d", h=8)
                if ev % 2 == 0:
                    P.act(lambda e, src=src, dst=dst: e.activation(out=dst, in_=src, func=AF.Copy), [pr], [vr_])
                else:
                    P.dve(lambda e, src=src, dst=dst: e.tensor_copy(out=dst, in_=src), [pr], [vr_])
                ev += 1
            P.dma("pool", VSv[:, :, b * 4:(b + 1) * 4, :].rearrange("p h t d -> p h (t d)"),
                  v_[:, :].rearrange("p (h x) -> p h x", h=8), reads=[vr_], writes=[])


def phase_efft(cx, layer, ga, mixT):
    P = cx.P
    GA = cx.dram[ga]
    with Phase(cx, "efft%d" % layer) as ph:
        cs1, cs1r = ph.sb(256, BF16)
        t3, t3r = ph.sb(128 * 64, BF16)
        cd, cdr = ph.sb(256, BF16)
        P.dma("sp", cs1[:, :], cx.dram["cs1"][:, :], writes=[cs1r])
        P.dma("sp", t3[:, :], cx.dram["t3"][:, :], writes=[t3r])
        P.dma("sp", cd[:, :], cx.dram["cdft"][:, :], writes=[cdr])
        t33 = v3(t3, 128, 64)
        Fg = Rot([ph.sb(128 * 128, BF16) for _ in range(2)])
        Ag, Agr = ph.sb(128 * 256, BF16)
        A5 = Ag[:, :].rearrange("p (r k c) -> p r k c", r=2, k=128, c=128)
        XT, XTr = ph.sb(2 * T, BF16)
        XT4 = XT[:, :].rearrange("p (r b k) -> p r b k", r=2, b=16, k=128)
        XT3 = v3(XT, 2, T)
        yT = Rot([ph.sb(512, BF16) for _ in range(2)])
        pA = Rot([ph.ps(512) for _ in range(3)])
        pX = Rot([ph.ps(512) for _ in range(2)])
        pY = Rot([ph.ps(512) for _ in range(2)])
        ev = 0
        for g in range(4):
            F, Fr = Fg.next()
            F3 = v3(F, 128, 128)
            for r in range(NCORES):
                src = GA[r, 0:512, :].rearrange("r (a c) -> (r a) c", c=512)[:, g * 128:(g + 1) * 128]
                P.dma("sp", F3[r * 16:(r + 1) * 16, :, :], src.rearrange("(t s) c -> t s c", s=128), writes=[Fr])
            for c2 in range(64):
                p, pr = pA.next()
                for j in range(2):
                    c = c2 * 2 + j
                    P.pe(lambda e, c=c, j=j, p=p, F3=F3: e.matmul(p[:, j * 256:(j + 1) * 256], lhsT=F3[:, :, c],
                                                                 rhs=cs1[:, :], start=True, stop=True),
                         [Fr, cs1r], [pr])
                dst = A5[:, :, :, c2 * 2:c2 * 2 + 2]
                src = p[:, :].rearrange("p (j r k) -> p r k j", j=2, r=2, k=128)
                if ev % 2 == 0:
                    P.act(lambda e, src=src, dst=dst: e.activation(out=dst, in_=src, func=AF.Copy), [pr], [Agr])
                else:
                    P.dve(lambda e, src=src, dst=dst: e.tensor_copy(out=dst, in_=src), [pr], [Agr])
                ev += 1
            for kb in range(8):
                p, pr = pX.next()
                for kl in range(16):
                    k1 = kb * 16 + kl
                    P.pe(lambda e, k1=k1, kl=kl, p=p: e.matmul(p[:, kl * 32:(kl + 1) * 32], lhsT=A5[:, 0, k1, :],
                                                               rhs=t33[:, k1, 0:32], start=True, stop=False),
                         [Agr, t3r], [pr])
                    P.pe(lambda e, k1=k1, kl=kl, p=p: e.matmul(p[:, kl * 32:(kl + 1) * 32], lhsT=A5[:, 1, k1, :],
                                                               rhs=t33[:, k1, 32:64], start=False, stop=True),
                         [Agr, t3r], [pr])
                p3 = p[:, :].rearrange("p (k x) -> p k x", k=16, x=32)
                for ri in range(2):
                    src = p3[:, :, ri * 16:(ri + 1) * 16].rearrange("p k b -> p b k")
                    dst = XT4[:, ri, :, kb * 16:(kb + 1) * 16]
                    if ri == 0:
                        P.act(lambda e, src=src, dst=dst: e.activation(out=dst, in_=src, func=AF.Copy), [pr], [XTr])
                    else:
                        P.dve(lambda e, src=src, dst=dst: e.tensor_copy(out=dst, in_=src), [pr], [XTr])
            for tt in range(4):
                p, pr = pY.next()
                P.pe(lambda e, p=p, tt=tt: e.matmul(p[:, :], lhsT=cd[:, 0:128], rhs=XT3[:, 0, tt * 512:(tt + 1) * 512],
                                                    start=True, stop=False), [cdr, XTr], [pr])
                P.pe(lambda e, p=p, tt=tt: e.matmul(p[:, :], lhsT=cd[:, 128:256], rhs=XT3[:, 1, tt * 512:(tt + 1) * 512],
                                                    start=False, stop=True), [cdr, XTr], [pr])
                y, yr = yT.next()
                P.act(lambda e, p=p, y=y: e.activation(out=y[:, :], in_=p[:, :], func=AF.Copy), [pr], [yr])
                P.dma("pool", cx.dram[mixT][8 + g, :, tt * 512:(tt + 1) * 512], y[:, :], reads=[yr], writes=[])


def phase_eattn(cx, layer, ga, mixT):
    P = cx.P
    GA = cx.dram[ga]
    NCH = 8
    with Phase(cx, "eattn%d" % layer) as ph:
        ones, onesr = ph.sb(128, BF16)
        P.pool(lambda e: e.memset(ones[:, :], 1.0), [], [onesr])
        KR, _ = ph.sb(SEQ, BF16)
        krres = [P.res() for _ in range(NCH)]
        zr = P.res()
        P.pool(lambda e: e.memset(KR[64:128, :], 0.0), [], [zr])
        for r in range(NCH):
            P.dma("sp", KR[0:64, r * 2048:(r + 1) * 2048], GA[r, 768:832, :], writes=[krres[r]])
        KN, _ = ph.sb(SEQ, BF16)
        knres = [P.res() for _ in range(NCH)]
        VV, _ = ph.sb(SEQ, BF16)
        VV3 = v3(VV, 128, 128)
        vres = [P.res() for _ in range(NCH)]
        QN = Rot([ph.sb(T, BF16) for _ in range(2)])
        QR = Rot([ph.sb(T, BF16) for _ in range(2)])
        for (q_, qr_) in QR.items:
            P.pool(lambda e, q_=q_: e.memset(q_[64:128, :], 0.0), [], [qr_])
        PT = Rot([ph.sb(1024, BF16) for _ in range(4)])
        ACC = Rot([ph.sb(1024, F32) for _ in range(2)])
        ACB = Rot([ph.sb(1024, BF16) for _ in range(2)])
        rs = Rot([ph.sb(512, F32) for _ in range(2)])
        ob = Rot([ph.sb(512, BF16) for _ in range(2)])
        pS = Rot([ph.ps(1024) for _ in range(2)])
        pO = Rot([ph.ps(512) for _ in range(2)])
        pR = Rot([ph.ps(512) for _ in range(2)])
        KSd, VSd = cx.dram["KS"], cx.dram["VS"]
        for h in range(8):
            qn, qnr = QN.next()
            qr, qrr = QR.next()
            P.dma("sp", qn[:, :], cx.dram["QSn"][h, :, :], writes=[qnr])
            P.dma("sp", qr[0:64, :], cx.dram["QSr"][h, :, :], writes=[qrr])
            for c in range(NCH):
                P.dma("sp", KN[:, c * 2048:(c + 1) * 2048], KSd[h, :, c * 2048:(c + 1) * 2048], writes=[knres[c]])
                P.dma("sp", VV3[:, c * 16:(c + 1) * 16, :], VSd[h, :, c * 16:(c + 1) * 16, :], writes=[vres[c]])
            for qt in range(4):
                qc = slice(qt * 512, (qt + 1) * 512)
                po, por = pO.next()
                pr_, prr = pR.next()
                acc, accr = ACC.next()

                def qk(pi, qn=qn, qr=qr, qnr=qnr, qrr=qrr, qc=qc):
                    s, sr = pS.next()
                    for j in range(2):
                        kt = pi * 2 + j
                        c = kt // 16
                        kc = slice(kt * 128, (kt + 1) * 128)
                        P.pe(lambda e, s=s, j=j, kc=kc: e.matmul(s[:, j * 512:(j + 1) * 512], lhsT=KN[:, kc], rhs=qn[:, qc],
                                                                 start=True, stop=False), [knres[c], qnr], [sr])
                        P.pe(lambda e, s=s, j=j, kc=kc: e.matmul(s[:, j * 512:(j + 1) * 512], lhsT=KR[:, kc], rhs=qr[:, qc],
                                                                 start=False, stop=True), [krres[c], zr, qrr], [sr])
                    return s, sr

                def pv(pi, pt, ptr, po=po, por=por, acc=acc, accr=accr):
                    for j in range(2):
                        kt = pi * 2 + j
                        c = kt // 16
                        P.pe(lambda e, j=j, kt=kt, pt=pt: e.matmul(po[:, :], lhsT=VV3[:, kt, :], rhs=pt[:, j * 512:(j + 1) * 512],
                                                                   start=(kt == 0), stop=(kt == 127)), [vres[c], ptr], [por])
                    if pi == 0:
                        P.dve(lambda e, pt=pt: e.tensor_copy(out=acc[:, :], in_=pt[:, :]), [ptr], [accr])
                    else:
                        P.dve(lambda e, pt=pt: e.tensor_tensor(out=acc[:, :], in0=acc[:, :], in1=pt[:, :], op=ALU.add),
                              [ptr, accr], [accr])

                pend = None
                for pi in range(64):
                    s, sr = qk(pi)
                    pt, ptr = PT.next()
                    P.act(lambda e, s=s, pt=pt: e.activation(out=pt[:, :], in_=s[:, :], func=AF.Exp, scale=SCALE),
                          [sr], [ptr])
                    if pend is not None:
                        pv(*pend)
                    pend = (pi, pt, ptr)
                pv(*pend)
                ab, abr = ACB.next()
                P.act(lambda e, ab=ab, acc=acc: e.activation(out=ab[:, :], in_=acc[:, :], func=AF.Copy), [accr], [abr])
                for j in range(2):
                    P.pe(lambda e, j=j, ab=ab, pr_=pr_: e.matmul(pr_[:, :], lhsT=ones[:, :], rhs=ab[:, j * 512:(j + 1) * 512],
                                                                 start=(j == 0), stop=(j == 1)), [onesr, abr], [prr])
                r_, rr = rs.next()
                o_, orr = ob.next()
                P.dve(lambda e, r_=r_, pr_=pr_: e.reciprocal(out=r_[:, :], in_=pr_[:, :]), [prr], [rr])
                P.dve(lambda e, r_=r_, o_=o_, po=po: e.tensor_tensor(out=o_[:, :], in0=po[:, :], in1=r_[:, :], op=ALU.mult),
                      [por, rr], [orr])
                P.dma("pool", cx.dram[mixT][h, :, qc], o_[:, :], reads=[orr], writes=[])


def phase_eout(cx, layer, mixT, xsrc, xdst, xTdst):
    P = cx.P
    i = cx.lmap["even"][layer]
    last = None
    with Phase(cx, "eout%d" % layer) as ph:
        idt, idr = load_ident(cx, ph)
        pT, pTr = ph.ps(D, BF16)
        ln = LNState(cx, ph, cx.dram["mix_ln_g"][layer:layer + 1, :], cx.dram["mix_ln_b"][layer:layer + 1, :],
                     idt, idr, pT, pTr)
        stg = None
        wo, wor = ph.sb(12 * D, BF16)
        wo3 = v3(wo, 12, D)
        load_cast(cx, stg, wo3, wor, cx.dram["even_w_out"][i].rearrange("(c p) d -> p c d", p=128), 12, D)
        mx, mxr = ph.sb(12 * T, BF16)
        mx3 = v3(mx, 12, T)
        P.dma("sp", mx3, cx.dram[mixT].ap().rearrange("c p t -> p c t"), writes=[mxr])
        pY = Rot([ph.ps(D) for _ in range(2)])
        for ti in range(T // 128):
            xpre = ln.prefetch(xsrc, ti)
            py, pyr = pY.next()
            for half in range(2):
                for c in range(12):
                    P.pe(lambda e, c=c, half=half, py=py, ti=ti: e.matmul(
                        py[:, half * 512:(half + 1) * 512], lhsT=mx3[:, c, ti * 128:(ti + 1) * 128],
                        rhs=wo3[:, c, half * 512:(half + 1) * 512], start=(c == 0), stop=(c == 11)), [mxr, wor], [pyr])
            last = ln.tile(py[:, :], pyr, ti, xdst, xpre, xTdst)
        ln.flush()
    return last


W_EVEN = [("even_w_in", [D, 1216]), ("even_w_uq", [384, 1536]), ("even_w_uk", [256, 1024]),
          ("even_w_uv", [256, 1024]), ("even_w_out", [1536, D]), ("kvg_t", [128, 2]), ("qg_t", [128, 3])]
W_ODD = [("odd_w_in", [D, 4096]), ("odd_sgu_norm_g", [2048]), ("odd_sgu_norm_b", [2048]),
         ("odd_wsT", [8, 128, 128]), ("odd_b16", [2048]), ("odd_w_out", [2048, D])]
W_FFN = [("ffn_w_gate", [D, DFF]), ("ffn_w_up", [D, DFF]), ("ffn_w_down", [DFF, D])]
W_LN = [("mix_ln_g", [4, D]), ("mix_ln_b", [4, D]), ("ffn_ln_g", [4, D]), ("ffn_ln_b", [4, D])]
CONSTS = [("ident", [128, 128], BF16), ("cos2", [64, T], F32), ("sin2", [64, T], F32), ("cs1", [128, 256], BF16),
          ("t3", [128, 128 * 64], BF16), ("cdft", [128, 256], BF16)]

PROG_LAYERS = {"F1": ([0], [], []), "M": ([0], [], []), "A": ([0], [], []), "B": ([0, 2], [1], [0, 1]), "C": ([2], [3], [2, 3]),
               "F": ([0, 2], [1, 3], [0, 1, 2, 3])}


def build_program(kind):
    nc = bass.Bass("TRN2", target_bir_lowering=False)
    cx = Ctx(nc)
    if kind.startswith("P:"):
        PROG_LAYERS[kind] = ([0], [1], [0])
    ev, od, ff = PROG_LAYERS[kind]
    cx.lmap = {"even": {l: k for k, l in enumerate(ev)}, "odd": {l: k for k, l in enumerate(od)},
               "ffn": {l: k for k, l in enumerate(ff)}}
    gath = {}
    if kind in ("F", "F1"):
        for l in (0, 2):
            gbh = nc.dram_tensor("gb%d" % l, [GROWS, 1024], F32)
            gah = nc.dram_tensor("ga%d" % l, [NCORES * GROWS, 1024], F32)
            cx.dram["gb%d" % l] = gbh.ap().bitcast(BF16)
            cx.dram["ga%d" % l] = gah.ap().bitcast(BF16).rearrange("(r g) c -> r g c", r=NCORES)
            gath[l] = (gbh, gah)
    cx.dt("x", [T, D], F32, "ExternalInput")
    for n, shp, dt_ in CONSTS:
        cx.dt(n, shp, dt_, "ExternalInput")
    for n, shp in W_LN:
        cx.dt(n, shp, F32, "ExternalInput")
    for grp, lays in ((W_EVEN, ev), (W_ODD, od), (W_FFN, ff)):
        if lays:
            for n, shp in grp:
                cx.dt(n, [len(lays)] + shp, F32, "ExternalInput")
    sk = "ExternalOutput" if (kind in ("F", "F1") and SCRATCH_AS_OUTPUT) else "Internal"
    for n in ("xs0", "xs1"):
        cx.dt(n, [T, D], F32, sk)
    for n in ("xT0", "xT1"):
        cx.dt(n, [128, 8, T], BF16, sk)
    dk = "ExternalOutput" if kind == "M" else sk
    cx.dt("QSn", [8, 128, T], BF16, dk)
    cx.dt("QSr", [8, 64, T], BF16, dk)
    cx.dt("KS", [8, 128, SEQ], BF16, sk)
    cx.dt("VS", [8, 128, 128, 128], BF16, sk)
    cx.dt("mixT", [12, 128, T], BF16, dk)
    P = cx.P

    def main_even(layer, ga, xsrc, xTsrc, xdst, xTdst):
        phase_eq(cx, layer, xTsrc)
        phase_ekv(cx, layer, ga)
        phase_efft(cx, layer, ga, "mixT")
        phase_eattn(cx, layer, ga, "mixT")
        phase_eout(cx, layer, "mixT", xsrc, xdst, xTdst)

    if kind.startswith("P:"):
        cx.dt("ga", [NCORES, GROWS, 2048], BF16, "ExternalInput")
        cx.dt("gb", [GROWS, 2048], BF16, "ExternalOutput")
        for p in kind[2:].split(","):
            if p == "prep":
                phase_prep(cx, "x", "xT0")
            elif p == "eproj":
                phase_eproj(cx, 0, "xT0", "gb")
            elif p == "eq":
                phase_eq(cx, 0, "xT0")
            elif p == "ekv":
                phase_ekv(cx, 0, "ga")
            elif p == "efft":
                phase_efft(cx, 0, "ga", "mixT")
            elif p == "eattn":
                phase_eattn(cx, 0, "ga", "mixT")
            elif p == "eout":
                phase_eout(cx, 0, "mixT", "x", "xs1", "xT1")
            elif p == "ffn":
                phase_ffn(cx, 0, "x", "xT0", "xs0", "xT1")
            elif p == "odd":
                phase_odd(cx, 1, "x", "xT0", "xs1", "xT1")
    elif kind == "A":
        cx.dt("gb", [GROWS, 2048], BF16, "ExternalOutput")
        cx.dt("xTout", [128, 8, T], BF16, "ExternalOutput")
        phase_prep(cx, "x", "xTout")
        phase_eproj(cx, 0, "xTout", "gb")
    elif kind == "B":
        cx.dt("ga", [NCORES, GROWS, 2048], BF16, "ExternalInput")
        cx.dt("xTin", [128, 8, T], BF16, "ExternalInput")
        cx.dt("gb", [GROWS, 2048], BF16, "ExternalOutput")
        cx.dt("xmid", [T, D], F32, "ExternalOutput")
        cx.dt("xTout", [128, 8, T], BF16, "ExternalOutput")
        main_even(0, "ga", "x", "xTin", "xs1", "xT1")
        phase_ffn(cx, 0, "xs1", "xT1", "xs0", "xT0")
        phase_odd(cx, 1, "xs0", "xT0", "xs1", "xT1")
        phase_ffn(cx, 1, "xs1", "xT1", "xmid", "xTout")
        phase_eproj(cx, 2, "xTout", "gb")
    elif kind == "M":
        cx.dt("ga", [NCORES, GROWS, 2048], BF16, "ExternalInput")
        cx.dt("xmid", [T, D], F32, "ExternalOutput")
        phase_prep(cx, "x", "xT0")
        main_even(0, "ga", "x", "xT0", "xmid", None)
    elif kind == "C":
        cx.dt("ga", [NCORES, GROWS, 2048], BF16, "ExternalInput")
        cx.dt("out", [T, D], F32, "ExternalOutput")
        cx.dt("xTin", [128, 8, T], BF16, "ExternalInput")
        main_even(2, "ga", "x", "xTin", "xs1", "xT1")
        phase_ffn(cx, 2, "xs1", "xT1", "xs0", "xT0")
        phase_odd(cx, 3, "xs0", "xT0", "xs1", "xT1")
        phase_ffn(cx, 3, "xs1", "xT1", "out", None)
    elif kind in ("F", "F1"):
        cx.dt("out", [T, D], F32, "ExternalOutput")
        def all_gather(l):
            gbh, gah = gath[l]
            if DUMMY_CC:
                dgi = nc.dram_tensor("dgi%d" % l, [16, 128], F32)
                dgo = nc.dram_tensor("dgo%d" % l, [NCORES * 16, 128], F32)
                P.add("pool", lambda e: e.collective_compute("AllGather", ALU.bypass,
                                                             replica_groups=[list(range(NCORES))],
                                                             ins=[dgi.ap().opt()], outs=[dgo.ap().opt()]),
                      dma=True, inc=1)
                P.barrier()
            if NO_CC:
                for r in range(NCORES):
                    P.dma("pool", gah[r * GROWS:(r + 1) * GROWS, :], gbh[:, :])
                P.barrier()
                return
            P.add("pool", lambda e: e.collective_compute("AllGather", ALU.bypass,
                                                         replica_groups=[list(range(NCORES))],
                                                         ins=[gbh.ap().opt()], outs=[gah.ap().opt()]),
                  dma=True, inc=1)
            P.barrier()

        phase_prep(cx, "x", "xT0")
        phase_eproj(cx, 0, "xT0", "gb0")
        all_gather(0)
        if kind == "F1":
            main_even(0, "ga0", "x", "xT0", "out", None)
            P.emit()
            return nc, cx
        main_even(0, "ga0", "x", "xT0", "xs1", "xT1")
        phase_ffn(cx, 0, "xs1", "xT1", "xs0", "xT0")
        phase_odd(cx, 1, "xs0", "xT0", "xs1", "xT1")
        phase_ffn(cx, 1, "xs1", "xT1", "xs0", "xT0")
        phase_eproj(cx, 2, "xT0", "gb2")
        all_gather(2)
        main_even(2, "ga2", "xs0", "xT0", "xs1", "xT1")
        phase_ffn(cx, 2, "xs1", "xT1", "xs0", "xT0")
        phase_odd(cx, 3, "xs0", "xT0", "xs1", "xT1")
        phase_ffn(cx, 3, "xs1", "xT1", "out", None)
    else:
        raise ValueError(kind)
    P.emit()
    return nc, cx


def _bf(a):
    return np.asarray(a, dtype=np.float32).astype(ml_dtypes.bfloat16)


def _const_tables(core):
    inv = 1.0 / (10000.0 ** (np.arange(0, 64, 2, dtype=np.float64) / 64.0))
    pos = np.arange(core * T, (core + 1) * T, dtype=np.float64)
    ang = (pos[:, None].astype(np.float32) * inv[None, :].astype(np.float32)).astype(np.float32)
    c, s_ = np.cos(ang.astype(np.float64)), np.sin(ang.astype(np.float64))
    cos2 = np.concatenate([c.T, c.T], 0).astype(np.float32)
    sin2 = np.concatenate([-s_.T, s_.T], 0).astype(np.float32)
    n = np.arange(128, dtype=np.float64)
    th = 2 * np.pi * np.outer(n, n) / 128.0
    cs1 = np.concatenate([np.cos(th), -np.sin(th)], 1)
    norm = 1.0 / np.sqrt(SEQ * 128.0)
    cdft = np.concatenate([np.cos(th) * norm, np.sin(th) * norm], 1)
    s2 = np.arange(128, dtype=np.float64)[:, None, None]
    k1 = np.arange(128, dtype=np.float64)[None, :, None]
    k2 = (16 * core + np.arange(16, dtype=np.float64))[None, None, :]
    k = k1 + 128.0 * k2
    ph = 2 * np.pi * ((s2 * k) % SEQ) / SEQ
    mre, mim = np.cos(ph), -np.sin(ph)
    t3 = np.concatenate([mre, mim, -mim, mre], 2).reshape(128, 128 * 64)
    return {"ident": _bf(np.eye(128)), "cos2": np.ascontiguousarray(cos2), "sin2": np.ascontiguousarray(sin2),
            "cs1": _bf(cs1), "t3": _bf(t3), "cdft": _bf(cdft)}


def _weights(inp, kind):
    ev, od, ff = PROG_LAYERS[kind]
    w = {}
    for n in ("mix_ln_g", "mix_ln_b", "ffn_ln_g", "ffn_ln_b"):
        w[n] = np.ascontiguousarray(inp[n], dtype=np.float32)
    if ev:
        ei = [l // 2 for l in ev]
        for n in ("even_w_in", "even_w_uq", "even_w_uk", "even_w_uv", "even_w_out"):
            w[n] = np.ascontiguousarray(np.asarray(inp[n], dtype=np.float32)[ei])
        w["kvg_t"] = np.ascontiguousarray(np.asarray(inp["even_kv_norm"], dtype=np.float32)[ei].reshape(len(ei), 2, 128).transpose(0, 2, 1))
        w["qg_t"] = np.ascontiguousarray(np.asarray(inp["even_q_norm"], dtype=np.float32)[ei].reshape(len(ei), 3, 128).transpose(0, 2, 1))
    if od:
        oi = [l // 2 for l in od]
        for n in ("odd_w_in", "odd_sgu_norm_g", "odd_sgu_norm_b", "odd_w_out"):
            w[n] = np.ascontiguousarray(np.asarray(inp[n], dtype=np.float32)[oi])
        w["odd_wsT"] = np.ascontiguousarray(np.asarray(inp["odd_w_spatial"], dtype=np.float32)[oi].transpose(0, 1, 3, 2))
        w["odd_b16"] = np.ascontiguousarray(np.repeat(np.asarray(inp["odd_b_spatial"], dtype=np.float32)[oi], 2, axis=1).reshape(len(oi), 2048))
    if ff:
        for n in ("ffn_w_gate", "ffn_w_up", "ffn_w_down"):
            w[n] = np.ascontiguousarray(np.asarray(inp[n], dtype=np.float32)[ff])
    return w


_PROGS = {}


def _prog(kind):
    if kind not in _PROGS:
        _PROGS[kind] = build_program(kind)[0]
    return _PROGS[kind]


def _launch(kind, inp, xs, ga=None, xT=None):
    w = _weights(inp, kind)
    maps = []
    for c in range(NCORES):
        m = dict(w)
        m.update(_const_tables(c))
        m["x"] = np.ascontiguousarray(xs[c], dtype=np.float32)
        if ga is not None:
            m["ga"] = ga
        if xT is not None:
            m["xTin"] = xT[c]
        maps.append(m)
    res = run_bass_kernel_spmd(_prog(kind), maps, core_ids=list(range(NCORES)))
    return res.results


FUSED = False


def kernel(**inp):
    x = np.asarray(inp["x"], dtype=np.float32).reshape(SEQ, D)
    xs = [x[c * T:(c + 1) * T] for c in range(NCORES)]
    if FUSED:
        r = _launch("F", inp, xs)
        out = np.concatenate([r[c]["out"] for c in range(NCORES)], 0)
        return out.reshape(1, SEQ, D).astype(np.float32)
    ra = _launch("A", inp, xs)
    ga = np.ascontiguousarray(np.stack([ra[c]["gb"] for c in range(NCORES)], 0))
    rb = _launch("B", inp, xs, ga, [ra[c]["xTout"] for c in range(NCORES)])
    ga = np.ascontiguousarray(np.stack([rb[c]["gb"] for c in range(NCORES)], 0))
    xs = [rb[c]["xmid"] for c in range(NCORES)]
    rc = _launch("C", inp, xs, ga, [rb[c]["xTout"] for c in range(NCORES)])
    out = np.concatenate([rc[c]["out"] for c in range(NCORES)], 0)
    return out.reshape(1, SEQ, D).astype(np.float32)
```

```python
import contextlib
import numpy as np
import ml_dtypes
import concourse.bass as bass
import concourse.mybir as mybir
from concourse.bass_utils import run_bass_kernel_spmd

F32 = mybir.dt.float32
BF16 = mybir.dt.bfloat16
AF = mybir.ActivationFunctionType
ALU = mybir.AluOpType

NCORES = 8
SEQ = 16384
T = SEQ // NCORES
D = 1024
DFF = 2816
NFC = DFF // 128
DEPTH = 4
ALPHA = float((2 * DEPTH) ** 0.25)
LN_EPS = 1e-5
RMS_EPS = 1e-6
GROWS = 832
SCALE = float(192 ** -0.5)
NO_CC = False
DUMMY_CC = False
SCRATCH_AS_OUTPUT = True

ENGS = ["pe", "act", "dve", "pool", "sp"]
SEM_EPOCH = 20000
NDMA_SEMS = {"sp": 4, "pool": 2, "act": 1, "pe": 1, "dve": 1}


class Res:
    __slots__ = ("name", "w", "r")

    def __init__(self, name=""):
        self.name = name
        self.w = None
        self.r = []


class Op:
    __slots__ = ("eng", "fn", "deps", "sig", "sem", "val", "dma", "inc")


class Prog:
    def __init__(self, nc):
        self.nc = nc
        self.ops = {e: [] for e in ENGS}
        self.n_dma = {e: 0 for e in ENGS}
        self.dma_last = {}
        self.all_res = []

    def res(self, name=""):
        r = Res(name)
        self.all_res.append(r)
        return r

    def add(self, eng, fn, reads=(), writes=(), dma=False, inc=None, extra=()):
        op = Op()
        op.inc = inc if inc is not None else (16 if dma else 1)
        op.eng = eng
        op.fn = fn
        op.dma = dma
        op.sig = dma
        op.sem = None
        op.val = 0
        deps = set(extra)
        for r in reads:
            if r.w is not None:
                deps.add(r.w)
        for w in writes:
            if w.w is not None:
                deps.add(w.w)
            deps.update(w.r)
        for r in reads:
            r.r.append(op)
        for w in writes:
            w.w = op
            w.r = []
        deps.discard(op)
        if dma:
            if op.inc == 16:
                k = (eng, self.n_dma[eng] % NDMA_SEMS[eng])
                self.n_dma[eng] += 1
            else:
                k = (eng, "cc")
            prev = self.dma_last.get(k)
            if prev is not None:
                deps.add(prev)
            self.dma_last[k] = op
            op.sem = k
        if eng == "pe" and not dma:
            deps = {d for d in deps if not (d.eng == "pe" and not d.dma)}
        op.deps = deps
        self.ops[eng].append(op)
        return op

    def pe(self, fn, reads=(), writes=()):
        return self.add("pe", fn, reads, writes)

    def act(self, fn, reads=(), writes=()):
        return self.add("act", fn, reads, writes)

    def dve(self, fn, reads=(), writes=()):
        return self.add("dve", fn, reads, writes)

    def pool(self, fn, reads=(), writes=()):
        return self.add("pool", fn, reads, writes)

    def dma(self, q, out, in_, reads=(), writes=()):
        return self.add(q, lambda e: e.dma_start(out=out, in_=in_), reads, writes, dma=True)

    def barrier(self):
        last = []
        for e in ENGS:
            for op in reversed(self.ops[e]):
                if not op.dma:
                    last.append(op)
                    break
        last.extend(self.dma_last.values())
        for r in self.all_res:
            r.w = None
            r.r = []
        for e in ENGS:
            self.add(e, None, extra=last)

    def emit(self, final_ops=()):
        nc = self.nc
        for e in ENGS:
            for op in self.ops[e]:
                for d in op.deps:
                    d.sig = True
        for op in final_ops:
            op.sig = True
        sems = {}

        def get_sem(key):
            if key not in sems:
                sems[key] = nc.alloc_semaphore("s_%s_%s" % key)
            return sems[key]

        dma_cnt = {}
        for e in ENGS:
            c = 0
            for op in self.ops[e]:
                if op.dma:
                    k = op.sem
                    dma_cnt[k] = dma_cnt.get(k, 0) + op.inc
                    op.sem = get_sem(("d" + k[0], k[1]))
                    op.val = dma_cnt[k]
                elif op.sig:
                    ep = c // SEM_EPOCH
                    c += 1
                    op.sem = get_sem((e, ep))
                    op.val = c - ep * SEM_EPOCH
        self.nsems = len(sems)
        nwaits = {e: 0 for e in ENGS}
        ninst = {e: 0 for e in ENGS}

        def run(e, eo):
            seen = {}
            for op in self.ops[e]:
                need = {}
                for d in op.deps:
                    k = id(d.sem)
                    if seen.get(k, 0) >= d.val:
                        continue
                    if k not in need or need[k][1] < d.val:
                        need[k] = (d.sem, d.val)
                for k, (sm, v) in need.items():
                    eo.wait_ge(sm, v)
                    seen[k] = v
                    nwaits[e] += 1
                if op.fn is None:
                    if op.sig:
                        eo.nop().then_inc(op.sem, op.inc)
                    continue
                ins = op.fn(eo)
                ninst[e] += 1
                if op.sig:
                    ins.then_inc(op.sem, op.inc)
            if e == "sp":
                for op in final_ops:
                    if seen.get(id(op.sem), 0) < op.val:
                        eo.wait_ge(op.sem, op.val)
                        seen[id(op.sem)] = op.val

        with nc.Block() as block:
            @block.tensor
            def _(eo):
                run("pe", eo)

            @block.scalar
            def _(eo):
                run("act", eo)

            @block.vector
            def _(eo):
                run("dve", eo)

            @block.gpsimd
            def _(eo):
                run("pool", eo)

            @block.sync
            def _(eo):
                run("sp", eo)
        self.nwaits = nwaits
        self.ninst = ninst


def v3(t, a, b):
    return t[:, :].rearrange("p (a b) -> p a b", a=a, b=b)


class Ctx:
    def __init__(self, nc):
        self.nc = nc
        self.P = Prog(nc)
        self.dram = {}
        self.dres = {}
        self.lmap = {}

    def dt(self, name, shape, dtype, kind="Internal"):
        t = self.nc.dram_tensor(name, list(shape), dtype, kind=kind)
        self.dram[name] = t
        self.dres[name] = self.P.res(name)
        return t


class Phase:
    def __init__(self, cx, name):
        self.cx = cx
        self.name = name
        self.st = contextlib.ExitStack()
        self.n = 0

    def __enter__(self):
        self.st.__enter__()
        return self

    def __exit__(self, *a):
        self.cx.P.barrier()
        return self.st.__exit__(*a)

    def sb(self, cols, dtype, parts=128):
        self.n += 1
        t = self.st.enter_context(self.cx.nc.sbuf_tensor("%s_sb%d" % (self.name, self.n), [parts, cols], dtype))
        return t, self.cx.P.res()

    def ps(self, cols, dtype=F32):
        self.n += 1
        t = self.st.enter_context(self.cx.nc.psum_tensor("%s_ps%d" % (self.name, self.n), [128, cols], dtype))
        return t, self.cx.P.res()


class Rot:
    def __init__(self, items):
        self.items = items
        self.i = 0

    def next(self):
        it = self.items[self.i % len(self.items)]
        self.i += 1
        return it


def load_cast(cx, stg, dst3, dres, src3, A, B, eng="pool", scols=2048):
    cx.P.dma("pool", dst3, src3, writes=[dres])


class LNState:
    def __init__(self, cx, ph, g_ap, b_ap, ident, ident_r, pT, pT_r, nbuf=2):
        P = cx.P
        self.cx = cx
        self.g, self.gr = ph.sb(D, F32)
        self.b, self.br = ph.sb(D, F32)
        P.dma("sp", self.g[:, :], g_ap.partition_broadcast(128), writes=[self.gr])
        P.dma("sp", self.b[:, :], b_ap.partition_broadcast(128), writes=[self.br])
        self.xt = Rot([ph.sb(D, F32) for _ in range(nbuf)])
        self.zt = Rot([ph.sb(D, F32) for _ in range(nbuf)])
        self.xb = Rot([ph.sb(D, BF16) for _ in range(nbuf)])
        self.xT = Rot([ph.sb(D, BF16) for _ in range(nbuf)])
        self.st = Rot([ph.sb(16, F32) for _ in range(2)])
        self.mh, self.mhr = ph.sb(1, F32)
        P.pool(lambda e: e.memset(self.mh[:, :], -0.5), [], [self.mhr])
        self.ident, self.ident_r = ident, ident_r
        self.pT, self.pT_r = pT, pT_r

    def prefetch(self, xsrc, ti):
        cx = self.cx
        xt, xtr = self.xt.next()
        cx.P.dma("sp", xt[:, :], cx.dram[xsrc][ti * 128:(ti + 1) * 128, :], writes=[xtr])
        return xt, xtr

    def tile(self, ypsum, yres, ti, xdst, xpre, xTdst):
        cx = self.cx
        P = cx.P
        self.flush()
        xt, xtr = xpre
        zt, ztr = self.zt.next()
        xb, xbr = self.xb.next()
        xT, xTr = self.xT.next()
        st, str_ = self.st.next()
        P.dve(lambda e: e.scalar_tensor_tensor(out=zt[:, :], in0=xt[:, :], scalar=ALPHA, in1=ypsum,
                                               op0=ALU.mult, op1=ALU.add), [xtr, yres], [ztr])
        for c in range(2):
            P.dve(lambda e, c=c: e.bn_stats(out=st[:, c * 6:(c + 1) * 6], in_=zt[:, c * 512:(c + 1) * 512]),
                  [ztr], [str_])
        P.dve(lambda e: e.bn_aggr(out=st[:, 12:14], in_=st[:, 0:12]), [str_], [str_])
        P.dve(lambda e: e.tensor_scalar(out=st[:, 14:15], in0=st[:, 13:14], scalar1=LN_EPS, scalar2=None,
                                        op0=ALU.add), [str_], [str_])
        P.pool(lambda e: e.tensor_tensor(out=st[:, 14:15], in0=st[:, 14:15], in1=self.mh[:, :], op=ALU.pow),
               [str_, self.mhr], [str_])
        P.dve(lambda e: e.tensor_scalar(out=zt[:, :], in0=zt[:, :], scalar1=st[:, 12:13], scalar2=st[:, 14:15],
                                        op0=ALU.subtract, op1=ALU.mult), [str_, ztr], [ztr])
        P.pool(lambda e: e.tensor_tensor(out=zt[:, :], in0=zt[:, :], in1=self.g[:, :], op=ALU.mult),
               [ztr, self.gr], [ztr])
        P.pool(lambda e: e.tensor_tensor(out=zt[:, :], in0=zt[:, :], in1=self.b[:, :], op=ALU.add),
               [ztr, self.br], [ztr])
        P.act(lambda e: e.activation(out=xb[:, :], in_=zt[:, :], func=AF.Copy), [ztr], [xbr])
        o1 = P.dma("pool", cx.dram[xdst][ti * 128:(ti + 1) * 128, :], zt[:, :], reads=[ztr], writes=[])
        if xTdst is not None:
            self.pending = (xb, xbr, xT, xTr, ti, xTdst)
        return o1

    pending = None

    def flush(self):
        if self.pending is None:
            return
        cx = self.cx
        P = cx.P
        xb, xbr, xT, xTr, ti, xTdst = self.pending
        self.pending = None
        for c in range(8):
            P.pe(lambda e, c=c: e.transpose(out=self.pT[:, c * 128:(c + 1) * 128], in_=xb[:, c * 128:(c + 1) * 128],
                                            identity=self.ident[:, :]), [xbr, self.ident_r], [self.pT_r])
        P.dve(lambda e: e.tensor_copy(out=xT[:, :], in_=self.pT[:, :]), [self.pT_r], [xTr])
        P.dma("pool", cx.dram[xTdst][:, :, ti * 128:(ti + 1) * 128], v3(xT, 8, 128), reads=[xTr], writes=[])


def load_ident(cx, ph):
    idt, idr = ph.sb(128, BF16)
    cx.P.dma("sp", idt[:, :], cx.dram["ident"][:, :], writes=[idr])
    return idt, idr


def phase_prep(cx, xsrc, xTdst):
    P = cx.P
    with Phase(cx, "prep") as ph:
        idt, idr = load_ident(cx, ph)
        pT, pTr = ph.ps(D, BF16)
        xt = Rot([ph.sb(D, F32) for _ in range(2)])
        xb = Rot([ph.sb(D, BF16) for _ in range(2)])
        xT = Rot([ph.sb(D, BF16) for _ in range(2)])
        for ti in range(T // 128):
            a, ar = xt.next()
            b, br = xb.next()
            c_, cr = xT.next()
            P.dma("sp", a[:, :], cx.dram[xsrc][ti * 128:(ti + 1) * 128, :], writes=[ar])
            P.act(lambda e, a=a, b=b: e.activation(out=b[:, :], in_=a[:, :], func=AF.Copy), [ar], [br])
            for c in range(8):
                P.pe(lambda e, c=c, b=b: e.transpose(out=pT[:, c * 128:(c + 1) * 128], in_=b[:, c * 128:(c + 1) * 128],
                                                    identity=idt[:, :]), [br, idr], [pTr])
            P.dve(lambda e, c_=c_: e.tensor_copy(out=c_[:, :], in_=pT[:, :]), [pTr], [cr])
            P.dma("pool", cx.dram[xTdst][:, :, ti * 128:(ti + 1) * 128], v3(c_, 8, 128), reads=[cr], writes=[])


def phase_ffn(cx, layer, xsrc, xTsrc, xdst, xTdst):
    P = cx.P
    last = None
    fl = cx.lmap["ffn"][layer]
    with Phase(cx, "ffn%d" % layer) as ph:
        idt, idr = load_ident(cx, ph)
        pT, pTr = ph.ps(D, BF16)
        ln = LNState(cx, ph, cx.dram["ffn_ln_g"][layer:layer + 1, :], cx.dram["ffn_ln_b"][layer:layer + 1, :],
                     idt, idr, pT, pTr)
        xTs, xTr = ph.sb(8 * T, BF16)
        xT3 = v3(xTs, 8, T)
        xTres = [P.res() for _ in range(4)]
        for q in range(4):
            P.dma("sp", xT3[:, :, q * 512:(q + 1) * 512], cx.dram[xTsrc][:, :, q * 512:(q + 1) * 512],
                  writes=[xTres[q]])
        stg = None
        wd, wdr = ph.sb(NFC * D, BF16)
        wd3 = v3(wd, NFC, D)
        wg = Rot([ph.sb(8 * 256, BF16) for _ in range(2)])
        wu = Rot([ph.sb(8 * 256, BF16) for _ in range(2)])
        hT, _ = ph.sb(NFC * 1024, BF16)
        hT3 = v3(hT, NFC, 1024)
        hres = [[P.res() for _ in range(2)] for _ in range(NFC)]
        sg = Rot([ph.sb(512, F32) for _ in range(2)])
        pA = Rot([ph.ps(D) for _ in range(3)])
        wgate = cx.dram["ffn_w_gate"][fl].rearrange("(kc p) f -> p kc f", p=128)
        wup = cx.dram["ffn_w_up"][fl].rearrange("(kc p) f -> p kc f", p=128)
        wdown = cx.dram["ffn_w_down"][fl].rearrange("(fc p) d -> p fc d", p=128)
        def load_fg(hh, fg):
            g_t, g_r = wg.next()
            u_t, u_r = wu.next()
            load_cast(cx, stg, v3(g_t, 8, 256), g_r, wgate[:, :, fg * 256:(fg + 1) * 256], 8, 256)
            load_cast(cx, stg, v3(u_t, 8, 256), u_r, wup[:, :, fg * 256:(fg + 1) * 256], 8, 256)
            if hh == 0:
                load_cast(cx, stg, wd3[:, 2 * fg:2 * fg + 2, :], wdr, wdown[:, 2 * fg:2 * fg + 2, :], 2, D)
            return g_t, g_r, u_t, u_r

        seq = [(hh, fg) for hh in range(2) for fg in range(NFC // 2)]
        pending = load_fg(*seq[0])
        for si, (hh, fg) in enumerate(seq):
            g_t, g_r, u_t, u_r = pending
            if si + 1 < len(seq):
                pending = load_fg(*seq[si + 1])
            g3 = v3(g_t, 8, 256)
            u3 = v3(u_t, 8, 256)
            for fc in range(2):
                fcg = fg * 2 + fc
                for tt in range(2):
                    q = hh * 2 + tt
                    ab_, ar = pA.next()
                    br = ar
                    a = ab_[:, 0:512]
                    b = ab_[:, 512:1024]
                    for kc in range(8):
                        P.pe(lambda e, kc=kc, a=a, g3=g3, fc=fc, q=q: e.matmul(
                            a, lhsT=g3[:, kc, fc * 128:(fc + 1) * 128],
                            rhs=xT3[:, kc, q * 512:(q + 1) * 512], start=(kc == 0), stop=(kc == 7)),
                            [g_r, xTres[q]], [ar])
                    for kc in range(8):
                        P.pe(lambda e, kc=kc, b=b, u3=u3, fc=fc, q=q: e.matmul(
                            b, lhsT=u3[:, kc, fc * 128:(fc + 1) * 128],
                            rhs=xT3[:, kc, q * 512:(q + 1) * 512], start=(kc == 0), stop=(kc == 7)),
                            [u_r, xTres[q]], [br])
                    s, sr = sg.next()
                    P.act(lambda e, s=s, a=a: e.activation(out=s[:, :], in_=a, func=AF.Silu), [ar], [sr])
                    P.dve(lambda e, s=s, b=b, fcg=fcg, tt=tt: e.tensor_tensor(
                        out=hT3[:, fcg, tt * 512:(tt + 1) * 512], in0=s[:, :], in1=b, op=ALU.mult),
                        [sr, br], [hres[fcg][tt]])
            if fg != NFC // 2 - 1:
                continue
            for tl in range(8):
                ti = hh * 8 + tl
                xpre = ln.prefetch(xsrc, ti)
                py, pyr = pA.next()
                for half in range(2):
                    for fcg in range(NFC):
                        P.pe(lambda e, fcg=fcg, tl=tl, half=half, py=py: e.matmul(
                            py[:, half * 512:(half + 1) * 512], lhsT=hT3[:, fcg, tl * 128:(tl + 1) * 128],
                            rhs=wd3[:, fcg, half * 512:(half + 1) * 512], start=(fcg == 0), stop=(fcg == NFC - 1)),
                            [hres[fcg][tl // 4], wdr], [pyr])
                last = ln.tile(py[:, :], pyr, ti, xdst, xpre, xTdst)
        ln.flush()
    return last


def phase_odd(cx, layer, xsrc, xTsrc, xdst, xTdst):
    P = cx.P
    i = cx.lmap["odd"][layer]
    last = None
    with Phase(cx, "odd%d" % layer) as ph:
        idt, idr = load_ident(cx, ph)
        pT, pTr = ph.ps(D, BF16)
        ln = LNState(cx, ph, cx.dram["mix_ln_g"][layer:layer + 1, :], cx.dram["mix_ln_b"][layer:layer + 1, :],
                     idt, idr, pT, pTr, nbuf=1)
        stg = None
        wu, wur = ph.sb(8 * 2048, BF16)
        wv, wvr = ph.sb(8 * 2048, BF16)
        wo, wor = ph.sb(16 * D, BF16)
        wst, wsr = ph.sb(8 * 128, BF16)
        wu3, wv3, wo3, ws3 = v3(wu, 8, 2048), v3(wv, 8, 2048), v3(wo, 16, D), v3(wst, 8, 128)
        win = cx.dram["odd_w_in"][i].rearrange("(kc p) f -> p kc f", p=128)
        wurs = [P.res() for _ in range(4)]
        wvrs = [P.res() for _ in range(4)]
        wors = [P.res() for _ in range(2)]
        for n in range(4):
            load_cast(cx, stg, wu3[:, :, n * 512:(n + 1) * 512], wurs[n], win[:, :, n * 512:(n + 1) * 512], 8, 512, scols=1024)
        for n in range(4):
            load_cast(cx, stg, wv3[:, :, n * 512:(n + 1) * 512], wvrs[n], win[:, :, 2048 + n * 512:2048 + (n + 1) * 512],
                      8, 512, scols=1024)
        load_cast(cx, stg, ws3, wsr, cx.dram["odd_wsT"][i].rearrange("g q p -> q g p"), 8, 128, scols=1024)
        wod = cx.dram["odd_w_out"][i].rearrange("(cc p) d -> p cc d", p=128)
        for hf in range(2):
            load_cast(cx, stg, wo3[:, :, hf * 512:(hf + 1) * 512], wors[hf], wod[:, :, hf * 512:(hf + 1) * 512], 16, 512)
        sg, sgr = ph.sb(2048, F32)
        sb_, sbr = ph.sb(2048, F32)
        bt, btr = ph.sb(2048, F32)
        P.dma("sp", sg[:, :], cx.dram["odd_sgu_norm_g"][i:i + 1, :].partition_broadcast(128), writes=[sgr])
        P.dma("sp", sb_[:, :], cx.dram["odd_sgu_norm_b"][i:i + 1, :].partition_broadcast(128), writes=[sbr])
        P.dma("sp", bt[:, :], cx.dram["odd_b16"][i:i + 1, :].partition_broadcast(128), writes=[btr])
        xTt = Rot([ph.sb(8 * 512, BF16) for _ in range(2)])
        uT, _ = ph.sb(16 * 512, BF16)
        uT3 = v3(uT, 16, 512)
        ures = [P.res() for _ in range(16)]
        vbs = Rot([ph.sb(2048, BF16) for _ in range(2)])
        t32, t32r = ph.sb(2048, F32)
        vn, vnr = ph.sb(2048, BF16)
        mT, mTr = ph.sb(2048, BF16)
        mT3 = v3(mT, 16, 128)
        st, str_ = ph.sb(32, F32)
        pA = Rot([ph.ps(1024) for _ in range(3)])
        pS = Rot([ph.ps(512) for _ in range(1)])
        state = {}

        def emit_u(stile):
            xt_, xtr_ = xTt.next()
            x3 = v3(xt_, 8, 512)
            P.dma("sp", x3, cx.dram[xTsrc][:, :, stile * 512:(stile + 1) * 512], writes=[xtr_])
            state["x"] = (x3, xtr_)
            for cc in range(16):
                pb, pbr = pA.next()
                pu = pb[:, 0:512]
                for kc in range(8):
                    P.pe(lambda e, kc=kc, cc=cc, x3=x3, pu=pu: e.matmul(pu, lhsT=wu3[:, kc, cc * 128:(cc + 1) * 128],
                                                                   rhs=x3[:, kc, :], start=(kc == 0), stop=(kc == 7)),
                         [wurs[cc // 4], xtr_], [pbr])
                P.act(lambda e, cc=cc, pu=pu: e.activation(out=uT3[:, cc, :], in_=pu, func=AF.Gelu), [pbr], [ures[cc]])

        def emit_v(ti):
            x3, xtr_ = state["x"]
            tl = ti % 4
            vb, vbr = vbs.next()
            for vh in range(2):
                pv, pvr = pA.next()
                for n in range(2):
                    for kc in range(8):
                        P.pe(lambda e, kc=kc, n=n, vh=vh, pv=pv, x3=x3, tl=tl: e.matmul(
                            pv[:, n * 512:(n + 1) * 512], lhsT=x3[:, kc, tl * 128:(tl + 1) * 128],
                            rhs=wv3[:, kc, (vh * 2 + n) * 512:(vh * 2 + n + 1) * 512],
                            start=(kc == 0), stop=(kc == 7)), [wvrs[vh * 2 + n], xtr_], [pvr])
                P.act(lambda e, pv=pv, vh=vh, vb=vb: e.activation(out=vb[:, vh * 1024:(vh + 1) * 1024], in_=pv[:, :],
                                                                  func=AF.Gelu), [pvr], [vbr])
            state[ti] = (vb, vbr)

        def emit_rest(ti):
            vb, vbr = state.pop(ti)
            tl = ti % 4
            xpre = ln.prefetch(xsrc, ti)
            for c in range(4):
                P.dve(lambda e, c=c: e.bn_stats(out=st[:, c * 6:(c + 1) * 6], in_=vb[:, c * 512:(c + 1) * 512]),
                      [vbr], [str_])
            P.dve(lambda e: e.bn_aggr(out=st[:, 24:26], in_=st[:, 0:24]), [str_], [str_])
            P.dve(lambda e: e.tensor_scalar(out=st[:, 26:27], in0=st[:, 25:26], scalar1=LN_EPS, scalar2=None,
                                            op0=ALU.add), [str_], [str_])
            P.pool(lambda e: e.tensor_tensor(out=st[:, 26:27], in0=st[:, 26:27], in1=ln.mh[:, :], op=ALU.pow),
                   [str_, ln.mhr], [str_])
            P.dve(lambda e: e.scalar_tensor_tensor(out=t32[:, :], in0=vb[:, :], scalar=st[:, 24:25], in1=sg[:, :],
                                                   op0=ALU.subtract, op1=ALU.mult), [vbr, str_, sgr], [t32r])
            P.dve(lambda e: e.scalar_tensor_tensor(out=vn[:, :], in0=t32[:, :], scalar=st[:, 26:27], in1=sb_[:, :],
                                                   op0=ALU.mult, op1=ALU.add), [t32r, str_, sbr], [vnr])
            for sq in range(4):
                ps_, psr = pS.next()
                for c4 in range(4):
                    cc = sq * 4 + c4
                    P.pe(lambda e, cc=cc, c4=c4, ps_=ps_: e.matmul(ps_[:, c4 * 128:(c4 + 1) * 128],
                                                                    lhsT=vn[:, cc * 128:(cc + 1) * 128],
                                                                    rhs=ws3[:, cc // 2, :], start=True, stop=True),
                         [vnr, wsr], [psr])
                P.dve(lambda e, sq=sq, ps_=ps_: e.tensor_tensor(out=t32[:, sq * 512:(sq + 1) * 512], in0=ps_[:, :],
                                                                in1=bt[:, sq * 512:(sq + 1) * 512], op=ALU.add),
                      [psr, btr], [t32r])
            P.pool(lambda e, tl=tl: e.tensor_tensor(out=mT3, in0=v3(t32, 16, 128),
                                                    in1=uT3[:, :, tl * 128:(tl + 1) * 128], op=ALU.mult),
                   [t32r] + ures, [mTr])
            py, pyr = pA.next()
            for half in range(2):
                for cc in range(16):
                    P.pe(lambda e, cc=cc, half=half, py=py: e.matmul(
                        py[:, half * 512:(half + 1) * 512], lhsT=mT3[:, cc, :],
                        rhs=wo3[:, cc, half * 512:(half + 1) * 512], start=(cc == 0), stop=(cc == 15)),
                        [mTr, wors[half]], [pyr])
            return ln.tile(py[:, :], pyr, ti, xdst, xpre, xTdst)

        NTT = T // 128
        emit_u(0)
        emit_v(0)
        for ti in range(NTT):
            if ti + 1 < NTT and (ti + 1) % 4 != 0:
                emit_v(ti + 1)
            last = emit_rest(ti)
            if ti + 1 < NTT and (ti + 1) % 4 == 0:
                emit_u((ti + 1) // 4)
                emit_v(ti + 1)
        ln.flush()
    return last


def rms_chunks(cx, ph, pacc, paccr, pss, pssr, ones, onesr, wT3, wr, col0, nch, x3, xr, tcols, gcol, gr,
               craw, crawr, csq, csqr, rb, rbr, out3, outr, dim):
    P = cx.P
    for c in range(nch):
        pa, par = pacc.next()
        for kc in range(8):
            P.pe(lambda e, kc=kc, c=c, pa=pa: e.matmul(pa[:, :], lhsT=wT3[:, kc, col0 + c * 128:col0 + (c + 1) * 128],
                                                        rhs=x3[:, kc, tcols], start=(kc == 0), stop=(kc == 7)),
                 [wr, xr], [par])
        P.act(lambda e, c=c, pa=pa: e.activation(out=craw[:, c * 512:(c + 1) * 512], in_=pa[:, :], func=AF.Copy),
              [par], [crawr])
        P.act(lambda e, c=c, pa=pa: e.activation(out=csq[:, c * 512:(c + 1) * 512], in_=pa[:, :], func=AF.Square),
              [par], [csqr])
    for c in range(nch):
        P.pe(lambda e, c=c: e.matmul(pss[:, :], lhsT=ones[:, :], rhs=csq[:, c * 512:(c + 1) * 512],
                                     start=(c == 0), stop=(c == nch - 1)), [onesr, csqr], [pssr])
    P.dve(lambda e: e.tensor_scalar(out=rb[:, :], in0=pss[:, :], scalar1=1.0 / dim, scalar2=RMS_EPS,
                                    op0=ALU.mult, op1=ALU.add), [pssr], [rbr])
    P.act(lambda e: e.activation(out=rb[:, :], in_=rb[:, :], func=AF.Sqrt), [rbr], [rbr])
    P.dve(lambda e: e.reciprocal(out=rb[:, :], in_=rb[:, :]), [rbr], [rbr])
    for c in range(nch):
        P.dve(lambda e, c=c: e.scalar_tensor_tensor(out=out3[:, c, :], in0=craw[:, c * 512:(c + 1) * 512],
                                                    scalar=gcol[:, c:c + 1], in1=rb[:, :],
                                                    op0=ALU.mult, op1=ALU.mult), [crawr, gr, rbr], [outr])


def rotary(cx, p0, p0r, p1, p1r, cos, sin, csr, tcols, t1, t1r, t2, t2r, out, outr):
    P = cx.P
    P.dve(lambda e: e.tensor_tensor(out=t1[0:64, :], in0=p0[0:64, :], in1=cos[0:64, tcols], op=ALU.mult),
          [p0r, csr], [t1r])
    P.dve(lambda e: e.tensor_tensor(out=t2[0:64, :], in0=p1[0:64, :], in1=sin[0:64, tcols], op=ALU.mult),
          [p1r, csr], [t2r])
    P.pool(lambda e: e.tensor_tensor(out=out, in0=t1[0:64, :], in1=t2[0:64, :], op=ALU.add), [t1r, t2r], [outr])


def phase_eproj(cx, layer, xTsrc, gb):
    P = cx.P
    i = cx.lmap["even"][layer]
    GB = cx.dram[gb]
    with Phase(cx, "eproj%d" % layer) as ph:
        stg = None
        xTs, xr = ph.sb(8 * T, BF16)
        x3 = v3(xTs, 8, T)
        xrs = [P.res() for _ in range(4)]
        for q_ in range(4):
            P.dma("sp", x3[:, :, q_ * 512:(q_ + 1) * 512], cx.dram[xTsrc][:, :, q_ * 512:(q_ + 1) * 512], writes=[xrs[q_]])
        win = cx.dram["even_w_in"][i].rearrange("(kc p) f -> p kc f", p=128)
        wkv, wkvr = ph.sb(8 * 256, BF16)
        wkr, wkrr = ph.sb(8 * 256, BF16)
        wf, wfr = ph.sb(8 * 512, BF16)
        wkv3, wkr3, wf3 = v3(wkv, 8, 256), v3(wkr, 8, 256), v3(wf, 8, 512)
        P.pool(lambda e: e.memset(wkr[:, :], 0.0), [], [wkrr])
        load_cast(cx, stg, wkv3, wkvr, win[:, :, 384:640], 8, 256)
        load_cast(cx, stg, wkr3[:, :, 0:64], wkrr, win[:, :, 640:704], 8, 64)
        load_cast(cx, stg, wkr3[:, :, 128:160], wkrr, win[:, :, 672:704], 8, 32)
        load_cast(cx, stg, wkr3[:, :, 160:192], wkrr, win[:, :, 640:672], 8, 32)
        load_cast(cx, stg, wf3, wfr, win[:, :, 704:1216], 8, 512)
        ones, onesr = ph.sb(128, BF16)
        P.pool(lambda e: e.memset(ones[:, :], 1.0), [], [onesr])
        gcol, gr = ph.sb(2, F32)
        P.dma("sp", gcol[:, :], cx.dram["kvg_t"][i], writes=[gr])
        cos, csr = ph.sb(T, F32)
        sin, _ = ph.sb(T, F32)
        P.dma("sp", cos[0:64, :], cx.dram["cos2"][:, :], writes=[csr])
        P.dma("sp", sin[0:64, :], cx.dram["sin2"][:, :], writes=[csr])
        craw, crawr = ph.sb(2 * 512, F32)
        csq, csqr = ph.sb(2 * 512, BF16)
        rb, rbr = ph.sb(512, F32)
        cn = Rot([ph.sb(2 * 512, BF16) for _ in range(2)])
        t1, t1r = ph.sb(512, F32)
        t2, t2r = ph.sb(512, F32)
        kro = Rot([ph.sb(512, BF16) for _ in range(2)])
        fo = Rot([ph.sb(512, BF16) for _ in range(2)])
        pacc = Rot([ph.ps(512) for _ in range(3)])
        pss, pssr = ph.ps(512)
        pk = [ph.ps(512) for _ in range(2)]
        pf = Rot([ph.ps(512) for _ in range(2)])
        fview = GB[0:512, :].rearrange("r (a c) -> (r a) c", c=512)
        for tt in range(4):
            tcols = slice(tt * 512, (tt + 1) * 512)
            xr = xrs[tt]
            o, orr = cn.next()
            o3 = v3(o, 2, 512)
            rms_chunks(cx, ph, pacc, None, pss, pssr, ones, onesr, wkv3, wkvr, 0, 2, x3, xr, tcols, gcol, gr,
                       craw, crawr, csq, csqr, rb, rbr, o3, orr, 256.0)
            P.dma("pool", GB[512:768, tcols].rearrange("(c p) t -> p c t", p=128), o3, reads=[orr], writes=[])
            for j in range(2):
                for kc in range(8):
                    P.pe(lambda e, kc=kc, j=j, tcols=tcols: e.matmul(pk[j][0][:, :], lhsT=wkr3[:, kc, j * 128:(j + 1) * 128],
                                                                     rhs=x3[:, kc, tcols], start=(kc == 0), stop=(kc == 7)),
                         [wkrr, xr], [pk[j][1]])
            ko, kor = kro.next()
            rotary(cx, pk[0][0], pk[0][1], pk[1][0], pk[1][1], cos, sin, csr, tcols, t1, t1r, t2, t2r, ko[0:64, :], kor)
            P.dma("pool", GB[768:832, tcols], ko[0:64, :], reads=[kor], writes=[])
            for tl in range(4):
                t0 = tt * 512 + tl * 128
                p, pr = pf.next()
                for kc in range(8):
                    P.pe(lambda e, kc=kc, p=p, t0=t0: e.matmul(p[:, :], lhsT=x3[:, kc, t0:t0 + 128], rhs=wf3[:, kc, :],
                                                              start=(kc == 0), stop=(kc == 7)), [wfr, xr], [pr])
                f_, fr = fo.next()
                P.act(lambda e, p=p, f_=f_: e.activation(out=f_[:, :], in_=p[:, :], func=AF.Copy), [pr], [fr])
                P.dma("pool", fview[t0:t0 + 128, :], f_[:, :], reads=[fr], writes=[])


def phase_eq(cx, layer, xTsrc):
    P = cx.P
    i = cx.lmap["even"][layer]
    with Phase(cx, "eq%d" % layer) as ph:
        stg = None
        xTs, xr = ph.sb(8 * T, BF16)
        x3 = v3(xTs, 8, T)
        xrs = [P.res() for _ in range(4)]
        for q_ in range(4):
            P.dma("sp", x3[:, :, q_ * 512:(q_ + 1) * 512], cx.dram[xTsrc][:, :, q_ * 512:(q_ + 1) * 512], writes=[xrs[q_]])
        win = cx.dram["even_w_in"][i].rearrange("(kc p) f -> p kc f", p=128)
        wuq_d = cx.dram["even_w_uq"][i].rearrange("(kc p) f -> p kc f", p=128)
        wq1, wq1r = ph.sb(8 * 384, BF16)
        wq13 = v3(wq1, 8, 384)
        load_cast(cx, stg, wq13, wq1r, win[:, :, 0:384], 8, 384)
        wuq, wuqr = ph.sb(3 * 1536, BF16)
        wuq3 = v3(wuq, 3, 1536)
        for n in range(3):
            load_cast(cx, stg, wuq3[:, :, n * 512:(n + 1) * 512], wuqr, wuq_d[:, :, n * 512:(n + 1) * 512], 3, 512)
        wsw, wswr = ph.sb(3 * 512, BF16)
        wsw3 = v3(wsw, 3, 512)
        wswrs = [P.res() for _ in range(8)]
        for h in range(8):
            b = h * 192 + 128
            load_cast(cx, stg, wsw3[:, :, h * 64:h * 64 + 32], wswrs[h], wuq_d[:, :, b + 32:b + 64], 3, 32)
            load_cast(cx, stg, wsw3[:, :, h * 64 + 32:h * 64 + 64], wswrs[h], wuq_d[:, :, b:b + 32], 3, 32)
        ones, onesr = ph.sb(128, BF16)
        P.pool(lambda e: e.memset(ones[:, :], 1.0), [], [onesr])
        gcol, gr = ph.sb(3, F32)
        P.dma("sp", gcol[:, :], cx.dram["qg_t"][i], writes=[gr])
        cos, csr = ph.sb(T, F32)
        sin, _ = ph.sb(T, F32)
        P.dma("sp", cos[0:64, :], cx.dram["cos2"][:, :], writes=[csr])
        P.dma("sp", sin[0:64, :], cx.dram["sin2"][:, :], writes=[csr])
        craw, crawr = ph.sb(3 * 512, F32)
        csq, csqr = ph.sb(3 * 512, BF16)
        rb, rbr = ph.sb(512, F32)
        cqn, cqnr = ph.sb(3 * 512, BF16)
        cqn3 = v3(cqn, 3, 512)
        t1, t1r = ph.sb(512, F32)
        t2, t2r = ph.sb(512, F32)
        qno = Rot([ph.sb(512, BF16) for _ in range(2)])
        qro = Rot([ph.sb(512, BF16) for _ in range(2)])
        pacc = Rot([ph.ps(512) for _ in range(2)])
        pss, pssr = ph.ps(512)
        pn = Rot([ph.ps(512) for _ in range(2)])
        pk = [ph.ps(512) for _ in range(2)]
        for tt in range(4):
            tcols = slice(tt * 512, (tt + 1) * 512)
            xr = xrs[tt]
            rms_chunks(cx, ph, pacc, None, pss, pssr, ones, onesr, wq13, wq1r, 0, 3, x3, xr, tcols, gcol, gr,
                       craw, crawr, csq, csqr, rb, rbr, cqn3, cqnr, 384.0)
            for h in range(8):
                p, pr = pn.next()
                for kc in range(3):
                    P.pe(lambda e, kc=kc, p=p, h=h: e.matmul(p[:, :], lhsT=wuq3[:, kc, h * 192:h * 192 + 128],
                                                            rhs=cqn3[:, kc, :], start=(kc == 0), stop=(kc == 2)),
                         [wuqr, cqnr], [pr])
                qn, qnr = qno.next()
                P.act(lambda e, p=p, qn=qn: e.activation(out=qn[:, :], in_=p[:, :], func=AF.Copy), [pr], [qnr])
                P.dma("pool", cx.dram["QSn"][h, :, tcols], qn[:, :], reads=[qnr], writes=[])
                for kc in range(3):
                    P.pe(lambda e, kc=kc, h=h: e.matmul(pk[0][0][0:64, :], lhsT=wuq3[:, kc, h * 192 + 128:h * 192 + 192],
                                                        rhs=cqn3[:, kc, :], start=(kc == 0), stop=(kc == 2)),
                         [wuqr, cqnr], [pk[0][1]])
                for kc in range(3):
                    P.pe(lambda e, kc=kc, h=h: e.matmul(pk[1][0][0:64, :], lhsT=wsw3[:, kc, h * 64:(h + 1) * 64],
                                                        rhs=cqn3[:, kc, :], start=(kc == 0), stop=(kc == 2)),
                         [wswrs[h], cqnr], [pk[1][1]])
                qr_, qrr = qro.next()
                rotary(cx, pk[0][0], pk[0][1], pk[1][0], pk[1][1], cos, sin, csr, tcols, t1, t1r, t2, t2r,
                       qr_[0:64, :], qrr)
                P.dma("pool", cx.dram["QSr"][h, :, tcols], qr_[0:64, :], reads=[qrr], writes=[])


def phase_ekv(cx, layer, ga):
    P = cx.P
    i = cx.lmap["even"][layer]
    GA = cx.dram[ga]
    KSv = cx.dram["KS"].ap().rearrange("h p t -> p h t")
    VSv = cx.dram["VS"].ap().rearrange("h p k d -> p h k d")
    with Phase(cx, "ekv%d" % layer) as ph:
        stg = None
        wuk, wukr = ph.sb(2 * 1024, BF16)
        wuv, wuvr = ph.sb(2 * 1024, BF16)
        wuk3, wuv3 = v3(wuk, 2, 1024), v3(wuv, 2, 1024)
        load_cast(cx, stg, wuk3, wukr, cx.dram["even_w_uk"][i].rearrange("(c p) f -> p c f", p=128), 2, 1024)
        load_cast(cx, stg, wuv3, wuvr, cx.dram["even_w_uv"][i].rearrange("(c p) f -> p c f", p=128), 2, 1024)
        lat = Rot([ph.sb(2 * 512, BF16) for _ in range(3)])
        kt = Rot([ph.sb(8 * 512, BF16) for _ in range(2)])
        vt = Rot([ph.sb(4096, BF16) for _ in range(2)])
        pk = Rot([ph.ps(512) for _ in range(4)])
        pv = Rot([ph.ps(1024) for _ in range(2)])
        ev = 0
        for b in range(SEQ // 512):
            r, bb = b // 4, b % 4
            l, lr = lat.next()
            l3 = v3(l, 2, 512)
            P.dma("sp", l3, GA[r, 512:768, bb * 512:(bb + 1) * 512].rearrange("(c p) t -> p c t", p=128), writes=[lr])
            k_, kr_ = kt.next()
            k3 = v3(k_, 8, 512)
            for h in range(8):
                p, pr = pk.next()
                for c in range(2):
                    P.pe(lambda e, c=c, h=h, p=p, l3=l3: e.matmul(p[:, :], lhsT=wuk3[:, c, h * 128:(h + 1) * 128],
                                                                 rhs=l3[:, c, :], start=(c == 0), stop=(c == 1)),
                         [wukr, lr], [pr])
                if ev % 2 == 0:
                    P.act(lambda e, p=p, h=h, k3=k3: e.activation(out=k3[:, h, :], in_=p[:, :], func=AF.Copy), [pr], [kr_])
                else:
                    P.dve(lambda e, p=p, h=h, k3=k3: e.tensor_copy(out=k3[:, h, :], in_=p[:, :]), [pr], [kr_])
                ev += 1
            P.dma("pool", KSv[:, :, b * 512:(b + 1) * 512], k3, reads=[kr_], writes=[])
            v_, vr_ = vt.next()
            v4 = v_[:, :].rearrange("p (h t d) -> p h t d", t=4, h=8, d=128)
            for t4 in range(4):
                p, pr = pv.next()
                for n in range(2):
                    for c in range(2):
                        P.pe(lambda e, c=c, n=n, p=p, l3=l3, t4=t4: e.matmul(
                            p[:, n * 512:(n + 1) * 512], lhsT=l3[:, c, t4 * 128:(t4 + 1) * 128],
                            rhs=wuv3[:, c, n * 512:(n + 1) * 512], start=(c == 0), stop=(c == 1)), [wuvr, lr], [pr])
                dst = v4[:, :, t4, :]
                src = p[:, :].rearrange("p (h d) -> p h d", h=8)
                if ev % 2 == 0:
                    P.act(lambda e, src=src, dst=dst: e.activation(out=dst, in_=src, func=AF.Copy), [pr], [vr_])
                else:
                    P.dve(lambda e, src=src, dst=dst: e.tensor_copy(out=dst, in_=src), [pr], [vr_])
                ev += 1
            P.dma("pool", VSv[:, :, b * 4:(b + 1) * 4, :].rearrange("p h t d -> p h (t d)"),
                  v_[:, :].rearrange("p (h x) -> p h x", h=8), reads=[vr_], writes=[])


def phase_efft(cx, layer, ga, mixT):
    P = cx.P
    GA = cx.dram[ga]
    with Phase(cx, "efft%d" % layer) as ph:
        cs1, cs1r = ph.sb(256, BF16)
        t3, t3r = ph.sb(128 * 64, BF16)
        cd, cdr = ph.sb(256, BF16)
        P.dma("sp", cs1[:, :], cx.dram["cs1"][:, :], writes=[cs1r])
        P.dma("sp", t3[:, :], cx.dram["t3"][:, :], writes=[t3r])
        P.dma("sp", cd[:, :], cx.dram["cdft"][:, :], writes=[cdr])
        t33 = v3(t3, 128, 64)
        Fg = Rot([ph.sb(128 * 128, BF16) for _ in range(2)])
        Ag, Agr = ph.sb(128 * 256, BF16)
        A5 = Ag[:, :].rearrange("p (r k c) -> p r k c", r=2, k=128, c=128)
        XT, XTr = ph.sb(2 * T, BF16)
        XT4 = XT[:, :].rearrange("p (r b k) -> p r b k", r=2, b=16, k=128)
        XT3 = v3(XT, 2, T)
        yT = Rot([ph.sb(512, BF16) for _ in range(2)])
        pA = Rot([ph.ps(512) for _ in range(3)])
        pX = Rot([ph.ps(512) for _ in range(2)])
        pY = Rot([ph.ps(512) for _ in range(2)])
        ev = 0
        for g in range(4):
            F, Fr = Fg.next()
            F3 = v3(F, 128, 128)
            for r in range(NCORES):
                src = GA[r, 0:512, :].rearrange("r (a c) -> (r a) c", c=512)[:, g * 128:(g + 1) * 128]
                P.dma("sp", F3[r * 16:(r + 1) * 16, :, :], src.rearrange("(t s) c -> t s c", s=128), writes=[Fr])
            for c2 in range(64):
                p, pr = pA.next()
                for j in range(2):
                    c = c2 * 2 + j
                    P.pe(lambda e, c=c, j=j, p=p, F3=F3: e.matmul(p[:, j * 256:(j + 1) * 256], lhsT=F3[:, :, c],
                                                                 rhs=cs1[:, :], start=True, stop=True),
                         [Fr, cs1r], [pr])
                dst = A5[:, :, :, c2 * 2:c2 * 2 + 2]
                src = p[:, :].rearrange("p (j r k) -> p r k j", j=2, r=2, k=128)
                if ev % 2 == 0:
                    P.act(lambda e, src=src, dst=dst: e.activation(out=dst, in_=src, func=AF.Copy), [pr], [Agr])
                else:
                    P.dve(lambda e, src=src, dst=dst: e.tensor_copy(out=dst, in_=src), [pr], [Agr])
                ev += 1
            for kb in range(8):
                p, pr = pX.next()
                for kl in range(16):
                    k1 = kb * 16 + kl
                    P.pe(lambda e, k1=k1, kl=kl, p=p: e.matmul(p[:, kl * 32:(kl + 1) * 32], lhsT=A5[:, 0, k1, :],
                                                               rhs=t33[:, k1, 0:32], start=True, stop=False),
                         [Agr, t3r], [pr])
                    P.pe(lambda e, k1=k1, kl=kl, p=p: e.matmul(p[:, kl * 32:(kl + 1) * 32], lhsT=A5[:, 1, k1, :],
                                                               rhs=t33[:, k1, 32:64], start=False, stop=True),
                         [Agr, t3r], [pr])
                p3 = p[:, :].rearrange("p (k x) -> p k x", k=16, x=32)
                for ri in range(2):
                    src = p3[:, :, ri * 16:(ri + 1) * 16].rearrange("p k b -> p b k")
                    dst = XT4[:, ri, :, kb * 16:(kb + 1) * 16]
                    if ri == 0:
                        P.act(lambda e, src=src, dst=dst: e.activation(out=dst, in_=src, func=AF.Copy), [pr], [XTr])
                    else:
                        P.dve(lambda e, src=src, dst=dst: e.tensor_copy(out=dst, in_=src), [pr], [XTr])
            for tt in range(4):
                p, pr = pY.next()
                P.pe(lambda e, p=p, tt=tt: e.matmul(p[:, :], lhsT=cd[:, 0:128], rhs=XT3[:, 0, tt * 512:(tt + 1) * 512],
                                                    start=True, stop=False), [cdr, XTr], [pr])
                P.pe(lambda e, p=p, tt=tt: e.matmul(p[:, :], lhsT=cd[:, 128:256], rhs=XT3[:, 1, tt * 512:(tt + 1) * 512],
                                                    start=False, stop=True), [cdr, XTr], [pr])
                y, yr = yT.next()
                P.act(lambda e, p=p, y=y: e.activation(out=y[:, :], in_=p[:, :], func=AF.Copy), [pr], [yr])
                P.dma("pool", cx.dram[mixT][8 + g, :, tt * 512:(tt + 1) * 512], y[:, :], reads=[yr], writes=[])


def phase_eattn(cx, layer, ga, mixT):
    P = cx.P
    GA = cx.dram[ga]
    NCH = 8
    with Phase(cx, "eattn%d" % layer) as ph:
        ones, onesr = ph.sb(128, BF16)
        P.pool(lambda e: e.memset(ones[:, :], 1.0), [], [onesr])
        KR, _ = ph.sb(SEQ, BF16)
        krres = [P.res() for _ in range(NCH)]
        zr = P.res()
        P.pool(lambda e: e.memset(KR[64:128, :], 0.0), [], [zr])
        for r in range(NCH):
            P.dma("sp", KR[0:64, r * 2048:(r + 1) * 2048], GA[r, 768:832, :], writes=[krres[r]])
        KN, _ = ph.sb(SEQ, BF16)
        knres = [P.res() for _ in range(NCH)]
        VV, _ = ph.sb(SEQ, BF16)
        VV3 = v3(VV, 128, 128)
        vres = [P.res() for _ in range(NCH)]
        QN = Rot([ph.sb(T, BF16) for _ in range(2)])
        QR = Rot([ph.sb(T, BF16) for _ in range(2)])
        for (q_, qr_) in QR.items:
            P.pool(lambda e, q_=q_: e.memset(q_[64:128, :], 0.0), [], [qr_])
        PT = Rot([ph.sb(1024, BF16) for _ in range(4)])
        ACC = Rot([ph.sb(1024, F32) for _ in range(2)])
        ACB = Rot([ph.sb(1024, BF16) for _ in range(2)])
        rs = Rot([ph.sb(512, F32) for _ in range(2)])
        ob = Rot([ph.sb(512, BF16) for _ in range(2)])
        pS = Rot([ph.ps(1024) for _ in range(2)])
        pO = Rot([ph.ps(512) for _ in range(2)])
        pR = Rot([ph.ps(512) for _ in range(2)])
        KSd, VSd = cx.dram["KS"], cx.dram["VS"]
        for h in range(8):
            qn, qnr = QN.next()
            qr, qrr = QR.next()
            P.dma("sp", qn[:, :], cx.dram["QSn"][h, :, :], writes=[qnr])
            P.dma("sp", qr[0:64, :], cx.dram["QSr"][h, :, :], writes=[qrr])
            for c in range(NCH):
                P.dma("sp", KN[:, c * 2048:(c + 1) * 2048], KSd[h, :, c * 2048:(c + 1) * 2048], writes=[knres[c]])
                P.dma("sp", VV3[:, c * 16:(c + 1) * 16, :], VSd[h, :, c * 16:(c + 1) * 16, :], writes=[vres[c]])
            for qt in range(4):
                qc = slice(qt * 512, (qt + 1) * 512)
                po, por = pO.next()
                pr_, prr = pR.next()
                acc, accr = ACC.next()

                def qk(pi, qn=qn, qr=qr, qnr=qnr, qrr=qrr, qc=qc):
                    s, sr = pS.next()
                    for j in range(2):
                        kt = pi * 2 + j
                        c = kt // 16
                        kc = slice(kt * 128, (kt + 1) * 128)
                        P.pe(lambda e, s=s, j=j, kc=kc: e.matmul(s[:, j * 512:(j + 1) * 512], lhsT=KN[:, kc], rhs=qn[:, qc],
                                                                 start=True, stop=False), [knres[c], qnr], [sr])
                        P.pe(lambda e, s=s, j=j, kc=kc: e.matmul(s[:, j * 512:(j + 1) * 512], lhsT=KR[:, kc], rhs=qr[:, qc],
                                                                 start=False, stop=True), [krres[c], zr, qrr], [sr])
                    return s, sr

                def pv(pi, pt, ptr, po=po, por=por, acc=acc, accr=accr):
                    for j in range(2):
                        kt = pi * 2 + j
                        c = kt // 16
                        P.pe(lambda e, j=j, kt=kt, pt=pt: e.matmul(po[:, :], lhsT=VV3[:, kt, :], rhs=pt[:, j * 512:(j + 1) * 512],
                                                                   start=(kt == 0), stop=(kt == 127)), [vres[c], ptr], [por])
                    if pi == 0:
                        P.dve(lambda e, pt=pt: e.tensor_copy(out=acc[:, :], in_=pt[:, :]), [ptr], [accr])
                    else:
                        P.dve(lambda e, pt=pt: e.tensor_tensor(out=acc[:, :], in0=acc[:, :], in1=pt[:, :], op=ALU.add),
                              [ptr, accr], [accr])

                pend = None
                for pi in range(64):
                    s, sr = qk(pi)
                    pt, ptr = PT.next()
                    P.act(lambda e, s=s, pt=pt: e.activation(out=pt[:, :], in_=s[:, :], func=AF.Exp, scale=SCALE),
                          [sr], [ptr])
                    if pend is not None:
                        pv(*pend)
                    pend = (pi, pt, ptr)
                pv(*pend)
                ab, abr = ACB.next()
                P.act(lambda e, ab=ab, acc=acc: e.activation(out=ab[:, :], in_=acc[:, :], func=AF.Copy), [accr], [abr])
                for j in range(2):
                    P.pe(lambda e, j=j, ab=ab, pr_=pr_: e.matmul(pr_[:, :], lhsT=ones[:, :], rhs=ab[:, j * 512:(j + 1) * 512],
                                                                 start=(j == 0), stop=(j == 1)), [onesr, abr], [prr])
                r_, rr = rs.next()
                o_, orr = ob.next()
                P.dve(lambda e, r_=r_, pr_=pr_: e.reciprocal(out=r_[:, :], in_=pr_[:, :]), [prr], [rr])
                P.dve(lambda e, r_=r_, o_=o_, po=po: e.tensor_tensor(out=o_[:, :], in0=po[:, :], in1=r_[:, :], op=ALU.mult),
                      [por, rr], [orr])
                P.dma("pool", cx.dram[mixT][h, :, qc], o_[:, :], reads=[orr], writes=[])


def phase_eout(cx, layer, mixT, xsrc, xdst, xTdst):
    P = cx.P
    i = cx.lmap["even"][layer]
    last = None
    with Phase(cx, "eout%d" % layer) as ph:
        idt, idr = load_ident(cx, ph)
        pT, pTr = ph.ps(D, BF16)
        ln = LNState(cx, ph, cx.dram["mix_ln_g"][layer:layer + 1, :], cx.dram["mix_ln_b"][layer:layer + 1, :],
                     idt, idr, pT, pTr)
        stg = None
        wo, wor = ph.sb(12 * D, BF16)
        wo3 = v3(wo, 12, D)
        load_cast(cx, stg, wo3, wor, cx.dram["even_w_out"][i].rearrange("(c p) d -> p c d", p=128), 12, D)
        mx, mxr = ph.sb(12 * T, BF16)
        mx3 = v3(mx, 12, T)
        mxv = cx.dram[mixT].ap().rearrange("c p t -> p c t")
        mxrs = [P.res() for _ in range(4)]
        for q_ in range(4):
            P.dma("sp", mx3[:, :, q_ * 512:(q_ + 1) * 512], mxv[:, :, q_ * 512:(q_ + 1) * 512], writes=[mxrs[q_]])
        pY = Rot([ph.ps(D) for _ in range(2)])
        for ti in range(T // 128):
            xpre = ln.prefetch(xsrc, ti)
            py, pyr = pY.next()
            for half in range(2):
                for c in range(12):
                    P.pe(lambda e, c=c, half=half, py=py, ti=ti: e.matmul(
                        py[:, half * 512:(half + 1) * 512], lhsT=mx3[:, c, ti * 128:(ti + 1) * 128],
                        rhs=wo3[:, c, half * 512:(half + 1) * 512], start=(c == 0), stop=(c == 11)), [mxrs[ti // 4], wor], [pyr])
            last = ln.tile(py[:, :], pyr, ti, xdst, xpre, xTdst)
        ln.flush()
    return last


W_EVEN = [("even_w_in", [D, 1216]), ("even_w_uq", [384, 1536]), ("even_w_uk", [256, 1024]),
          ("even_w_uv", [256, 1024]), ("even_w_out", [1536, D]), ("kvg_t", [128, 2]), ("qg_t", [128, 3])]
W_ODD = [("odd_w_in", [D, 4096]), ("odd_sgu_norm_g", [2048]), ("odd_sgu_norm_b", [2048]),
         ("odd_wsT", [8, 128, 128]), ("odd_b16", [2048]), ("odd_w_out", [2048, D])]
W_FFN = [("ffn_w_gate", [D, DFF]), ("ffn_w_up", [D, DFF]), ("ffn_w_down", [DFF, D])]
W_LN = [("mix_ln_g", [4, D]), ("mix_ln_b", [4, D]), ("ffn_ln_g", [4, D]), ("ffn_ln_b", [4, D])]
CONSTS = [("ident", [128, 128], BF16), ("cos2", [64, T], F32), ("sin2", [64, T], F32), ("cs1", [128, 256], BF16),
          ("t3", [128, 128 * 64], BF16), ("cdft", [128, 256], BF16)]

PROG_LAYERS = {"F1": ([0], [], []), "M": ([0], [], []), "A": ([0], [], []), "B": ([0, 2], [1], [0, 1]), "C": ([2], [3], [2, 3]),
               "F": ([0, 2], [1, 3], [0, 1, 2, 3])}


def build_program(kind):
    nc = bass.Bass("TRN2", target_bir_lowering=False)
    cx = Ctx(nc)
    if kind.startswith("P:"):
        PROG_LAYERS[kind] = ([0], [1], [0])
    ev, od, ff = PROG_LAYERS[kind]
    cx.lmap = {"even": {l: k for k, l in enumerate(ev)}, "odd": {l: k for k, l in enumerate(od)},
               "ffn": {l: k for k, l in enumerate(ff)}}
    gath = {}
    if kind in ("F", "F1"):
        for l in (0, 2):
            gbh = nc.dram_tensor("gb%d" % l, [GROWS, 1024], F32)
            gah = nc.dram_tensor("ga%d" % l, [NCORES * GROWS, 1024], F32)
            cx.dram["gb%d" % l] = gbh.ap().bitcast(BF16)
            cx.dram["ga%d" % l] = gah.ap().bitcast(BF16).rearrange("(r g) c -> r g c", r=NCORES)
            gath[l] = (gbh, gah)
    cx.dt("x", [T, D], F32, "ExternalInput")
    for n, shp, dt_ in CONSTS:
        cx.dt(n, shp, dt_, "ExternalInput")
    for n, shp in W_LN:
        cx.dt(n, shp, F32, "ExternalInput")
    for grp, lays in ((W_EVEN, ev), (W_ODD, od), (W_FFN, ff)):
        if lays:
            for n, shp in grp:
                cx.dt(n, [len(lays)] + shp, F32, "ExternalInput")
    sk = "ExternalOutput" if (kind in ("F", "F1") and SCRATCH_AS_OUTPUT) else "Internal"
    for n in ("xs0", "xs1"):
        cx.dt(n, [T, D], F32, sk)
    for n in ("xT0", "xT1"):
        cx.dt(n, [128, 8, T], BF16, sk)
    dk = "ExternalOutput" if kind == "M" else sk
    cx.dt("QSn", [8, 128, T], BF16, dk)
    cx.dt("QSr", [8, 64, T], BF16, dk)
    cx.dt("KS", [8, 128, SEQ], BF16, sk)
    cx.dt("VS", [8, 128, 128, 128], BF16, sk)
    cx.dt("mixT", [12, 128, T], BF16, dk)
    P = cx.P

    def main_even(layer, ga, xsrc, xTsrc, xdst, xTdst):
        phase_eq(cx, layer, xTsrc)
        phase_ekv(cx, layer, ga)
        phase_efft(cx, layer, ga, "mixT")
        phase_eattn(cx, layer, ga, "mixT")
        phase_eout(cx, layer, "mixT", xsrc, xdst, xTdst)

    if kind.startswith("P:"):
        cx.dt("ga", [NCORES, GROWS, 2048], BF16, "ExternalInput")
        cx.dt("gb", [GROWS, 2048], BF16, "ExternalOutput")
        for p in kind[2:].split(","):
            if p == "prep":
                phase_prep(cx, "x", "xT0")
            elif p == "eproj":
                phase_eproj(cx, 0, "xT0", "gb")
            elif p == "eq":
                phase_eq(cx, 0, "xT0")
            elif p == "ekv":
                phase_ekv(cx, 0, "ga")
            elif p == "efft":
                phase_efft(cx, 0, "ga", "mixT")
            elif p == "eattn":
                phase_eattn(cx, 0, "ga", "mixT")
            elif p == "eout":
                phase_eout(cx, 0, "mixT", "x", "xs1", "xT1")
            elif p == "ffn":
                phase_ffn(cx, 0, "x", "xT0", "xs0", "xT1")
            elif p == "odd":
                phase_odd(cx, 1, "x", "xT0", "xs1", "xT1")
    elif kind == "A":
        cx.dt("gb", [GROWS, 2048], BF16, "ExternalOutput")
        cx.dt("xTout", [128, 8, T], BF16, "ExternalOutput")
        phase_prep(cx, "x", "xTout")
        phase_eproj(cx, 0, "xTout", "gb")
    elif kind == "B":
        cx.dt("ga", [NCORES, GROWS, 2048], BF16, "ExternalInput")
        cx.dt("xTin", [128, 8, T], BF16, "ExternalInput")
        cx.dt("gb", [GROWS, 2048], BF16, "ExternalOutput")
        cx.dt("xmid", [T, D], F32, "ExternalOutput")
        cx.dt("xTout", [128, 8, T], BF16, "ExternalOutput")
        main_even(0, "ga", "x", "xTin", "xs1", "xT1")
        phase_ffn(cx, 0, "xs1", "xT1", "xs0", "xT0")
        phase_odd(cx, 1, "xs0", "xT0", "xs1", "xT1")
        phase_ffn(cx, 1, "xs1", "xT1", "xmid", "xTout")
        phase_eproj(cx, 2, "xTout", "gb")
    elif kind == "M":
        cx.dt("ga", [NCORES, GROWS, 2048], BF16, "ExternalInput")
        cx.dt("xmid", [T, D], F32, "ExternalOutput")
        phase_prep(cx, "x", "xT0")
        main_even(0, "ga", "x", "xT0", "xmid", None)
    elif kind == "C":
        cx.dt("ga", [NCORES, GROWS, 2048], BF16, "ExternalInput")
        cx.dt("out", [T, D], F32, "ExternalOutput")
        cx.dt("xTin", [128, 8, T], BF16, "ExternalInput")
        main_even(2, "ga", "x", "xTin", "xs1", "xT1")
        phase_ffn(cx, 2, "xs1", "xT1", "xs0", "xT0")
        phase_odd(cx, 3, "xs0", "xT0", "xs1", "xT1")
        phase_ffn(cx, 3, "xs1", "xT1", "out", None)
    elif kind in ("F", "F1"):
        cx.dt("out", [T, D], F32, "ExternalOutput")
        def all_gather(l):
            gbh, gah = gath[l]
            if DUMMY_CC:
                dgi = nc.dram_tensor("dgi%d" % l, [16, 128], F32)
                dgo = nc.dram_tensor("dgo%d" % l, [NCORES * 16, 128], F32)
                P.add("pool", lambda e: e.collective_compute("AllGather", ALU.bypass,
                                                             replica_groups=[list(range(NCORES))],
                                                             ins=[dgi.ap().opt()], outs=[dgo.ap().opt()]),
                      dma=True, inc=1)
                P.barrier()
            if NO_CC:
                for r in range(NCORES):
                    P.dma("pool", gah[r * GROWS:(r + 1) * GROWS, :], gbh[:, :])
                P.barrier()
                return
            P.add("pool", lambda e: e.collective_compute("AllGather", ALU.bypass,
                                                         replica_groups=[list(range(NCORES))],
                                                         ins=[gbh.ap().opt()], outs=[gah.ap().opt()]),
                  dma=True, inc=1)
            P.barrier()

        phase_prep(cx, "x", "xT0")
        phase_eproj(cx, 0, "xT0", "gb0")
        all_gather(0)
        if kind == "F1":
            main_even(0, "ga0", "x", "xT0", "out", None)
            P.emit()
            return nc, cx
        main_even(0, "ga0", "x", "xT0", "xs1", "xT1")
        phase_ffn(cx, 0, "xs1", "xT1", "xs0", "xT0")
        phase_odd(cx, 1, "xs0", "xT0", "xs1", "xT1")
        phase_ffn(cx, 1, "xs1", "xT1", "xs0", "xT0")
        phase_eproj(cx, 2, "xT0", "gb2")
        all_gather(2)
        main_even(2, "ga2", "xs0", "xT0", "xs1", "xT1")
        phase_ffn(cx, 2, "xs1", "xT1", "xs0", "xT0")
        phase_odd(cx, 3, "xs0", "xT0", "xs1", "xT1")
        phase_ffn(cx, 3, "xs1", "xT1", "out", None)
    else:
        raise ValueError(kind)
    P.emit()
    return nc, cx


def _bf(a):
    return np.asarray(a, dtype=np.float32).astype(ml_dtypes.bfloat16)


def _const_tables(core):
    inv = 1.0 / (10000.0 ** (np.arange(0, 64, 2, dtype=np.float64) / 64.0))
    pos = np.arange(core * T, (core + 1) * T, dtype=np.float64)
    ang = (pos[:, None].astype(np.float32) * inv[None, :].astype(np.float32)).astype(np.float32)
    c, s_ = np.cos(ang.astype(np.float64)), np.sin(ang.astype(np.float64))
    cos2 = np.concatenate([c.T, c.T], 0).astype(np.float32)
    sin2 = np.concatenate([-s_.T, s_.T], 0).astype(np.float32)
    n = np.arange(128, dtype=np.float64)
    th = 2 * np.pi * np.outer(n, n) / 128.0
    cs1 = np.concatenate([np.cos(th), -np.sin(th)], 1)
    norm = 1.0 / np.sqrt(SEQ * 128.0)
    cdft = np.concatenate([np.cos(th) * norm, np.sin(th) * norm], 1)
    s2 = np.arange(128, dtype=np.float64)[:, None, None]
    k1 = np.arange(128, dtype=np.float64)[None, :, None]
    k2 = (16 * core + np.arange(16, dtype=np.float64))[None, None, :]
    k = k1 + 128.0 * k2
    ph = 2 * np.pi * ((s2 * k) % SEQ) / SEQ
    mre, mim = np.cos(ph), -np.sin(ph)
    t3 = np.concatenate([mre, mim, -mim, mre], 2).reshape(128, 128 * 64)
    return {"ident": _bf(np.eye(128)), "cos2": np.ascontiguousarray(cos2), "sin2": np.ascontiguousarray(sin2),
            "cs1": _bf(cs1), "t3": _bf(t3), "cdft": _bf(cdft)}


def _weights(inp, kind):
    ev, od, ff = PROG_LAYERS[kind]
    w = {}
    for n in ("mix_ln_g", "mix_ln_b", "ffn_ln_g", "ffn_ln_b"):
        w[n] = np.ascontiguousarray(inp[n], dtype=np.float32)
    if ev:
        ei = [l // 2 for l in ev]
        for n in ("even_w_in", "even_w_uq", "even_w_uk", "even_w_uv", "even_w_out"):
            w[n] = np.ascontiguousarray(np.asarray(inp[n], dtype=np.float32)[ei])
        w["kvg_t"] = np.ascontiguousarray(np.asarray(inp["even_kv_norm"], dtype=np.float32)[ei].reshape(len(ei), 2, 128).transpose(0, 2, 1))
        w["qg_t"] = np.ascontiguousarray(np.asarray(inp["even_q_norm"], dtype=np.float32)[ei].reshape(len(ei), 3, 128).transpose(0, 2, 1))
    if od:
        oi = [l // 2 for l in od]
        for n in ("odd_w_in", "odd_sgu_norm_g", "odd_sgu_norm_b", "odd_w_out"):
            w[n] = np.ascontiguousarray(np.asarray(inp[n], dtype=np.float32)[oi])
        w["odd_wsT"] = np.ascontiguousarray(np.asarray(inp["odd_w_spatial"], dtype=np.float32)[oi].transpose(0, 1, 3, 2))
        w["odd_b16"] = np.ascontiguousarray(np.repeat(np.asarray(inp["odd_b_spatial"], dtype=np.float32)[oi], 2, axis=1).reshape(len(oi), 2048))
    if ff:
        for n in ("ffn_w_gate", "ffn_w_up", "ffn_w_down"):
            w[n] = np.ascontiguousarray(np.asarray(inp[n], dtype=np.float32)[ff])
    return w


_PROGS = {}


def _prog(kind):
    if kind not in _PROGS:
        _PROGS[kind] = build_program(kind)[0]
    return _PROGS[kind]


def _launch(kind, inp, xs, ga=None, xT=None):
    w = _weights(inp, kind)
    maps = []
    for c in range(NCORES):
        m = dict(w)
        m.update(_const_tables(c))
        m["x"] = np.ascontiguousarray(xs[c], dtype=np.float32)
        if ga is not None:
            m["ga"] = ga
        if xT is not None:
            m["xTin"] = xT[c]
        maps.append(m)
    res = run_bass_kernel_spmd(_prog(kind), maps, core_ids=list(range(NCORES)))
    return res.results


FUSED = False


def kernel(**inp):
    x = np.asarray(inp["x"], dtype=np.float32).reshape(SEQ, D)
    xs = [x[c * T:(c + 1) * T] for c in range(NCORES)]
    if FUSED:
        r = _launch("F", inp, xs)
        out = np.concatenate([r[c]["out"] for c in range(NCORES)], 0)
        return out.reshape(1, SEQ, D).astype(np.float32)
    ra = _launch("A", inp, xs)
    ga = np.ascontiguousarray(np.stack([ra[c]["gb"] for c in range(NCORES)], 0))
    rb = _launch("B", inp, xs, ga, [ra[c]["xTout"] for c in range(NCORES)])
    ga = np.ascontiguousarray(np.stack([rb[c]["gb"] for c in range(NCORES)], 0))
    xs = [rb[c]["xmid"] for c in range(NCORES)]
    rc = _launch("C", inp, xs, ga, [rb[c]["xTout"] for c in range(NCORES)])
    out = np.concatenate([rc[c]["out"] for c in range(NCORES)], 0)
    return out.reshape(1, SEQ, D).astype(np.float32)
```
